# Optimizing a Trainium2 kernel written in Bass

```python
import jax, jax.numpy as jnp
from jax import lax
import numpy as np

D_MODEL = 1024
BATCH = 4
SEQ = 4096
DEPTH = 4
DEC_BATCH = 32
DEC_SEQ = 8
PAST_LEN = 8192
PAGE_SIZE = 128

N_A = DEPTH // 2
N_B = DEPTH - N_A
D_RNN = D_MODEL
N_RNN_BLOCKS = 8
RNN_BLOCK = D_RNN // N_RNN_BLOCKS
CONV_W = 4
RGLRU_C = 8.0
N_HEADS = 16
HEAD_DIM = D_MODEL // N_HEADS
D_ATTN = N_HEADS * HEAD_DIM
Q_BLOCK = 128
EPS = 1e-6

kernel_name = "hawk_fox_yoco_step"


def rms_norm(x, g):
    xf = x.astype(jnp.float32)
    y = xf * lax.rsqrt(jnp.mean(xf * xf, axis=-1, keepdims=True) + EPS)
    return (y * g.astype(jnp.float32)).astype(x.dtype)


def causal_conv(u, buf, w, b):
    T = u.shape[1]
    up = jnp.concatenate([buf.astype(u.dtype), u], axis=1)
    y = up[:, CONV_W - 1:CONV_W - 1 + T] * w[CONV_W - 1] + b
    for j in range(CONV_W - 1):
        y = y + up[:, j:j + T] * w[j]
    return y, up[:, -(CONV_W - 1):]


def rg_lru(u, h0, w_ga, b_ga, w_gx, b_gx, lam):
    B, T, _ = u.shape
    uf = u.astype(jnp.float32)
    ub = uf.reshape(B, T, N_RNN_BLOCKS, RNN_BLOCK)
    r = jax.nn.sigmoid(jnp.einsum('btnc,ncd->btnd', ub, w_ga.astype(jnp.float32)).reshape(B, T, D_RNN)
                       + b_ga.astype(jnp.float32))
    i = jax.nn.sigmoid(jnp.einsum('btnc,ncd->btnd', ub, w_gx.astype(jnp.float32)).reshape(B, T, D_RNN)
                       + b_gx.astype(jnp.float32))
    log_a = -RGLRU_C * r * jax.nn.softplus(-lam.astype(jnp.float32))
    a = jnp.exp(log_a)
    inp = jnp.sqrt(-jnp.expm1(2.0 * log_a)) * (i * uf)

    def combine(left, right):
        a1, b1 = left
        a2, b2 = right
        return a1 * a2, a2 * b1 + b2

    a_cum, b_cum = lax.associative_scan(combine, (a, inp), axis=1)
    h = a_cum * h0.astype(jnp.float32)[:, None, :] + b_cum
    return h.astype(u.dtype), h[:, -1].astype(u.dtype)


def recurrent_mixer(x, h0, conv_buf, w_in, conv_w, conv_b, w_ga, b_ga, w_gx, b_gx, lam, w_out):
    z = x @ w_in
    u, g = z[..., :D_RNN], z[..., D_RNN:]
    u, new_buf = causal_conv(u, conv_buf, conv_w, conv_b)
    h, h_last = rg_lru(u, h0, w_ga, b_ga, w_gx, b_gx, lam)
    return (h * jax.nn.silu(g)) @ w_out, h_last, new_buf


def shared_kv(x, kv_norm, w_kv, b_f):
    B, T, _ = x.shape
    z = rms_norm(x, kv_norm) @ w_kv
    k = z[..., :D_ATTN].reshape(B, T, N_HEADS, HEAD_DIM)
    v = z[..., D_ATTN:2 * D_ATTN].reshape(B, T, N_HEADS, HEAD_DIM)
    logf = jax.nn.log_sigmoid(z[..., 2 * D_ATTN:].astype(jnp.float32) + b_f.astype(jnp.float32))
    return k, v, logf


def fox_block(q, k, v, cq, ck, q_pos, k_pos):
    s = jnp.einsum('bqhd,bkhd->bhqk', q, k).astype(jnp.float32) * (HEAD_DIM ** -0.5)
    s = s + jnp.swapaxes(cq, 1, 2)[:, :, :, None] - jnp.swapaxes(ck, 1, 2)[:, :, None, :]
    mask = k_pos[None, :] <= q_pos[:, None]
    s = jnp.where(mask[None, None], s, -jnp.inf)
    p = jax.nn.softmax(s, axis=-1)
    return jnp.einsum('bhqk,bkhd->bqhd', p.astype(v.dtype), v)


def fox_attention(q, k, v, cq, ck, q_pos, k_pos):
    B, Tq = q.shape[:2]
    blk = min(Q_BLOCK, Tq)
    nb = Tq // blk
    qb = jnp.swapaxes(q.reshape(B, nb, blk, N_HEADS, HEAD_DIM), 0, 1)
    cqb = jnp.swapaxes(cq.reshape(B, nb, blk, N_HEADS), 0, 1)
    pb = q_pos.reshape(nb, blk)
    out = lax.map(lambda a: fox_block(a[0], k, v, a[1], ck, a[2], k_pos), (qb, cqb, pb))
    return jnp.swapaxes(out, 0, 1).reshape(B, Tq, N_HEADS, HEAD_DIM)


def fox_mixer(x, k, v, cq, ck, q_pos, k_pos, w_in, w_out):
    B, T, _ = x.shape
    z = x @ w_in
    q = z[..., :D_ATTN].reshape(B, T, N_HEADS, HEAD_DIM)
    g = z[..., D_ATTN:]
    o = fox_attention(q, k, v, cq, ck, q_pos, k_pos).reshape(B, T, D_ATTN)
    return (o * jax.nn.silu(g)) @ w_out


def run_group(x, rnn_state, conv_state, past_k, past_v, past_logf, pos0,
              a_pre_norm, a_post_norm, a_w_in, a_conv_w, a_conv_b, a_w_ga, a_b_ga,
              a_w_gx, a_b_gx, a_lambda, a_w_out, kv_norm, w_kv, b_f,
              b_pre_norm, b_post_norm, b_w_in, b_w_out):
    B, T, _ = x.shape
    new_rnn, new_conv = [], []
    kv = None
    for layer in range(DEPTH):
        if layer < N_A:
            l = layer
            y, h_last, buf = recurrent_mixer(rms_norm(x, a_pre_norm[l]), rnn_state[l], conv_state[l],
                                             a_w_in[l], a_conv_w[l], a_conv_b[l], a_w_ga[l], a_b_ga[l],
                                             a_w_gx[l], a_b_gx[l], a_lambda[l], a_w_out[l])
            x = x + rms_norm(y, a_post_norm[l])
            new_rnn.append(h_last)
            new_conv.append(buf)
            if layer == N_A - 1:
                k_new, v_new, logf_new = shared_kv(x, kv_norm, w_kv, b_f)
                if past_k is None:
                    k_all, v_all, logf_all = k_new, v_new, logf_new
                else:
                    k_all = jnp.concatenate([past_k.astype(k_new.dtype), k_new], axis=1)
                    v_all = jnp.concatenate([past_v.astype(v_new.dtype), v_new], axis=1)
                    logf_all = jnp.concatenate([past_logf.astype(jnp.float32), logf_new], axis=1)
                P = k_all.shape[1] - T
                c_all = jnp.cumsum(logf_all, axis=1)
                kv = (k_all, v_all, c_all[:, P:], c_all,
                      pos0 + jnp.arange(T), jnp.arange(P + T))
        else:
            l = layer - N_A
            k_all, v_all, cq, ck, q_pos, k_pos = kv
            y = fox_mixer(rms_norm(x, b_pre_norm[l]), k_all, v_all, cq, ck, q_pos, k_pos,
                          b_w_in[l], b_w_out[l])
            x = x + rms_norm(y, b_post_norm[l])
    return x, k_new, v_new, logf_new, jnp.stack(new_rnn), jnp.stack(new_conv)


def setup_inputs(seed: int = 0) -> dict:
    key = jax.random.key(seed)
    ks = list(jax.random.split(key, 32))
    f32 = jnp.float32
    n_pages = PAST_LEN // PAGE_SIZE
    n_used = DEC_BATCH * n_pages
    n_pool = n_used + n_used // 4
    nrm = lambda k, s, sc: jax.random.normal(k, s, f32) * sc
    x_prompt = nrm(ks[0], (BATCH, SEQ, D_MODEL), 1.0)
    x_sample = nrm(ks[1], (DEC_BATCH, DEC_SEQ, D_MODEL), 1.0)
    b_f = jax.random.uniform(ks[2], (N_HEADS,), f32, 1.0, 6.0)
    cache_k = nrm(ks[3], (n_pool, PAGE_SIZE, N_HEADS, HEAD_DIM), 1.0)
    cache_v = nrm(ks[4], (n_pool, PAGE_SIZE, N_HEADS, HEAD_DIM), 1.0)
    cache_logf = jax.nn.log_sigmoid(nrm(ks[5], (n_pool, PAGE_SIZE, N_HEADS), 1.0) + b_f)
    state_rnn = nrm(ks[6], (N_A, DEC_BATCH, D_RNN), 0.5)
    state_conv = nrm(ks[7], (N_A, DEC_BATCH, CONV_W - 1, D_RNN), 1.0)
    page_table = jax.random.permutation(ks[8], n_pool)[:n_used].reshape(DEC_BATCH, n_pages).astype(jnp.int32)
    p_lam = jax.random.uniform(ks[9], (N_A, D_RNN), f32, 0.9, 0.999)
    return {
        "x_prompt": x_prompt,
        "x_sample": x_sample,
        "cache_k": cache_k,
        "cache_v": cache_v,
        "cache_logf": cache_logf,
        "state_rnn": state_rnn,
        "state_conv": state_conv,
        "page_table": page_table,
        "a_pre_norm": 1.0 + nrm(ks[10], (N_A, D_MODEL), 0.05),
        "a_post_norm": 1.0 + nrm(ks[11], (N_A, D_MODEL), 0.05),
        "a_w_in": nrm(ks[12], (N_A, D_MODEL, 2 * D_RNN), D_MODEL ** -0.5),
        "a_conv_w": nrm(ks[13], (N_A, CONV_W, D_RNN), CONV_W ** -0.5),
        "a_conv_b": nrm(ks[14], (N_A, D_RNN), 0.02),
        "a_w_ga": nrm(ks[15], (N_A, N_RNN_BLOCKS, RNN_BLOCK, RNN_BLOCK), RNN_BLOCK ** -0.5),
        "a_b_ga": nrm(ks[16], (N_A, D_RNN), 0.1),
        "a_w_gx": nrm(ks[17], (N_A, N_RNN_BLOCKS, RNN_BLOCK, RNN_BLOCK), RNN_BLOCK ** -0.5),
        "a_b_gx": nrm(ks[18], (N_A, D_RNN), 0.1),
        "a_lambda": jnp.log(p_lam) - jnp.log1p(-p_lam),
        "a_w_out": nrm(ks[19], (N_A, D_RNN, D_MODEL), D_RNN ** -0.5),
        "kv_norm": 1.0 + nrm(ks[20], (D_MODEL,), 0.05),
        "w_kv": nrm(ks[21], (D_MODEL, 2 * D_ATTN + N_HEADS), D_MODEL ** -0.5),
        "b_f": b_f,
        "b_pre_norm": 1.0 + nrm(ks[22], (N_B, D_MODEL), 0.05),
        "b_post_norm": 1.0 + nrm(ks[23], (N_B, D_MODEL), 0.05),
        "b_w_in": nrm(ks[24], (N_B, D_MODEL, 2 * D_ATTN), D_MODEL ** -0.5),
        "b_w_out": nrm(ks[25], (N_B, D_ATTN, D_MODEL), D_ATTN ** -0.5),
    }


def reference(x_prompt, x_sample, cache_k, cache_v, cache_logf, state_rnn, state_conv, page_table,
              a_pre_norm, a_post_norm, a_w_in, a_conv_w, a_conv_b, a_w_ga, a_b_ga, a_w_gx, a_b_gx,
              a_lambda, a_w_out, kv_norm, w_kv, b_f, b_pre_norm, b_post_norm, b_w_in, b_w_out):
    weights = (a_pre_norm, a_post_norm, a_w_in, a_conv_w, a_conv_b, a_w_ga, a_b_ga, a_w_gx, a_b_gx,
               a_lambda, a_w_out, kv_norm, w_kv, b_f, b_pre_norm, b_post_norm, b_w_in, b_w_out)
    Bp = x_prompt.shape[0]
    rnn0 = jnp.zeros((N_A, Bp, D_RNN), x_prompt.dtype)
    conv0 = jnp.zeros((N_A, Bp, CONV_W - 1, D_RNN), x_prompt.dtype)
    y_prompt, k_p, v_p, logf_p, rnn_p, conv_p = run_group(
        x_prompt, rnn0, conv0, None, None, None, 0, *weights)
    Bs, n_pages = page_table.shape
    past_len = n_pages * cache_k.shape[1]
    past_k = cache_k[page_table].reshape(Bs, past_len, N_HEADS, HEAD_DIM)
    past_v = cache_v[page_table].reshape(Bs, past_len, N_HEADS, HEAD_DIM)
    past_logf = cache_logf[page_table].reshape(Bs, past_len, N_HEADS)
    y_sample, k_s, v_s, logf_s, rnn_s, conv_s = run_group(
        x_sample, state_rnn, state_conv, past_k, past_v, past_logf, past_len, *weights)
    return (y_prompt, y_sample, k_p, v_p, logf_p, rnn_p, conv_p, k_s, v_s, logf_s, rnn_s, conv_s)
```

```python
import numpy as np
from contextlib import ExitStack
import concourse.bass as bass
import concourse.mybir as mybir
from concourse.bass_utils import run_bass_kernel_spmd

F32, BF16, I32 = mybir.dt.float32, mybir.dt.bfloat16, mybir.dt.int32
ALU = mybir.AluOpType
AF = mybir.ActivationFunctionType

D = 1024
T = 4096
NT = T // 128
CH = 256
NCH = T // CH
H = 16
NPOOL = 2560
NPG = 64
NS = 32
EPS = 1e-6
NEG = -30000.0


class Prog:
    ENG = ("pe", "act", "dve", "pool", "sp")
    NDS = 6

    def __init__(self, nc, stack):
        self.nc = nc
        self.stack = stack
        self.streams = {e: [] for e in self.ENG}
        self.cnt = {e: 0 for e in self.ENG}
        self.sem = {}
        self.known = {e: {} for e in self.ENG}
        self.lastw = {}
        self.readers = {}
        self.dsem = {}
        self.dcnt = {}
        self.nsem = 0
        self.semobj = {}
        for q in ("sp", "pool", "act"):
            self.dsem[q] = [self._newsem() for _ in range(self.NDS)]
            self.dcnt[q] = 0

    def _newsem(self):
        s = self.stack.enter_context(self.nc.semaphore("s%d" % self.nsem))
        self.semobj[self.nsem] = s
        self.nsem += 1
        return self.nsem - 1

    def _deps(self, eng, r, w):
        need = {}
        def add(ev):
            sid, val, src = ev
            if src == "pe" and eng == "pe":
                return
            if need.get(sid, 0) < val:
                need[sid] = val
        for k in r:
            if k in self.lastw:
                add(self.lastw[k])
        for k in w:
            if k in self.lastw:
                add(self.lastw[k])
            for ev in self.readers.get(k, {}).values():
                add(ev)
        waits = []
        kn = self.known[eng]
        for sid, val in need.items():
            if kn.get(sid, 0) < val:
                kn[sid] = val
                waits.append((sid, val))
        return waits

    def _commit(self, ev, r, w):
        for k in r:
            self.readers.setdefault(k, {})[ev[0]] = ev
        for k in w:
            self.lastw[k] = ev
            self.readers[k] = {}

    def op(self, eng, fn, r=(), w=()):
        waits = self._deps(eng, r, w)
        if eng not in self.sem or self.cnt[eng] >= 6000:
            self.sem[eng] = self._newsem()
            self.cnt[eng] = 0
        self.cnt[eng] += 1
        ev = (self.sem[eng], self.cnt[eng], eng)
        self.streams[eng].append((waits, fn, (self.sem[eng], 1)))
        self._commit(ev, r, w)

    def dma(self, q, fn, r=(), w=()):
        waits = self._deps(q, r, w)
        i = self.dcnt[q]
        self.dcnt[q] += 1
        sid = self.dsem[q][i % self.NDS]
        prev = 16 * (i // self.NDS)
        if prev > 0 and self.known[q].get(sid, 0) < prev:
            self.known[q][sid] = prev
            waits.append((sid, prev))
        self.streams[q].append((waits, fn, (sid, 16)))
        self._commit((sid, prev + 16, "dma"), r, w)

    def drain(self, eng="sp"):
        waits = []
        for q in self.dsem:
            n = self.dcnt[q]
            for j, sid in enumerate(self.dsem[q]):
                cntj = (n - j + self.NDS - 1) // self.NDS if n > j else 0
                if cntj > 0 and self.known[eng].get(sid, 0) < 16 * cntj:
                    self.known[eng][sid] = 16 * cntj
                    waits.append((sid, 16 * cntj))
        self.streams[eng].append((waits, None, None))

    def flush(self):
        nc = self.nc
        streams = self.streams
        semobj = self.semobj
        self.streams = {e: [] for e in self.ENG}

        def replay(name):
            def f(e):
                for waits, fn, inc in streams[name]:
                    for sid, val in waits:
                        e.wait_ge(semobj[sid], val)
                    if fn is not None:
                        fn(e).then_inc(semobj[inc[0]], inc[1])
            return f
        with nc.Block() as block:
            if streams["pe"]:
                block.tensor(replay("pe"))
            if streams["act"]:
                block.scalar(replay("act"))
            if streams["dve"]:
                block.vector(replay("dve"))
            if streams["pool"]:
                block.gpsimd(replay("pool"))
            if streams["sp"]:
                block.sync(replay("sp"))


def fm(v):
    return np.ascontiguousarray(np.asarray(v, np.float32).reshape(8, 128).T)


PV = {}
def _pv_layout():
    c = 0
    for l in range(2):
        for nm in ("apre", "cw0", "cw1", "cw2", "cw3", "cb", "bga", "bgx", "lam"):
            PV[(nm, l)] = c; c += 8
    PV[("kvn", 0)] = c; c += 8
    for l in range(2):
        PV[("bpre", l)] = c; c += 8
    return c
NPV = _pv_layout()


def build_program(do_prompt=True, do_sample=True, npool=NPOOL, debug=False):
    nc = bass.Bass("TRN2", target_bir_lowering=False)
    dram = {}

    def din(name, shape, dt=F32):
        dram[name] = nc.dram_tensor(name, list(shape), dt, kind="ExternalInput").ap()
        return dram[name]

    def dout(name, shape, dt=F32):
        dram[name] = nc.dram_tensor(name, list(shape), dt, kind="ExternalOutput").ap()
        return dram[name]

    def dscr(name, shape, dt=F32):
        dram[name] = nc.dram_tensor(name, list(shape), dt, kind="Internal").ap()
        return dram[name]

    xp = din("xp", [T, D])
    a_w_in = din("a_w_in", [2, D, 2 * D]); a_w_out = din("a_w_out", [2, D, D])
    a_w_ga = din("a_w_ga", [2, 8, 128, 128]); a_w_gx = din("a_w_gx", [2, 8, 128, 128])
    w_kv = din("w_kv", [D, 2 * D + H])
    b_w_in = din("b_w_in", [2, D, 2 * D]); b_w_out = din("b_w_out", [2, D, D])
    pvec_d = din("pvec", [128, NPV])
    rowv_d = din("rowv", [5, D])
    cst_d = din("cst", [128, 6, 128])
    idx_d = din("idx", [128, 32], I32)
    xs_d = din("xs", [256, D])
    st_rnn_d = din("st_rnn", [2, 128, 8, NS]); st_conv_d = din("st_conv", [2, 128, 8, NS, 3])
    ck_full = din("ck_full", [8 * npool * 4, 4096])
    cv_full = din("cv_full", [8 * npool * 4, 4096])
    clf_full = din("clf_full", [8 * npool * 2, 128])
    ptT2_d = din("ptT2", [128, NS], I32)
    scst_d = din("scst", [128, 3, 256])
    ys_out = dout("ys_out", [256, D]); ks_out = dout("ks_out", [256, D]); vs_out = dout("vs_out", [256, D])
    lfs_out = dout("lfs_out", [256, H])
    rnns_out = dout("rnns_out", [2, 128, 8, NS]); convs_out = dout("convs_out", [2, 128, 8, NS, 3])
    xs1 = dscr("xs1", [256, D]); xs2 = dscr("xs2", [256, D])
    o_scr = [dscr("o_scr%d" % i, [256, D]) for i in range(2)]
    dbg = dout("dbg", [128, 4096]) if debug else None
    y_own = dout("y_own", [16, 128, D])
    k_out = dout("k_out", [T, D]); v_out = dout("v_out", [T, D]); lf_out = dout("lf_out", [T, H])
    rnn_out = dout("rnn_out", [2, 128, 8]); conv_out = dout("conv_out", [2, 128, 8, 3])
    x1s = dscr("x1s", [T, D]); x2s = dscr("x2s", [T, D])
    kTs = dscr("kTs", [8, 128, T], BF16)
    vs = dscr("vs", [8, 128, NT, 128], BF16)
    cs = dscr("cs", [T, H])

    st = ExitStack()
    P = Prog(nc, st)

    _uid = [0]

    def sb(name, shape, dt=F32, stack=None):
        _uid[0] += 1
        return (stack or st).enter_context(nc.sbuf_tensor("t%d_%s" % (_uid[0], name), list(shape), dt))

    ps = [st.enter_context(nc.psum_tensor("ps%d" % i, [128, 512], F32)) for i in range(8)]

    pvec = sb("pvec", [128, NPV])
    cst = sb("cstt", [128, 6, 128])
    identb = sb("identb", [128, 128], BF16)
    idx = sb("idxt", [128, 32], I32)
    cneg = sb("cneg", [128, 2, 8]); cnegh = sb("cnegh", [128, 2, 8]); cneg2 = sb("cneg2", [128, 2, 8])
    bgah = sb("bgah", [128, 2, 8]); bgxh = sb("bgxh", [128, 2, 8])
    bfB = sb("bfB", [128, H + 2])
    scst = sb("scst", [128, 3, 256])
    kTn = sb("kTn", [128, 8, 256], BF16)
    vn1 = sb("vn1", [128, 8, 129], BF16)
    negE = sb("negE", [128, H])
    ptT2 = sb("ptT2", [128, NS], I32)
    idxK = sb("idxK", [128, 2, 4, 8], I32)
    idxL = sb("idxL", [128, 4, 8], I32)
    offs = sb("offs", [128, 3, 8])
    ck = sb("ck", [128, NT, H])
    lacc = sb("lacc", [128, H])
    stat = sb("stat", [128, 16])

    ident = cst[:, 0, :]; tri = cst[:, 1, :]; ones = cst[:, 2, :]
    mask_a = cst[:, 3, :]; mask_b = cst[:, 4, :]

    def pv(nm, l, n=None):
        c = PV[(nm, l)]
        return pvec[:, c:c + 8] if n is None else pvec[:, c + n:c + n + 1]

    P.dma("sp", lambda e: e.dma_start(out=pvec[:], in_=pvec_d), w=["pvec"])
    P.dma("sp", lambda e: e.dma_start(out=cst[:], in_=cst_d), w=["cst"])
    P.dma("sp", lambda e: e.dma_start(out=idx[:], in_=idx_d), w=["idx"])
    P.dma("sp", lambda e: e.dma_start(out=bfB[:], in_=rowv_d[4:5, 0:H + 2].partition_broadcast(128)), w=["bfB"])
    P.dma("sp", lambda e: e.dma_start(out=scst[:], in_=scst_d), w=["scst"])
    P.dma("sp", lambda e: e.dma_start(out=ptT2[:], in_=ptT2_d), w=["ptT2"])
    P.op("pool", lambda e: e.memset(vn1[:, :, 128:129], 1.0), w=["vn1"])
    for hp in range(8):
        for half in range(2):
            P.op("pool", lambda e, half=half, hp=hp: e.tensor_scalar(
                out=offs[:, half, hp:hp + 1], in0=scst[:, 0, 129 + half:130 + half], scalar1=float(hp * npool * 4), scalar2=0.0,
                op0=ALU.add, op1=ALU.add), r=["scst"], w=["offs"])
        P.op("pool", lambda e, hp=hp: e.tensor_scalar(
            out=offs[:, 2, hp:hp + 1], in0=scst[:, 0, 128:129], scalar1=float(hp * npool * 2), scalar2=0.0,
            op0=ALU.add, op1=ALU.add), r=["scst"], w=["offs"])
    for hp in range(8):
        for half in range(2):
            P.op("pool", lambda e, half=half, hp=hp: e.tensor_scalar(
                out=idxK[:, half, :, hp], in0=ptT2[:, 0:4], scalar1=4.0, scalar2=offs[:, half, hp:hp + 1],
                op0=ALU.mult, op1=ALU.add), r=["ptT2", "offs"], w=["idxK"])
        P.op("pool", lambda e, hp=hp: e.tensor_scalar(
            out=idxL[:, :, hp], in0=ptT2[:, 0:4], scalar1=2.0, scalar2=offs[:, 2, hp:hp + 1],
            op0=ALU.mult, op1=ALU.add), r=["ptT2", "offs"], w=["idxL"])
    P.op("dve", lambda e: e.tensor_copy(out=identb[:], in_=ident), r=["cst"], w=["identb"])
    P.op("dve", lambda e: e.memset(lacc[:], 0.0), w=["lacc"])
    for l in range(2):
        P.op("act", lambda e, l=l: e.activation(out=cneg[:, l, :], in_=pv("lam", l), func=AF.Exp, scale=-1.0),
             r=["pvec"], w=["cneg"])
        P.op("act", lambda e, l=l: e.activation(out=cneg[:, l, :], in_=cneg[:, l, :], func=AF.Ln, bias=1.0),
             r=["cneg"], w=["cneg"])
        P.op("dve", lambda e, l=l: e.tensor_scalar(out=cnegh[:, l, :], in0=cneg[:, l, :], scalar1=-4.0, scalar2=None,
                                                   op0=ALU.mult), r=["cneg"], w=["cnegh"])
        P.op("dve", lambda e, l=l: e.tensor_scalar(out=cneg2[:, l, :], in0=cneg[:, l, :], scalar1=-8.0, scalar2=None,
                                                   op0=ALU.mult), r=["cneg"], w=["cneg2"])
        P.op("dve", lambda e, l=l: e.tensor_scalar(out=bgah[:, l, :], in0=pv("bga", l), scalar1=0.5, scalar2=None,
                                                   op0=ALU.mult), r=["pvec"], w=["bgah"])
        P.op("dve", lambda e, l=l: e.tensor_scalar(out=bgxh[:, l, :], in0=pv("bgx", l), scalar1=0.5, scalar2=None,
                                                   op0=ALU.mult), r=["pvec"], w=["bgxh"])

    def rms_to_featmajor(x_ap, xkey, ntt, gain_l, xh, xnT, tpb, pfx):
        for tt in range(ntt):
            c0 = tt * 2
            P.op("act", lambda e, tt=tt, c0=c0: e.activation(out=xh[:, tt, :], in_=x_ap(tt), func=AF.Square,
                                                             accum_out=stat[:, c0:c0 + 1]),
                 r=[xkey(tt)], w=[pfx + "xh", "stat"])
            P.op("act", lambda e, c0=c0: e.activation(out=stat[:, c0 + 1:c0 + 2], in_=stat[:, c0:c0 + 1], func=AF.Sqrt,
                                                      scale=1.0 / D, bias=epsb[:, 0:1]), r=["stat", "epsb"], w=["stat"])
            P.op("dve", lambda e, c0=c0: e.reciprocal(out=stat[:, c0 + 1:c0 + 2], in_=stat[:, c0 + 1:c0 + 2]),
                 r=["stat"], w=["stat"])
            P.op("pool", lambda e, tt=tt, c0=c0: e.tensor_scalar(out=xh[:, tt, :], in0=x_ap(tt), scalar1=stat[:, c0 + 1:c0 + 2],
                                                                 scalar2=0.0, op0=ALU.mult, op1=ALU.add),
                 r=[xkey(tt), "stat"], w=[pfx + "xh"])
        W = ntt * 128
        per = 1024 // W
        for bi in range(8 // per):
            bank = tpb[bi]
            bv = bank[:].bitcast(BF16)
            for kk in range(per):
                kc = bi * per + kk
                for tt in range(ntt):
                    P.op("pe", lambda e, kc=kc, kk=kk, tt=tt, bv=bv: e.transpose(
                        out=bv[:, kk * W + tt * 128: kk * W + (tt + 1) * 128], in_=xh[:, tt, kc * 128:(kc + 1) * 128],
                        identity=identb[:]), r=[pfx + "xh", "identb"], w=[("ps", id(bank))])
            for kk in range(per):
                kc = bi * per + kk
                if kc % 2 == 0:
                    P.op("dve", lambda e, kc=kc, kk=kk, bv=bv: e.tensor_scalar(
                        out=xnT[:, kc, 0:W], in0=bv[:, kk * W:(kk + 1) * W], scalar1=gain_l(kc), scalar2=None, op0=ALU.mult),
                        r=[("ps", id(bank)), "pvec"], w=[pfx + "xnT"])
                else:
                    P.op("act", lambda e, kc=kc, kk=kk, bv=bv: e.activation(
                        out=xnT[:, kc, 0:W], in_=bv[:, kk * W:(kk + 1) * W], func=AF.Copy, scale=gain_l(kc)),
                        r=[("ps", id(bank)), "pvec"], w=[pfx + "xnT"])

    def post_norm_residual(ybanks, x_tile_ap, xkey, gB, gkey, tmp):
        for hb in range(2):
            P.op("act", lambda e, hb=hb: e.activation(out=tmp[:, hb * 512:(hb + 1) * 512], in_=ybanks[hb][:], func=AF.Square,
                                                      accum_out=stat[:, 8 + hb:9 + hb]),
                 r=[("ps", id(ybanks[hb]))], w=["ytmp", "stat"])
        P.op("dve", lambda e: e.tensor_tensor(out=stat[:, 10:11], in0=stat[:, 8:9], in1=stat[:, 9:10], op=ALU.add),
             r=["stat"], w=["stat"])
        P.op("act", lambda e: e.activation(out=stat[:, 11:12], in_=stat[:, 10:11], func=AF.Sqrt, scale=1.0 / D,
                                           bias=epsb[:, 0:1]), r=["stat", "epsb"], w=["stat"])
        P.op("dve", lambda e: e.reciprocal(out=stat[:, 11:12], in_=stat[:, 11:12]), r=["stat"], w=["stat"])
        for hb in range(2):
            P.op("dve", lambda e, hb=hb: e.scalar_tensor_tensor(
                out=tmp[:, hb * 512:(hb + 1) * 512], in0=ybanks[hb][:], scalar=stat[:, 11:12],
                in1=gB[:, hb * 512:(hb + 1) * 512], op0=ALU.mult, op1=ALU.mult),
                r=[("ps", id(ybanks[hb])), "stat", gkey], w=["ytmp"])
        P.op("pool", lambda e: e.tensor_tensor(out=x_tile_ap, in0=x_tile_ap, in1=tmp[:], op=ALU.add),
             r=["ytmp", xkey], w=[xkey])

    epsb = sb("epsb", [128, 1])
    P.op("dve", lambda e: e.memset(epsb[:], EPS), w=["epsb"])

    def a_pass(l):
        with ExitStack() as ph:
            w_in = sb("aw_in", [128, 8, 2 * D], BF16, ph)
            w_out = sb("aw_out", [128, 8, D], BF16, ph)
            wga = sb("awga", [128, 8, 128], BF16, ph)
            wgx = sb("awgx", [128, 8, 128], BF16, ph)
            gB = sb("agB", [128, D], F32, ph)
            xt = sb("axt", [128, 2, 2, D], F32, ph)
            xh = sb("axh", [128, 2, D], BF16, ph)
            xnT = sb("axnT", [128, 8, CH], BF16, ph)
            ubuf = sb("aubuf", [128, 8, 352], F32, ph)
            hstS = sb("ahstS", [128, 8, NS], F32, ph)
            hlast = sb("ahlast", [128, 8, NS], F32, ph)
            ubv = lambda oc: ubuf[:, oc, 0:352].rearrange("p (s j) -> p s j", j=11)
            v3 = lambda ap: ap.rearrange("p (s t) -> p s t", t=8)
            uc = sb("auc", [128, 8, CH], F32, ph)
            ucb = sb("aucb", [128, 8, CH], BF16, ph)
            rbuf = sb("arbuf", [128, 8, CH], F32, ph)
            ibuf = sb("aibuf", [128, 8, CH], F32, ph)
            sbuf_ = sb("asbuf", [128, 8, CH], F32, ph)
            thg = sb("athg", [128, 8, CH], BF16, ph)
            hbuf = sbuf_
            mT = sb("amT", [128, 8, CH], BF16, ph)
            hst = sb("ahst", [128, 8], F32, ph)
            ytmp = sb("aytmp", [128, D], F32, ph)
            if l == 1:
                wkv = sb("awkv", [128, 8, 2 * D + H], BF16, ph)
                kvtok = sb("akvtok", [128, 2, D], F32, ph)
                vbf = sb("avbf", [128, 2, D], BF16, ph)
                kTc = sb("akTc", [128, 8, CH], BF16, ph)
                lft = sb("alft", [128, H], F32, ph)
                lfe = sb("alfe", [128, H], F32, ph)

            P.dma("pool", lambda e: e.dma_start(out=w_in[:], in_=a_w_in[l].rearrange("(kc p) n -> p kc n", p=128)), w=["aw_in"])
            P.dma("pool", lambda e: e.dma_start(out=w_out[:], in_=a_w_out[l].rearrange("(kc p) n -> p kc n", p=128)), w=["aw_out"])
            P.dma("pool", lambda e: e.dma_start(out=wga[:], in_=a_w_ga[l].rearrange("n c d -> c n d")), w=["awga"])
            P.dma("pool", lambda e: e.dma_start(out=wgx[:], in_=a_w_gx[l].rearrange("n c d -> c n d")), w=["awgx"])
            P.dma("sp", lambda e: e.dma_start(out=gB[:], in_=rowv_d[l:l + 1, :].partition_broadcast(128)), w=["agB"])
            if l == 1:
                P.dma("pool", lambda e: e.dma_start(out=wkv[:], in_=w_kv.rearrange("(kc p) n -> p kc n", p=128)), w=["awkv"])
            P.op("dve", lambda e: e.memset(ubuf[:, :, 0:3], 0.0), w=["ubuf"])
            P.op("dve", lambda e: e.memset(hst[:], 0.0), w=["hst"])

            src = xp if l == 0 else x1s
            dst = x1s if l == 0 else x2s
            tpb = [ps[0], ps[1]]
            chunks = (list(range(NCH)) if do_prompt else []) + (["S"] if do_sample else [])
            def front(ch):
                sample = (ch == "S")
                slot = 0 if sample else ch % 2
                xkey = lambda tt, slot=slot: ("axt", slot)
                x_ap = lambda tt, slot=slot: xt[:, slot, tt, :]
                if sample:
                    if do_prompt:
                        P.dma("pool", lambda e: e.dma_start(out=rnn_out[l], in_=hst[:]), r=["hst"], w=[("rnn_out", l)])
                        P.dma("pool", lambda e: e.dma_start(out=conv_out[l], in_=ubuf[:, :, 0:3]), r=["ubuf"], w=[("conv_out", l)])
                    ssrc = xs_d if l == 0 else xs1
                    P.dma("sp", lambda e, slot=slot: e.dma_start(
                        out=xt[:, slot, :, :], in_=ssrc.rearrange("(t p) f -> p t f", p=128)), w=[("axt", slot)])
                    for oc in range(8):
                        P.dma("sp", lambda e, oc=oc: e.dma_start(out=ubv(oc)[:, :, 0:3], in_=st_conv_d[l, :, oc]),
                              r=[], w=["ubuf"])
                    P.dma("sp", lambda e: e.dma_start(out=hstS[:], in_=st_rnn_d[l]), w=["hstS"])
                else:
                    P.dma("sp", lambda e, ch=ch, slot=slot: e.dma_start(
                        out=xt[:, slot, :, :], in_=src[ch * CH:(ch + 1) * CH, :].rearrange("(t p) f -> p t f", p=128)),
                        w=[("axt", slot)])
                rms_to_featmajor(x_ap, xkey, 2, lambda kc: pv("apre", l, kc), xh, xnT, tpb, "a")
                for oc in range(16):
                    bank = ps[2 + oc % 2]
                    for kc in range(8):
                        P.op("pe", lambda e, oc=oc, kc=kc, bank=bank: e.matmul(
                            out=bank[:, 0:CH], lhsT=w_in[:, kc, oc * 128:(oc + 1) * 128], rhs=xnT[:, kc, :],
                            start=(kc == 0), stop=(kc == 7)), r=["aw_in", "axnT"], w=[("ps", id(bank))])
                    if oc < 8:
                        if sample:
                            P.op("act", lambda e, oc=oc, bank=bank: e.activation(out=ubv(oc)[:, :, 3:11], in_=v3(bank[:, 0:CH]),
                                                                                 func=AF.Copy),
                                 r=[("ps", id(bank))], w=["ubuf"])
                        else:
                            P.op("act", lambda e, oc=oc, bank=bank: e.activation(out=ubuf[:, oc, 3:3 + CH], in_=bank[:, 0:CH],
                                                                                 func=AF.Copy),
                                 r=[("ps", id(bank))], w=["ubuf"])
                    else:
                        g = oc - 8
                        P.op("act", lambda e, g=g, bank=bank: e.activation(out=thg[:, g, :], in_=bank[:, 0:CH], func=AF.Tanh,
                                                                           scale=0.5),
                             r=[("ps", id(bank))], w=["thg"])
                        P.op("dve", lambda e, g=g, bank=bank: e.scalar_tensor_tensor(
                            out=thg[:, g, :], in0=thg[:, g, :], scalar=1.0, in1=bank[:, 0:CH], op0=ALU.add, op1=ALU.mult),
                            r=[("ps", id(bank)), "thg"], w=["thg"])

            def mid(ch):
                sample = (ch == "S")
                slot = 0 if sample else ch % 2
                xkey = lambda tt, slot=slot: ("axt", slot)
                x_ap = lambda tt, slot=slot: xt[:, slot, tt, :]
                for oc in range(8):
                    if sample:
                        uin = lambda oc, j: ubv(oc)[:, :, j:j + 8]
                        uco = lambda oc: v3(uc[:, oc, :])
                    else:
                        uin = lambda oc, j: ubuf[:, oc, j:j + CH]
                        uco = lambda oc: uc[:, oc, :]
                    P.op("pool", lambda e, oc=oc, uin=uin, uco=uco: e.tensor_scalar(
                        out=uco(oc), in0=uin(oc, 3), scalar1=pv("cw3", l, oc), scalar2=pv("cb", l, oc),
                        op0=ALU.mult, op1=ALU.add), r=["ubuf", "pvec"], w=["uc"])
                    for j in range(3):
                        P.op("dve", lambda e, oc=oc, j=j, uin=uin, uco=uco: e.scalar_tensor_tensor(
                            out=uco(oc), in0=uin(oc, j), scalar=pv("cw%d" % j, l, oc), in1=uco(oc),
                            op0=ALU.mult, op1=ALU.add), r=["ubuf", "uc", "pvec"], w=["uc"])
                P.op("act", lambda e: e.activation(out=ucb[:], in_=uc[:], func=AF.Copy), r=["uc"], w=["ucb"])
                if sample:
                    for oc in range(8):
                        P.dma("pool", lambda e, oc=oc: e.dma_start(out=convs_out[l, :, oc], in_=ubv(oc)[:, :, 8:11]),
                              r=["ubuf"], w=[("convs_out", l, oc)])
                else:
                    P.op("pool", lambda e: e.tensor_copy(out=ubuf[:, :, 0:3], in_=ubuf[:, :, CH:CH + 3]), r=["ubuf"], w=["ubuf"])
                for oc in range(8):
                    bank = ps[4 + oc % 2]
                    P.op("pe", lambda e, oc=oc, bank=bank: e.matmul(out=bank[:, 0:CH], lhsT=wga[:, oc, :], rhs=ucb[:, oc, :],
                                                                    start=True, stop=True),
                         r=["awga", "ucb"], w=[("ps", id(bank))])
                    P.op("pe", lambda e, oc=oc, bank=bank: e.matmul(out=bank[:, CH:2 * CH], lhsT=wgx[:, oc, :], rhs=ucb[:, oc, :],
                                                                    start=True, stop=True),
                         r=["awgx", "ucb"], w=[("ps", id(bank))])
                    P.op("act", lambda e, oc=oc, bank=bank: e.activation(out=rbuf[:, oc, :], in_=bank[:, 0:CH], func=AF.Tanh,
                                                                         scale=0.5, bias=bgah[:, l, oc:oc + 1]),
                         r=[("ps", id(bank)), "bgah"], w=["rbuf"])
                    P.op("act", lambda e, oc=oc, bank=bank: e.activation(out=ibuf[:, oc, :], in_=bank[:, CH:2 * CH], func=AF.Tanh,
                                                                         scale=0.5, bias=bgxh[:, l, oc:oc + 1]),
                         r=[("ps", id(bank)), "bgxh"], w=["ibuf"])
                for oc in range(8):
                    P.op("act", lambda e, oc=oc: e.activation(out=sbuf_[:, oc, :], in_=rbuf[:, oc, :], func=AF.Exp,
                                                              scale=cneg2[:, l, oc:oc + 1], bias=cneg2[:, l, oc:oc + 1]),
                         r=["rbuf", "cneg2"], w=["sbuf"])
                    P.op("act", lambda e, oc=oc: e.activation(out=rbuf[:, oc, :], in_=rbuf[:, oc, :], func=AF.Exp,
                                                              scale=cnegh[:, l, oc:oc + 1], bias=cnegh[:, l, oc:oc + 1]),
                         r=["rbuf", "cnegh"], w=["rbuf"])
                P.op("act", lambda e: e.activation(out=sbuf_[:], in_=sbuf_[:], func=AF.Sqrt, scale=-1.0, bias=oneb[:, 0:1]),
                     r=["sbuf", "oneb"], w=["sbuf"])
                for oc in range(8):
                    P.op("dve", lambda e, oc=oc: e.scalar_tensor_tensor(
                        out=ibuf[:, oc, :], in0=ibuf[:, oc, :], scalar=1.0, in1=uc[:, oc, :], op0=ALU.add, op1=ALU.mult),
                        r=["ibuf", "uc"], w=["ibuf"])
                    P.op("dve", lambda e, oc=oc: e.scalar_tensor_tensor(
                        out=ibuf[:, oc, :], in0=ibuf[:, oc, :], scalar=0.5, in1=sbuf_[:, oc, :], op0=ALU.mult, op1=ALU.mult),
                        r=["ibuf", "sbuf"], w=["ibuf"])
                    if sample:
                        for sq in range(NS):
                            P.op("dve", lambda e, oc=oc, sq=sq: e.tensor_tensor_scan(
                                out=hbuf[:, oc, sq * 8:(sq + 1) * 8], data0=rbuf[:, oc, sq * 8:(sq + 1) * 8],
                                data1=ibuf[:, oc, sq * 8:(sq + 1) * 8], initial=hstS[:, oc, sq:sq + 1],
                                op0=ALU.mult, op1=ALU.add), r=["rbuf", "ibuf", "hstS"], w=["sbuf"])
                    else:
                        P.op("dve", lambda e, oc=oc: e.tensor_tensor_scan(
                            out=hbuf[:, oc, :], data0=rbuf[:, oc, :], data1=ibuf[:, oc, :], initial=hst[:, oc:oc + 1],
                            op0=ALU.mult, op1=ALU.add), r=["rbuf", "ibuf", "hst"], w=["sbuf"])
                    P.op("dve", lambda e, oc=oc: e.scalar_tensor_tensor(
                        out=mT[:, oc, :], in0=hbuf[:, oc, :], scalar=0.5, in1=thg[:, oc, :], op0=ALU.mult, op1=ALU.mult),
                        r=["sbuf", "thg"], w=["amT"])
                if sample:
                    P.op("pool", lambda e: e.tensor_copy(
                        out=hlast[:], in_=hbuf[:].rearrange("p o (s t) -> p o s t", t=8)[:, :, :, 7]), r=["sbuf"], w=["hlast"])
                    P.dma("pool", lambda e: e.dma_start(out=rnns_out[l], in_=hlast[:]), r=["hlast"], w=[("rnns_out", l)])
                else:
                    P.op("pool", lambda e: e.tensor_copy(out=hst[:], in_=hbuf[:, :, CH - 1]), r=["sbuf"], w=["hst"])

            def tail(ch):
                sample = (ch == "S")
                slot = 0 if sample else ch % 2
                xkey = lambda tt, slot=slot: ("axt", slot)
                x_ap = lambda tt, slot=slot: xt[:, slot, tt, :]
                for tt in range(2):
                    yb = [ps[6], ps[7]]
                    for fc in range(2):
                        for kc in range(8):
                            P.op("pe", lambda e, tt=tt, fc=fc, kc=kc: e.matmul(
                                out=yb[fc][:], lhsT=mT[:, kc, tt * 128:(tt + 1) * 128], rhs=w_out[:, kc, fc * 512:(fc + 1) * 512],
                                start=(kc == 0), stop=(kc == 7)), r=["amT", "aw_out"], w=[("ps", id(yb[fc]))])
                    post_norm_residual(yb, xt[:, slot, tt, :], ("axt", slot), gB, "agB", ytmp)
                if sample:
                    sdst = xs1 if l == 0 else xs2
                    P.dma("pool", lambda e, slot=slot: e.dma_start(
                        out=sdst.rearrange("(t p) f -> p t f", p=128), in_=xt[:, slot, :, :]),
                        r=[("axt", slot)], w=[("sdst", l)])
                else:
                    P.dma("pool", lambda e, ch=ch, slot=slot: e.dma_start(
                        out=dst[ch * CH:(ch + 1) * CH, :].rearrange("(t p) f -> p t f", p=128), in_=xt[:, slot, :, :]),
                        r=[("axt", slot)], w=[("dst", ch)])
                if l == 1:
                    rms_to_featmajor(x_ap, xkey, 2, lambda kc: pv("kvn", 0, kc), xh, xnT, tpb, "a")
                    for oc in range(8):
                        bank = ps[2 + oc % 2]
                        for kc in range(8):
                            P.op("pe", lambda e, oc=oc, kc=kc, bank=bank: e.matmul(
                                out=bank[:, 0:CH], lhsT=wkv[:, kc, oc * 128:(oc + 1) * 128], rhs=xnT[:, kc, :],
                                start=(kc == 0), stop=(kc == 7)), r=["awkv", "axnT"], w=[("ps", id(bank))])
                        P.op("act", lambda e, oc=oc, bank=bank: e.activation(out=kTc[:, oc, :], in_=bank[:, 0:CH], func=AF.Copy),
                             r=[("ps", id(bank))], w=["kTc"])
                    if sample:
                        P.op("pool", lambda e: e.tensor_copy(out=kTn[:], in_=kTc[:]), r=["kTc"], w=["kTn"])
                    else:
                        P.dma("pool", lambda e, ch=ch: e.dma_start(out=kTs[:, :, ch * CH:(ch + 1) * CH].rearrange("h p t -> p h t"),
                                                                   in_=kTc[:]), r=["kTc"], w=[("kTs", ch)])
                    ko, vo, lo = (ks_out, vs_out, lfs_out) if sample else (k_out, v_out, lf_out)
                    for tt in range(2):
                        tg = tt if sample else ch * 2 + tt
                        for part in range(2):
                            for fc in range(2):
                                bank = ps[4 + fc]
                                c0 = part * D + fc * 512
                                for kc in range(8):
                                    P.op("pe", lambda e, tt=tt, kc=kc, bank=bank, c0=c0: e.matmul(
                                        out=bank[:], lhsT=xnT[:, kc, tt * 128:(tt + 1) * 128], rhs=wkv[:, kc, c0:c0 + 512],
                                        start=(kc == 0), stop=(kc == 7)), r=["awkv", "axnT"], w=[("ps", id(bank))])
                                if fc == 0:
                                    P.op("act", lambda e, part=part, fc=fc, bank=bank: e.activation(
                                        out=kvtok[:, part, fc * 512:(fc + 1) * 512], in_=bank[:], func=AF.Copy),
                                        r=[("ps", id(bank))], w=["kvtok"])
                                else:
                                    P.op("dve", lambda e, part=part, fc=fc, bank=bank: e.tensor_copy(
                                        out=kvtok[:, part, fc * 512:(fc + 1) * 512], in_=bank[:]),
                                        r=[("ps", id(bank))], w=["kvtok"])
                                if part == 1:
                                    P.op("pool", lambda e, fc=fc, tt=tt: e.tensor_copy(
                                        out=vbf[:, tt, fc * 512:(fc + 1) * 512], in_=kvtok[:, 1, fc * 512:(fc + 1) * 512]),
                                        r=["kvtok"], w=["vbf"])
                        P.dma("pool", lambda e, tg=tg, ko=ko: e.dma_start(out=ko[tg * 128:(tg + 1) * 128, :], in_=kvtok[:, 0, :]),
                              r=["kvtok"], w=[("k_out", sample, tg)])
                        P.dma("pool", lambda e, tg=tg, vo=vo: e.dma_start(out=vo[tg * 128:(tg + 1) * 128, :], in_=kvtok[:, 1, :]),
                              r=["kvtok"], w=[("v_out", sample, tg)])
                        bank = ps[6]
                        for kc in range(8):
                            P.op("pe", lambda e, tt=tt, kc=kc, bank=bank: e.matmul(
                                out=bank[:, 0:H], lhsT=xnT[:, kc, tt * 128:(tt + 1) * 128], rhs=wkv[:, kc, 2 * D:2 * D + H],
                                start=(kc == 0), stop=(kc == 7)), r=["awkv", "axnT"], w=[("ps", id(bank))])
                        P.op("dve", lambda e, bank=bank: e.tensor_tensor(out=lfe[:], in0=bank[:, 0:H], in1=bfB[:, 0:H], op=ALU.add),
                             r=[("ps", id(bank)), "bfB"], w=["lfe"])
                        P.op("act", lambda e: e.activation(out=lfe[:], in_=lfe[:], func=AF.Exp, scale=-1.0), r=["lfe"], w=["lfe"])
                        P.op("act", lambda e: e.activation(out=lfe[:], in_=lfe[:], func=AF.Ln, bias=1.0), r=["lfe"], w=["lfe"])
                        P.op("dve", lambda e: e.tensor_scalar(out=lft[:], in0=lfe[:], scalar1=-1.0, scalar2=None, op0=ALU.mult),
                             r=["lfe"], w=["lft"])
                        P.dma("pool", lambda e, tg=tg, lo=lo: e.dma_start(out=lo[tg * 128:(tg + 1) * 128, :], in_=lft[:]),
                              r=["lft"], w=[("lf_out", sample, tg)])
                        if sample:
                            if tt == 0:
                                P.op("pool", lambda e: e.tensor_copy(
                                    out=vn1[:, :, 0:128], in_=vbf[:, 0, :].rearrange("p (h d) -> p h d", h=8)),
                                    r=["vbf"], w=["vn1"])
                                bank2 = ps[7]
                                P.op("pe", lambda e, bank2=bank2: e.matmul(out=bank2[:, 0:H], lhsT=scst[:, 2, 0:128], rhs=lft[:],
                                                                           start=True, stop=True),
                                     r=["scst", "lft"], w=[("ps", id(bank2))])
                                P.op("dve", lambda e, bank2=bank2: e.tensor_scalar(out=negE[:], in0=bank2[:, 0:H], scalar1=-1.0,
                                                                                  scalar2=None, op0=ALU.mult),
                                     r=[("ps", id(bank2))], w=["negE"])
                            continue
                        bank = ps[7]
                        P.op("pe", lambda e, bank=bank: e.matmul(out=bank[:, 0:H], lhsT=tri, rhs=lft[:], start=True, stop=False),
                             r=["cst", "lft"], w=[("ps", id(bank))])
                        P.op("pe", lambda e, bank=bank: e.matmul(out=bank[:, 0:H], lhsT=ones, rhs=lacc[:], start=False, stop=True),
                             r=["cst", "lacc"], w=[("ps", id(bank))])
                        P.op("dve", lambda e, tg=tg, bank=bank: e.tensor_copy(out=ck[:, tg, :], in_=bank[:, 0:H]),
                             r=[("ps", id(bank))], w=["ck"])
                        P.op("dve", lambda e: e.tensor_tensor(out=lacc[:], in0=lacc[:], in1=lft[:], op=ALU.add),
                             r=["lacc", "lft"], w=["lacc"])
                    for tt in range(2 if not sample else 0):
                        P.dma("pool", lambda e, ch=ch, tt=tt: e.dma_start(
                            out=vs[:, :, ch * 2 + tt, :].rearrange("h p d -> p h d"),
                            in_=vbf[:, tt, :].rearrange("p (h d) -> p h d", h=8)), r=["vbf"], w=[("vs", ch, tt)])

            for ci, ch in enumerate(chunks):
                if ci == 0:
                    front(ch)
                mid(ch)
                if ci + 1 < len(chunks):
                    front(chunks[ci + 1])
                tail(ch)
            if do_prompt and not do_sample:
                P.dma("pool", lambda e: e.dma_start(out=rnn_out[l], in_=hst[:]), r=["hst"], w=[("rnn_out", l)])
                P.dma("pool", lambda e: e.dma_start(out=conv_out[l], in_=ubuf[:, :, 0:3]), r=["ubuf"], w=[("conv_out", l)])
            if l == 1 and do_prompt:
                P.dma("pool", lambda e: e.dma_start(out=cs.rearrange("(t p) h -> p t h", p=128), in_=ck[:]), r=["ck"], w=["cs"])
            P.drain("sp")
            P.flush()

    oneb = sb("oneb", [128, 1])
    P.op("dve", lambda e: e.memset(oneb[:], 1.0), w=["oneb"])

    a_pass(0)
    a_pass(1)

    def b_phase():
        with ExitStack() as ph:
            xo = sb("bxo", [128, 16, D], F32, ph)
            RB = sb("bRB", [128, 16, H], F32, ph)
            QT = sb("bQT", [128, 8, 2048], BF16, ph)
            sgT = sb("bsgT", [128, 8, 2048], BF16, ph)
            gB = sb("bgB", [128, D], F32, ph)
            ytmp = sb("bytmp", [128, D], F32, ph)
            for j in range(16):
                P.dma("pool", lambda e, j=j: e.indirect_dma_start(
                    out=xo[:, j, :], out_offset=None, in_=x2s,
                    in_offset=bass.IndirectOffsetOnAxis(ap=idx[:, j:j + 1], axis=0)), r=["idx"], w=[("bxo", j)])
                P.dma("pool", lambda e, j=j: e.indirect_dma_start(
                    out=RB[:, j, :], out_offset=None, in_=cs,
                    in_offset=bass.IndirectOffsetOnAxis(ap=idx[:, 16 + j:17 + j], axis=0)), r=["idx"], w=["bRB"])
            for l in range(2):
                with ExitStack() as p1:
                    w_in = sb("bw_in", [128, 8, 2 * D], BF16, p1)
                    xh = sb("bxh", [128, 4, D], BF16, p1)
                    xnT = sb("bxnT", [128, 8, 512], BF16, p1)
                    thg = sb("bthg", [128, 2, 512], F32, p1)
                    P.dma("pool", lambda e: e.dma_start(out=w_in[:], in_=b_w_in[l].rearrange("(kc p) n -> p kc n", p=128)),
                          w=["bw_in"])
                    P.dma("sp", lambda e: e.dma_start(out=gB[:], in_=rowv_d[2 + l:3 + l, :].partition_broadcast(128)), w=["bgB"])
                    for grp in range(4):
                        x_ap = lambda tt, grp=grp: xo[:, grp * 4 + tt, :]
                        xkey = lambda tt, grp=grp: ("bxo", grp * 4 + tt)
                        rms_to_featmajor(x_ap, xkey, 4, lambda kc: pv("bpre", l, kc), xh, xnT, [ps[0], ps[1], ps[2], ps[3]], "b")
                        for oc in range(16):
                            bank = ps[4 + oc % 2]
                            for kc in range(8):
                                P.op("pe", lambda e, oc=oc, kc=kc, bank=bank: e.matmul(
                                    out=bank[:], lhsT=w_in[:, kc, oc * 128:(oc + 1) * 128], rhs=xnT[:, kc, :],
                                    start=(kc == 0), stop=(kc == 7)), r=["bw_in", "bxnT"], w=[("ps", id(bank))])
                            if oc < 8:
                                P.op("act", lambda e, oc=oc, bank=bank, grp=grp: e.activation(
                                    out=QT[:, oc, grp * 512:(grp + 1) * 512], in_=bank[:], func=AF.Copy),
                                    r=[("ps", id(bank))], w=["bQT"])
                            else:
                                g = oc - 8
                                ts_ = g % 2
                                P.op("act", lambda e, bank=bank, ts_=ts_: e.activation(out=thg[:, ts_, :], in_=bank[:], func=AF.Tanh,
                                                                                       scale=0.5),
                                     r=[("ps", id(bank))], w=[("bthg", ts_)])
                                P.op("dve", lambda e, bank=bank, ts_=ts_: e.scalar_tensor_tensor(
                                    out=thg[:, ts_, :], in0=thg[:, ts_, :], scalar=1.0, in1=bank[:], op0=ALU.add, op1=ALU.mult),
                                    r=[("ps", id(bank)), ("bthg", ts_)], w=[("bthg", ts_)])
                                P.op("pool", lambda e, g=g, grp=grp, ts_=ts_: e.tensor_scalar(
                                    out=sgT[:, g, grp * 512:(grp + 1) * 512], in0=thg[:, ts_, :], scalar1=0.5, scalar2=0.0,
                                    op0=ALU.mult, op1=ALU.add), r=[("bthg", ts_)], w=["bsgT"])
                    P.drain("sp")
                    P.flush()
                with ExitStack() as p2:
                    kT = sb("bkT", [128, 2, T], BF16, p2)
                    vv = sb("bvv", [128, 2, NT, 2, 65], BF16, p2)
                    bias = sb("bbias", [128, 2, NT, H], F32, p2)
                    sm = sb("bsm", [128, 2, 128], F32, p2)
                    pT = sb("bpT", [128, 4, 128], BF16, p2)
                    on = sb("bon", [128, 2, 128], BF16, p2)
                    rden = sb("brden", [128, 2, 2], F32, p2)
                    P.op("pool", lambda e: e.memset(vv[:, :, :, :, 64:65], 1.0), w=[("bvv", 0), ("bvv", 1)])
                    for hp in range(8):
                        sl = hp % 2
                        P.dma("sp", lambda e, hp=hp, sl=sl: e.dma_start(out=kT[:, sl, :], in_=kTs[hp]),
                              w=[("bkT", sl)])
                        P.dma("sp", lambda e, hp=hp, sl=sl: e.dma_start(
                            out=vv[:, sl, :, :, 0:64], in_=vs[hp].rearrange("p t (e d) -> p t e d", e=2)),
                            w=[("bvv", sl)])
                        for j in range(16):
                            nk = 2 * j + 2
                            bs = j % 2
                            P.op("dve", lambda e, j=j, nk=nk, bs=bs: e.tensor_tensor(
                                out=bias[:, bs, 0:nk, :], in0=RB[:, j:j + 1, :].to_broadcast([128, nk, H]), in1=ck[:, 0:nk, :],
                                op=ALU.subtract), r=["bRB", "ck"], w=[("bbias", bs)])
                            obs = [ps[6], ps[7]]
                            items = [(ee, kt) for kt in range(nk) for ee in range(2)]

                            def emit_s(n, j=j, hp=hp, sl=sl, nk=nk, bs=bs):
                                ee, kt = items[n]
                                h = hp * 2 + ee
                                sbk = ps[n % 4]
                                pslot = n % 4
                                P.op("pe", lambda e: e.matmul(
                                    out=sbk[:, 0:128], lhsT=kT[ee * 64:(ee + 1) * 64, sl, kt * 128:(kt + 1) * 128],
                                    rhs=QT[ee * 64:(ee + 1) * 64, hp, j * 128:(j + 1) * 128], start=True, stop=True),
                                    r=[("bkT", sl), "bQT"], w=[("ps", id(sbk))])
                                if kt >= nk - 2:
                                    mk = mask_a if kt == nk - 2 else mask_b
                                    ms = kt - (nk - 2)
                                    P.op("dve", lambda e: e.scalar_tensor_tensor(
                                        out=sm[:, ms, :], in0=sbk[:, 0:128], scalar=0.125, in1=mk, op0=ALU.mult, op1=ALU.add),
                                        r=[("ps", id(sbk)), "cst"], w=[("bsm", ms)])
                                    P.op("act", lambda e: e.activation(
                                        out=pT[:, pslot, :], in_=sm[:, ms, :], func=AF.Exp, bias=bias[:, bs, kt, h:h + 1]),
                                        r=[("bsm", ms), ("bbias", bs)], w=[("bpT", pslot)])
                                else:
                                    P.op("act", lambda e: e.activation(
                                        out=pT[:, pslot, :], in_=sbk[:, 0:128], func=AF.Exp, scale=0.125,
                                        bias=bias[:, bs, kt, h:h + 1]),
                                        r=[("ps", id(sbk)), ("bbias", bs)], w=[("bpT", pslot)])

                            def emit_pv(n, sl=sl, nk=nk, obs=obs):
                                ee, kt = items[n]
                                pslot = n % 4
                                ob = obs[ee]
                                P.op("pe", lambda e: e.matmul(
                                    out=ob[:, 0:65], lhsT=pT[:, pslot, :], rhs=vv[:, sl, kt, ee, :],
                                    start=(kt == 0), stop=(kt == nk - 1)),
                                    r=[("bpT", pslot), ("bvv", sl)], w=[("ps", id(ob))])
                            LAG = 2
                            for n in range(len(items) + LAG):
                                if n < len(items):
                                    emit_s(n)
                                if n >= LAG:
                                    emit_pv(n - LAG)
                            osl = j % 2
                            for ee in range(2):
                                P.op("dve", lambda e, ob=obs[ee], osl=osl, ee=ee: e.reciprocal(
                                    out=rden[:, osl, ee:ee + 1], in_=ob[:, 64:65]),
                                    r=[("ps", id(obs[ee]))], w=[("brden", osl, ee)])
                                P.op("dve", lambda e, ob=obs[ee], osl=osl, ee=ee: e.tensor_scalar(
                                    out=on[:, osl, ee * 64:(ee + 1) * 64], in0=ob[:, 0:64],
                                    scalar1=rden[:, osl, ee:ee + 1], scalar2=None, op0=ALU.mult),
                                    r=[("ps", id(obs[ee])), ("brden", osl, ee)], w=[("bon", osl)])
                            tb = ps[4 + j % 2]
                            tbv = tb[:].bitcast(BF16)
                            P.op("pe", lambda e, osl=osl, tbv=tbv: e.transpose(out=tbv[:, 0:128], in_=on[:, osl, :], identity=identb[:]),
                                 r=[("bon", osl), "identb"], w=[("ps", id(tb))])
                            P.op("dve", lambda e, tbv=tbv, hp=hp, j=j: e.tensor_tensor(
                                out=sgT[:, hp, j * 128:(j + 1) * 128], in0=tbv[:, 0:128], in1=sgT[:, hp, j * 128:(j + 1) * 128],
                                op=ALU.mult), r=[("ps", id(tb)), "bsgT"], w=["bsgT"])
                    P.drain("sp")
                    P.flush()
                with ExitStack() as p3:
                    w_out = sb("bw_out", [128, 8, D], BF16, p3)
                    P.dma("pool", lambda e: e.dma_start(out=w_out[:], in_=b_w_out[l].rearrange("(kc p) n -> p kc n", p=128)),
                          w=["bw_out"])
                    for j in range(16):
                        yb = [ps[6], ps[7]]
                        for fc in range(2):
                            for kc in range(8):
                                P.op("pe", lambda e, j=j, fc=fc, kc=kc, yb=yb: e.matmul(
                                    out=yb[fc][:], lhsT=sgT[:, kc, j * 128:(j + 1) * 128], rhs=w_out[:, kc, fc * 512:(fc + 1) * 512],
                                    start=(kc == 0), stop=(kc == 7)), r=["bsgT", "bw_out"], w=[("ps", id(yb[fc]))])
                        post_norm_residual(yb, xo[:, j, :], ("bxo", j), gB, "bgB", ytmp)
                        if l == 1:
                            P.dma("sp", lambda e, j=j: e.dma_start(out=y_own[j], in_=xo[:, j, :]), r=[("bxo", j)],
                                  w=[("y_own", j)])
                    P.drain("sp")
                    P.flush()

    def s_phase():
        RG = [list(range(8))]
        with ExitStack() as ph:
            xs_t = sb("sxs", [128, 2, D], F32, ph)
            Dall = sb("sDall", [128, NS, 64, 2], F32, ph)
            P.op("dve", lambda e: e.memset(ytmp[:], 0.0), w=["ytmp"])
            for i_ in range(2):
                for t_ in range(2):
                    P.dma("sp", lambda e, i_=i_, t_=t_: e.dma_start(out=o_scr[i_][t_ * 128:(t_ + 1) * 128, :], in_=ytmp[:]),
                          r=["ytmp"], w=[("o_scr", i_)])
            gB = sb("sgB", [128, D], F32, ph)
            ytmp = sb("sytmp", [128, D], F32, ph)
            onesf = sb("sonesf", [128, 64], F32, ph)
            P.dma("sp", lambda e: e.dma_start(out=xs_t[:], in_=xs2.rearrange("(t p) f -> p t f", p=128)), w=["sxs"])
            P.op("dve", lambda e: e.memset(onesf[:], 1.0), w=["sonesf"])
            with ExitStack() as p0:
                Lg = sb("sLg", [128, 2, 2, 64], F32, p0)
                pre = sb("spre", [128, 2, 2, 64], F32, p0)
                LT = sb("sLT", [128, 2, 2], F32, p0)
                TT = sb("sTT", [128, 2, 2], F32, p0)
                zc = sb("szc", [128, 1], F32, p0)
                P.op("dve", lambda e: e.memset(zc[:], 0.0), w=["szc"])
                for sq in range(NS):
                    sl = sq % 2
                    P.dma("pool", lambda e, sq=sq, sl=sl: e.indirect_dma_start(
                        out=Lg[:, sl, :, :].rearrange("p a b -> p (a b)"), out_offset=None, in_=clf_full,
                        in_offset=bass.IndirectOffsetOnAxis(ap=idxL[:, sq // 8, sq % 8:sq % 8 + 1], axis=0)), r=["idxL"], w=[("sLg", sl)])
                    for ee in range(2):
                        P.op("dve", lambda e, sl=sl, ee=ee: e.tensor_tensor_scan(
                            out=pre[:, sl, ee, :], data0=onesf[:], data1=Lg[:, sl, ee, :], initial=zc[:, 0:1],
                            op0=ALU.mult, op1=ALU.add), r=[("sLg", sl), "sonesf", "szc"], w=[("spre", sl)])
                    bank = ps[sq % 2]
                    P.op("dve", lambda e, sl=sl: e.tensor_copy(out=TT[:, sl, :], in_=pre[:, sl, :, 63]),
                         r=[("spre", sl)], w=[("sTT", sl)])
                    P.op("pe", lambda e, sl=sl, bank=bank: e.matmul(out=bank[:, 0:2], lhsT=scst[:, 0, 0:128], rhs=TT[:, sl, :],
                                                                    start=True, stop=True),
                         r=["scst", ("sTT", sl)], w=[("ps", id(bank))])
                    P.op("dve", lambda e, sl=sl, bank=bank: e.tensor_tensor(out=LT[:, sl, :], in0=bank[:, 0:2], in1=TT[:, sl, :],
                                                                            op=ALU.add),
                         r=[("ps", id(bank)), ("sTT", sl)], w=[("sLT", sl)])
                    for ee in range(2):
                        P.op("dve", lambda e, sl=sl, ee=ee, sq=sq: e.tensor_scalar(
                            out=Dall[:, sq, :, ee], in0=pre[:, sl, ee, :], scalar1=-1.0, scalar2=LT[:, sl, ee:ee + 1],
                            op0=ALU.mult, op1=ALU.add), r=[("spre", sl), ("sLT", sl)], w=["sDall"])
                if debug:
                    P.dma("sp", lambda e: e.dma_start(out=dbg[:, 2836:2964], in_=Lg[:, 1].rearrange("p a b -> p (a b)")),
                          r=[("sLg", 1)], w=["dbg5"])
                    P.dma("sp", lambda e: e.dma_start(out=dbg[:, 3092:3220], in_=pre[:, 1].rearrange("p a b -> p (a b)")),
                          r=[("spre", 1)], w=["dbg6"])
                    P.dma("sp", lambda e: e.dma_start(out=dbg[:, 3348:3352], in_=LT[:].rearrange("p a b -> p (a b)")),
                          r=[("sLT", 0), ("sLT", 1)], w=["dbg7"])
                    P.dma("sp", lambda e: e.dma_start(out=dbg[:, 3352:3356], in_=TT[:].rearrange("p a b -> p (a b)")),
                          r=[("sTT", 0), ("sTT", 1)], w=["dbg8"])
                    P.dma("sp", lambda e: e.dma_start(out=dbg[:, 3356:3388].bitcast(I32), in_=idxL[:].rearrange("p a b -> p (a b)")),
                          r=["idxL"], w=["dbg9"])
                P.drain("sp")
                P.flush()
            for l in range(2):
                with ExitStack() as p1:
                    wg = sb("swg", [128, 8, D], BF16, p1)
                    wq = sb("swq", [128, 8, D], BF16, p1)
                    w_out = sb("sw_out", [128, 8, D], BF16, p1)
                    xh = sb("sxh", [128, 2, D], BF16, p1)
                    xnT = sb("sxnT", [128, 8, 256], BF16, p1)
                    sg = sb("ssg", [128, 2, D], BF16, p1)
                    thg = sb("sthg", [128, 512], F32, p1)
                    Qbd = sb("sQbd", [128, NS, 16], BF16, p1)
                    Kst = sb("sKst", [128, 2, 32, 128], F32, p1)
                    Vst = sb("sVst", [128, 1, 32, 128], F32, p1)
                    Vbf = sb("sVbf", [128, 2, 32, 129], BF16, p1)
                    KT = sb("sKT", [128, 2, 32, 128], BF16, p1)
                    Ssb = sb("sSsb", [128, 2, 512], F32, p1)
                    Psb = sb("sPsb", [128, 2, 512], BF16, p1)
                    Sn = sb("sSn", [128, 16], F32, p1)
                    Pn = sb("sPn", [128, 16], BF16, p1)
                    Osb = sb("sOsb", [16, 2, 130], F32, p1)
                    ofull = Kst[:, 0, 0:16, :].rearrange("p (a b) c -> p a (b c)", a=2)
                    mtok = sb("smtok", [128, 2, D], BF16, p1)
                    mT = sb("smT", [128, 8, 256], BF16, p1)
                    P.dma("pool", lambda e: e.dma_start(out=wg[:], in_=b_w_in[l][:, D:2 * D].rearrange("(kc p) n -> p kc n", p=128)),
                          w=["swg"])
                    P.dma("pool", lambda e: e.dma_start(out=wq[:], in_=b_w_in[l][:, 0:D].rearrange("(kc p) n -> p kc n", p=128)),
                          w=["swq"])
                    P.dma("pool", lambda e: e.dma_start(out=w_out[:], in_=b_w_out[l].rearrange("(kc p) n -> p kc n", p=128)),
                          w=["sw_out"])
                    P.dma("sp", lambda e: e.dma_start(out=gB[:], in_=rowv_d[2 + l:3 + l, :].partition_broadcast(128)), w=["sgB"])
                    P.op("pool", lambda e: e.memset(Vbf[:, :, :, 128:129], 1.0), w=[("sVbf", 0), ("sVbf", 1)])
                    P.op("pool", lambda e: e.memset(Qbd[:], 0.0), w=["sQbd"])
                    x_ap = lambda tt: xs_t[:, tt, :]
                    xkey = lambda tt: "sxs"
                    rms_to_featmajor(x_ap, xkey, 2, lambda kc: pv("bpre", l, kc), xh, xnT, [ps[0], ps[1]], "s")
                    Qv = Qbd[:].rearrange("p (s h) c -> p s h c", h=8)
                    for hp in range(8):
                        bank = ps[2 + hp % 2]
                        for kc in range(8):
                            P.op("pe", lambda e, kc=kc, hp=hp, bank=bank: e.matmul(
                                out=bank[:, 0:256], lhsT=wq[:, kc, hp * 128:(hp + 1) * 128], rhs=xnT[:, kc, :],
                                start=(kc == 0), stop=(kc == 7)), r=["swq", "sxnT"], w=[("ps", id(bank))])
                        for ee in range(2):
                            P.op("dve", lambda e, ee=ee, hp=hp, bank=bank: e.tensor_scalar(
                                out=Qv[ee * 64:(ee + 1) * 64, :, hp, ee * 8:(ee + 1) * 8],
                                in0=bank[ee * 64:(ee + 1) * 64, 0:32].rearrange("p (s t) -> p s t", t=8),
                                scalar1=0.125, scalar2=None, op0=ALU.mult), r=[("ps", id(bank))], w=["sQbd"])
                    for tt in range(2):
                        for fc in range(2):
                            bank = ps[4 + fc]
                            for kc in range(8):
                                P.op("pe", lambda e, kc=kc, tt=tt, fc=fc, bank=bank: e.matmul(
                                    out=bank[:], lhsT=xnT[:, kc, tt * 128:(tt + 1) * 128], rhs=wg[:, kc, fc * 512:(fc + 1) * 512],
                                    start=(kc == 0), stop=(kc == 7)), r=["swg", "sxnT"], w=[("ps", id(bank))])
                            P.op("act", lambda e, bank=bank: e.activation(out=thg[:], in_=bank[:], func=AF.Tanh, scale=0.5),
                                 r=[("ps", id(bank))], w=["sthg"])
                            P.op("dve", lambda e, bank=bank: e.scalar_tensor_tensor(
                                out=thg[:], in0=thg[:], scalar=1.0, in1=bank[:], op0=ALU.add, op1=ALU.mult),
                                r=[("ps", id(bank)), "sthg"], w=["sthg"])
                            P.op("pool", lambda e, tt=tt, fc=fc: e.tensor_scalar(
                                out=sg[:, tt, fc * 512:(fc + 1) * 512], in0=thg[:], scalar1=0.5, scalar2=0.0,
                                op0=ALU.mult, op1=ALU.add), r=["sthg"], w=["ssg"])
                    for sq in range(NS):
                        ob = ps[6 + sq % 2]
                        osl = sq % 2
                        for half in range(2):
                            bs = (sq * 2 + half) % 2
                            P.dma("pool", lambda e, sq=sq, half=half, bs=bs: e.indirect_dma_start(
                                out=Kst[:, bs, :, :].rearrange("p a b -> p (a b)"), out_offset=None, in_=ck_full,
                                in_offset=bass.IndirectOffsetOnAxis(ap=idxK[:, half, sq // 8, sq % 8:sq % 8 + 1], axis=0)),
                                r=["idxK"], w=[("sKst", bs)])
                            P.dma("pool", lambda e, sq=sq, half=half, bs=bs: e.indirect_dma_start(
                                out=Vst[:, 0, :, :].rearrange("p a b -> p (a b)"), out_offset=None, in_=cv_full,
                                in_offset=bass.IndirectOffsetOnAxis(ap=idxK[:, half, sq // 8, sq % 8:sq % 8 + 1], axis=0)),
                                r=["idxK"], w=[("sVst", 0)])
                            P.op("act", lambda e, bs=bs: e.activation(out=Vbf[:, bs, :, 0:128], in_=Vst[:, 0, :, :], func=AF.Copy),
                                 r=[("sVst", 0)], w=[("sVbf", bs)])
                            for g4 in range(8):
                                tb = ps[g4 % 2]
                                for k4 in range(4):
                                    t = g4 * 4 + k4
                                    P.op("pe", lambda e, bs=bs, t=t, k4=k4, tb=tb: e.transpose(
                                        out=tb[:, k4 * 128:(k4 + 1) * 128], in_=Kst[:, bs, t, :], identity=ident),
                                        r=[("sKst", bs), "cst"], w=[("ps", id(tb))])
                                if g4 % 2 == 0:
                                    P.op("act", lambda e, bs=bs, g4=g4, tb=tb: e.activation(
                                        out=KT[:, bs, g4 * 4:(g4 + 1) * 4, :], in_=tb[:].rearrange("p (a b) -> p a b", b=128),
                                        func=AF.Copy), r=[("ps", id(tb))], w=[("sKT", bs)])
                                else:
                                    P.op("dve", lambda e, bs=bs, g4=g4, tb=tb: e.tensor_copy(
                                        out=KT[:, bs, g4 * 4:(g4 + 1) * 4, :], in_=tb[:].rearrange("p (a b) -> p a b", b=128)),
                                        r=[("ps", id(tb))], w=[("sKT", bs)])
                            sbk = ps[2 + bs]
                            for t in range(32):
                                P.op("pe", lambda e, bs=bs, t=t, sq=sq, sbk=sbk: e.matmul(
                                    out=sbk[:, t * 16:(t + 1) * 16], lhsT=KT[:, bs, t, :], rhs=Qbd[:, sq, :], start=True, stop=True),
                                    r=[("sKT", bs), "sQbd"], w=[("ps", id(sbk))])
                            P.op("dve", lambda e, bs=bs, sq=sq, half=half, sbk=sbk: e.tensor_tensor(
                                out=Ssb[:, bs, :].rearrange("p (a q) -> p a q", q=8), in0=sbk[:].rearrange("p (a q) -> p a q", q=8),
                                in1=Dall[:, sq, half * 32:(half + 1) * 32, :].rearrange("p t e -> p (t e)").unsqueeze(2).to_broadcast([128, 64, 8]),
                                op=ALU.add), r=[("ps", id(sbk)), "sDall"], w=[("sSsb", bs)])
                            P.op("act", lambda e, bs=bs: e.activation(out=Psb[:, bs, :], in_=Ssb[:, bs, :], func=AF.Exp),
                                 r=[("sSsb", bs)], w=[("sPsb", bs)])
                            for t in range(32):
                                P.op("pe", lambda e, bs=bs, t=t, half=half, ob=ob: e.matmul(
                                    out=ob[0:16, 0:129], lhsT=Psb[:, bs, t * 16:(t + 1) * 16], rhs=Vbf[:, bs, t, :],
                                    start=(half == 0 and t == 0), stop=False),
                                    r=[("sPsb", bs), ("sVbf", bs)], w=[("ps", id(ob))])
                        sl_, hp_ = sq // 8, sq % 8
                        nb = ps[4]
                        P.op("pe", lambda e, hp_=hp_, sq=sq, nb=nb: e.matmul(out=nb[:, 0:16], lhsT=kTn[:, hp_, 0:128],
                                                                            rhs=Qbd[:, sq, :], start=True, stop=True),
                             r=["kTn", "sQbd"], w=[("ps", id(nb))])
                        P.op("dve", lambda e, sl_=sl_, nb=nb: e.tensor_tensor(
                            out=Sn[:], in0=nb[:, 0:16], in1=scst[:, 1, sl_ * 16:(sl_ + 1) * 16], op=ALU.add),
                            r=[("ps", id(nb)), "scst"], w=["sSn"])
                        P.op("dve", lambda e, hp_=hp_: e.tensor_tensor(
                            out=Sn[:].rearrange("p (e q) -> p e q", q=8), in0=Sn[:].rearrange("p (e q) -> p e q", q=8),
                            in1=negE[:, 2 * hp_:2 * hp_ + 2].unsqueeze(2).to_broadcast([128, 2, 8]), op=ALU.add),
                            r=["sSn", "negE"], w=["sSn"])
                        P.op("act", lambda e: e.activation(out=Pn[:], in_=Sn[:], func=AF.Exp), r=["sSn"], w=["sPn"])
                        P.op("pe", lambda e, hp_=hp_, ob=ob: e.matmul(out=ob[0:16, 0:129], lhsT=Pn[:], rhs=vn1[:, hp_, :],
                                                                      start=False, stop=True),
                             r=["sPn", "vn1"], w=[("ps", id(ob))])
                        P.op("dve", lambda e, ob=ob, osl=osl: e.reciprocal(out=Osb[:, osl, 129:130], in_=ob[0:16, 128:129]),
                             r=[("ps", id(ob))], w=[("sOsb", osl)])
                        P.op("dve", lambda e, ob=ob, osl=osl: e.tensor_scalar(
                            out=Osb[:, osl, 0:128], in0=ob[0:16, 0:128], scalar1=Osb[:, osl, 129:130], scalar2=None, op0=ALU.mult),
                            r=[("ps", id(ob)), ("sOsb", osl)], w=[("sOsb", osl)])
                        for ee in range(2):
                            P.dma("sp", lambda e, sq=sq, ee=ee, osl=osl: e.dma_start(
                                out=o_scr[l][(sq // 8) * 8:(sq // 8 + 1) * 8, (sq % 8) * 128 + ee * 64:(sq % 8) * 128 + (ee + 1) * 64],
                                in_=Osb[ee * 8:(ee + 1) * 8, osl, ee * 64:(ee + 1) * 64]),
                                r=[("sOsb", osl), ("o_scr", l)], w=[("o_scrw", l)])
                    P.dma("sp", lambda e: e.dma_start(out=ofull, in_=o_scr[l].rearrange("(t p) f -> p t f", p=128)),
                          r=[("o_scrw", l), ("o_scr", l)], w=[("sKst", 0)])
                    if debug and l == 0:
                        P.dma("sp", lambda e: e.dma_start(out=dbg[:, 0:512], in_=Dall[:, 0:4, :, :].rearrange("p a b c -> p (a b c)")),
                              r=["sDall"], w=["dbg0"])
                        P.dma("sp", lambda e: e.dma_start(out=dbg[:, 512:1536], in_=ofull[:, 0, :]), r=[("sKst", 0)], w=["dbg1"])
                        P.dma("sp", lambda e: e.dma_start(out=dbg[:, 1536:2560], in_=Ssb[:].rearrange("p a b -> p (a b)")),
                              r=[("sSsb", 0), ("sSsb", 1)], w=["dbg2"])
                        P.dma("sp", lambda e: e.dma_start(out=dbg[0:16, 2560:2816].rearrange("p (a b) -> p a b", a=2), in_=Osb[:, :, 0:128]),
                              r=[("sOsb", 0), ("sOsb", 1)], w=["dbg3"])
                        P.dma("sp", lambda e: e.dma_start(out=dbg[:, 2820:2836], in_=Sn[:]), r=["sSn"], w=["dbg4"])
                    for tt in range(2):
                        P.op("dve", lambda e, tt=tt: e.tensor_tensor(out=mtok[:, tt, :], in0=ofull[:, tt, :], in1=sg[:, tt, :],
                                                                     op=ALU.mult), r=[("sKst", 0), "ssg"], w=["smtok"])
                    for bi in range(2):
                        bank = ps[bi]
                        bv = bank[:].bitcast(BF16)
                        for kk in range(4):
                            kc = bi * 4 + kk
                            for tt in range(2):
                                P.op("pe", lambda e, kc=kc, kk=kk, tt=tt, bv=bv: e.transpose(
                                    out=bv[:, kk * 256 + tt * 128: kk * 256 + (tt + 1) * 128],
                                    in_=mtok[:, tt, kc * 128:(kc + 1) * 128], identity=identb[:]),
                                    r=["smtok", "identb"], w=[("ps", id(bank))])
                        P.op("dve", lambda e, bi=bi, bv=bv: e.tensor_copy(
                            out=mT[:, bi * 4:(bi + 1) * 4, :], in_=bv[:].rearrange("p (a b) -> p a b", b=256)),
                            r=[("ps", id(bank))], w=["smT"])
                    for tt in range(2):
                        yb = [ps[6], ps[7]]
                        for fc in range(2):
                            for kc in range(8):
                                P.op("pe", lambda e, tt=tt, fc=fc, kc=kc, yb=yb: e.matmul(
                                    out=yb[fc][:], lhsT=mT[:, kc, tt * 128:(tt + 1) * 128], rhs=w_out[:, kc, fc * 512:(fc + 1) * 512],
                                    start=(kc == 0), stop=(kc == 7)), r=["smT", "sw_out"], w=[("ps", id(yb[fc]))])
                        post_norm_residual(yb, xs_t[:, tt, :], "sxs", gB, "sgB", ytmp)
                    if l == 1:
                        P.dma("sp", lambda e: e.dma_start(out=ys_out.rearrange("(t p) f -> p t f", p=128), in_=xs_t[:]),
                              r=["sxs"], w=["ys_out"])
                    P.drain("sp")
                    P.flush()

    if do_prompt:
        b_phase()
    if do_sample:
        s_phase()

    P.drain("sp")
    P.flush()
    st.close()
    return nc


_NC_CACHE = {}


def kernel(x_prompt, x_sample, cache_k, cache_v, cache_logf, state_rnn, state_conv, page_table,
           a_pre_norm, a_post_norm, a_w_in, a_conv_w, a_conv_b, a_w_ga, a_b_ga, a_w_gx, a_b_gx,
           a_lambda, a_w_out, kv_norm, w_kv, b_f, b_pre_norm, b_post_norm, b_w_in, b_w_out,
           _trace=False, _do_prompt=True, _debug=False):
    f32 = np.float32
    A = lambda v: np.ascontiguousarray(np.asarray(v, f32))
    npool = int(np.asarray(cache_k).shape[0])
    key = ("prog", npool, _do_prompt, _debug)
    if key not in _NC_CACHE:
        _NC_CACHE[key] = build_program(_do_prompt, True, npool=npool, debug=_debug)
    nc = _NC_CACHE[key]

    pvec = np.zeros((128, NPV), f32)
    for l in range(2):
        pvec[:, PV[("apre", l)]:PV[("apre", l)] + 8] = fm(a_pre_norm[l])
        for j in range(4):
            pvec[:, PV[("cw%d" % j, l)]:PV[("cw%d" % j, l)] + 8] = fm(np.asarray(a_conv_w)[l, j])
        pvec[:, PV[("cb", l)]:PV[("cb", l)] + 8] = fm(a_conv_b[l])
        pvec[:, PV[("bga", l)]:PV[("bga", l)] + 8] = fm(a_b_ga[l])
        pvec[:, PV[("bgx", l)]:PV[("bgx", l)] + 8] = fm(a_b_gx[l])
        pvec[:, PV[("lam", l)]:PV[("lam", l)] + 8] = fm(a_lambda[l])
        pvec[:, PV[("bpre", l)]:PV[("bpre", l)] + 8] = fm(b_pre_norm[l])
    pvec[:, PV[("kvn", 0)]:PV[("kvn", 0)] + 8] = fm(kv_norm)
    rowv = np.zeros((5, D), f32)
    rowv[0:2] = A(a_post_norm); rowv[2:4] = A(b_post_norm); rowv[4, 0:H] = A(b_f)

    ii = np.arange(128)
    ident = np.eye(128, dtype=f32)
    tri = (ii[:, None] <= ii[None, :]).astype(f32)
    ones = np.ones((128, 128), f32)
    causal = np.where(ii[:, None] <= ii[None, :], 0.0, NEG).astype(f32)
    full_ok = np.zeros((128, 128), f32)
    full_no = np.full((128, 128), NEG, f32)

    scst = np.zeros((128, 3, 256), f32)
    order = (ii % 64) * 2 + ii // 64
    scst[:, 0, 0:128] = (order[:, None] > order[None, :]).astype(f32)
    scst[:, 0, 128] = ii // 64
    scst[:, 0, 129] = 2 * (ii // 64)
    scst[:, 0, 130] = 2 * (ii // 64) + 1
    ms = np.full((128, 16, 2, 8), NEG, f32)
    for p_ in range(128):
        s_, t_ = p_ // 8, p_ % 8
        ms[p_, s_, :, t_:] = 0.0
    scst[:, 1, :] = ms.reshape(128, 256)
    scst[:, 2, 0:128] = ((ii[:, None] // 8 == ii[None, :] // 8) & (ii[:, None] <= ii[None, :])).astype(f32)

    ck_full = np.ascontiguousarray(
        np.asarray(cache_k, f32).reshape(npool, 128, 8, 128).transpose(2, 0, 1, 3)).reshape(8 * npool * 4, 4096)
    cv_full = np.ascontiguousarray(
        np.asarray(cache_v, f32).reshape(npool, 128, 8, 128).transpose(2, 0, 1, 3)).reshape(8 * npool * 4, 4096)
    clf_full = np.ascontiguousarray(
        np.asarray(cache_logf, f32).reshape(npool, 2, 64, 8, 2).transpose(3, 0, 1, 4, 2)).reshape(8 * npool * 2, 128)
    pt = np.asarray(page_table).astype(np.int32)
    xs_all = A(x_sample)
    srnn = A(state_rnn); sconv = A(state_conv)

    xp_all = A(x_prompt)
    shared = {"a_w_in": A(a_w_in), "a_w_out": A(a_w_out), "a_w_ga": A(a_w_ga), "a_w_gx": A(a_w_gx), "w_kv": A(w_kv),
              "b_w_in": A(b_w_in), "b_w_out": A(b_w_out), "pvec": pvec, "scst": scst, "rowv": rowv,
              "ck_full": ck_full, "cv_full": cv_full, "clf_full": clf_full}
    in_maps = []
    for c in range(8):
        b, p = c // 2, c % 2
        cst = np.zeros((128, 6, 128), f32)
        cst[:, 0] = ident; cst[:, 1] = tri; cst[:, 2] = ones
        cst[:, 3] = causal if p == 0 else full_ok
        cst[:, 4] = full_no if p == 0 else causal
        idx = np.zeros((128, 32), np.int32)
        for j in range(16):
            idx[:, j] = (2 * j + p) * 128 + ii
            idx[:, 16 + j] = (2 * j + p) * 128 + 63
        xs = np.zeros((256, D), f32); xs[0:32] = xs_all[4 * c:4 * c + 4].reshape(32, D)
        st_rnn = np.zeros((2, NS, D), f32); st_rnn[:, 0:4] = srnn[:, 4 * c:4 * c + 4]
        st_conv = np.zeros((2, NS, 3, D), f32); st_conv[:, 0:4] = sconv[:, 4 * c:4 * c + 4]
        ptT2 = np.zeros((128, NS), np.int32)
        ptT2[:, 0:4] = np.tile(pt[4 * c:4 * c + 4].T, (2, 1))
        m = dict(shared)
        m.update({"xp": xp_all[b], "cst": cst, "idx": idx, "xs": xs, "ptT2": ptT2,
                  "st_rnn": np.ascontiguousarray(st_rnn.reshape(2, NS, 8, 128).transpose(0, 3, 2, 1)),
                  "st_conv": np.ascontiguousarray(st_conv.reshape(2, NS, 3, 8, 128).transpose(0, 4, 3, 1, 2))})
        in_maps.append(m)

    res = run_bass_kernel_spmd(nc, in_maps, core_ids=list(range(8)), trace=_trace)
    R = res.results
    y_prompt = np.zeros((4, T, D), f32)
    k_p = np.zeros((4, T, H, 64), f32); v_p = np.zeros((4, T, H, 64), f32); lf_p = np.zeros((4, T, H), f32)
    rnn_p = np.zeros((2, 4, D), f32); conv_p = np.zeros((2, 4, 3, D), f32)
    y_s = np.zeros((NS, 8, D), f32); k_s = np.zeros((NS, 8, H, 64), f32); v_s = np.zeros((NS, 8, H, 64), f32)
    lf_s = np.zeros((NS, 8, H), f32); rnn_s = np.zeros((2, NS, D), f32); conv_s = np.zeros((2, NS, 3, D), f32)
    for c in range(8):
        b, p = c // 2, c % 2
        r = R[c]
        if _do_prompt:
            y_prompt[b].reshape(16, 2, 128, D)[:, p] = np.asarray(r["y_own"])
            if p == 0:
                k_p[b] = np.asarray(r["k_out"]).reshape(T, H, 64)
                v_p[b] = np.asarray(r["v_out"]).reshape(T, H, 64)
                lf_p[b] = np.asarray(r["lf_out"])
                rnn_p[:, b] = np.asarray(r["rnn_out"]).transpose(0, 2, 1).reshape(2, D)
                conv_p[:, b] = np.asarray(r["conv_out"]).transpose(0, 3, 2, 1).reshape(2, 3, D)
        sl = slice(4 * c, 4 * c + 4)
        y_s[sl] = np.asarray(r["ys_out"], f32)[0:32].reshape(4, 8, D)
        k_s[sl] = np.asarray(r["ks_out"], f32)[0:32].reshape(4, 8, H, 64)
        v_s[sl] = np.asarray(r["vs_out"], f32)[0:32].reshape(4, 8, H, 64)
        lf_s[sl] = np.asarray(r["lfs_out"], f32)[0:32].reshape(4, 8, H)
        rnn_s[:, sl] = np.asarray(r["rnns_out"], f32).transpose(0, 3, 2, 1).reshape(2, NS, D)[:, 0:4]
        conv_s[:, sl] = np.asarray(r["convs_out"], f32).transpose(0, 3, 4, 2, 1).reshape(2, NS, 3, D)[:, 0:4]
    if _trace:
        kernel.last_exec_ns = res.exec_time_ns
    if _debug:
        kernel.last_dbg = np.asarray(R[0]["dbg"])
    return (y_prompt, y_s, k_p, v_p, lf_p, rnn_p, conv_p, k_s, v_s, lf_s, rnn_s, conv_s)
```

```python
import numpy as np
from contextlib import ExitStack
import concourse.bass as bass
import concourse.mybir as mybir
from concourse.bass_utils import run_bass_kernel_spmd

F32, BF16, I32 = mybir.dt.float32, mybir.dt.bfloat16, mybir.dt.int32
ALU = mybir.AluOpType
AF = mybir.ActivationFunctionType

D = 1024
T = 4096
NT = T // 128
CH = 256
NCH = T // CH
H = 16
NPOOL = 2560
NPG = 64
NS = 32
EPS = 1e-6
NEG = -30000.0


class Prog:
    ENG = ("pe", "act", "dve", "pool", "sp")
    NDS = 6

    def __init__(self, nc, stack):
        self.nc = nc
        self.stack = stack
        self.streams = {e: [] for e in self.ENG}
        self.cnt = {e: 0 for e in self.ENG}
        self.sem = {}
        self.known = {e: {} for e in self.ENG}
        self.lastw = {}
        self.readers = {}
        self.dsem = {}
        self.dcnt = {}
        self.nsem = 0
        self.semobj = {}
        for q in ("sp", "pool", "act"):
            self.dsem[q] = [self._newsem() for _ in range(self.NDS)]
            self.dcnt[q] = 0

    def _newsem(self):
        s = self.stack.enter_context(self.nc.semaphore("s%d" % self.nsem))
        self.semobj[self.nsem] = s
        self.nsem += 1
        return self.nsem - 1

    def _deps(self, eng, r, w):
        need = {}
        def add(ev):
            sid, val, src = ev
            if src == "pe" and eng == "pe":
                return
            if need.get(sid, 0) < val:
                need[sid] = val
        for k in r:
            if k in self.lastw:
                add(self.lastw[k])
        for k in w:
            if k in self.lastw:
                add(self.lastw[k])
            for ev in self.readers.get(k, {}).values():
                add(ev)
        waits = []
        kn = self.known[eng]
        for sid, val in need.items():
            if kn.get(sid, 0) < val:
                kn[sid] = val
                waits.append((sid, val))
        return waits

    def _commit(self, ev, r, w):
        for k in r:
            self.readers.setdefault(k, {})[ev[0]] = ev
        for k in w:
            self.lastw[k] = ev
            self.readers[k] = {}

    def op(self, eng, fn, r=(), w=()):
        waits = self._deps(eng, r, w)
        if eng not in self.sem or self.cnt[eng] >= 6000:
            self.sem[eng] = self._newsem()
            self.cnt[eng] = 0
        self.cnt[eng] += 1
        ev = (self.sem[eng], self.cnt[eng], eng)
        self.streams[eng].append((waits, fn, (self.sem[eng], 1)))
        self._commit(ev, r, w)

    def dma(self, q, fn, r=(), w=()):
        waits = self._deps(q, r, w)
        i = self.dcnt[q]
        self.dcnt[q] += 1
        sid = self.dsem[q][i % self.NDS]
        prev = 16 * (i // self.NDS)
        if prev > 0 and self.known[q].get(sid, 0) < prev:
            self.known[q][sid] = prev
            waits.append((sid, prev))
        self.streams[q].append((waits, fn, (sid, 16)))
        self._commit((sid, prev + 16, "dma"), r, w)

    def drain(self, eng="sp"):
        waits = []
        for q in self.dsem:
            n = self.dcnt[q]
            for j, sid in enumerate(self.dsem[q]):
                cntj = (n - j + self.NDS - 1) // self.NDS if n > j else 0
                if cntj > 0 and self.known[eng].get(sid, 0) < 16 * cntj:
                    self.known[eng][sid] = 16 * cntj
                    waits.append((sid, 16 * cntj))
        self.streams[eng].append((waits, None, None))

    def flush(self):
        nc = self.nc
        streams = self.streams
        semobj = self.semobj
        self.streams = {e: [] for e in self.ENG}

        def replay(name):
            def f(e):
                for waits, fn, inc in streams[name]:
                    if fn is None:
                        for sid, val in waits:
                            e.wait_ge(semobj[sid], val)
                        continue
                    for sid, val in waits[:-1]:
                        e.wait_ge(semobj[sid], val)
                    inst = fn(e)
                    if waits:
                        inst._wait_ge(semobj[waits[-1][0]], waits[-1][1])
                    inst.then_inc(semobj[inc[0]], inc[1])
            return f
        with nc.Block() as block:
            if streams["pe"]:
                block.tensor(replay("pe"))
            if streams["act"]:
                block.scalar(replay("act"))
            if streams["dve"]:
                block.vector(replay("dve"))
            if streams["pool"]:
                block.gpsimd(replay("pool"))
            if streams["sp"]:
                block.sync(replay("sp"))


def fm(v):
    return np.ascontiguousarray(np.asarray(v, np.float32).reshape(8, 128).T)


PV = {}
def _pv_layout():
    c = 0
    for l in range(2):
        for nm in ("apre", "cw0", "cw1", "cw2", "cw3", "cb", "bga", "bgx", "lam"):
            PV[(nm, l)] = c; c += 8
    PV[("kvn", 0)] = c; c += 8
    for l in range(2):
        PV[("bpre", l)] = c; c += 8
    return c
NPV = _pv_layout()


def build_program(do_prompt=True, do_sample=True, npool=NPOOL, debug=False):
    nc = bass.Bass("TRN2", target_bir_lowering=False)
    dram = {}

    def din(name, shape, dt=F32):
        dram[name] = nc.dram_tensor(name, list(shape), dt, kind="ExternalInput").ap()
        return dram[name]

    def dout(name, shape, dt=F32):
        dram[name] = nc.dram_tensor(name, list(shape), dt, kind="ExternalOutput").ap()
        return dram[name]

    def dscr(name, shape, dt=F32):
        dram[name] = nc.dram_tensor(name, list(shape), dt, kind="Internal").ap()
        return dram[name]

    xp = din("xp", [T, D])
    a_w_in = din("a_w_in", [2, D, 2 * D]); a_w_out = din("a_w_out", [2, D, D])
    a_w_ga = din("a_w_ga", [2, 8, 128, 128]); a_w_gx = din("a_w_gx", [2, 8, 128, 128])
    w_kv = din("w_kv", [D, 2 * D + H])
    b_w_in = din("b_w_in", [2, D, 2 * D]); b_w_out = din("b_w_out", [2, D, D])
    pvec_d = din("pvec", [128, NPV])
    rowv_d = din("rowv", [5, D])
    cst_d = din("cst", [128, 6, 128])
    idx_d = din("idx", [128, 32], I32)
    xs_d = din("xs", [256, D])
    st_rnn_d = din("st_rnn", [2, 128, 8, NS]); st_conv_d = din("st_conv", [2, 128, 8, NS, 3])
    ck_full = din("ck_full", [8 * npool * 4, 4096])
    cv_full = din("cv_full", [8 * npool * 4, 4096])
    clf_full = din("clf_full", [8 * npool * 2, 128])
    ptT2_d = din("ptT2", [128, NS], I32)
    scst_d = din("scst", [128, 3, 256])
    ys_out = dout("ys_out", [256, D]); ks_out = dout("ks_out", [256, D]); vs_out = dout("vs_out", [256, D])
    lfs_out = dout("lfs_out", [256, H])
    rnns_out = dout("rnns_out", [2, 128, 8, NS]); convs_out = dout("convs_out", [2, 128, 8, NS, 3])
    xs1 = dscr("xs1", [256, D]); xs2 = dscr("xs2", [256, D])
    o_scr = [dscr("o_scr%d" % i, [256, D]) for i in range(2)]
    dbg = dout("dbg", [128, 4096]) if debug else None
    y_own = dout("y_own", [16, 128, D])
    k_out = dout("k_out", [T, D]); v_out = dout("v_out", [T, D]); lf_out = dout("lf_out", [T, H])
    rnn_out = dout("rnn_out", [2, 128, 8]); conv_out = dout("conv_out", [2, 128, 8, 3])
    x1s = dscr("x1s", [T, D]); x2s = dscr("x2s", [T, D])
    kTs = dscr("kTs", [8, 128, T], BF16)
    vs = dscr("vs", [8, 128, NT, 128], BF16)
    cs = dscr("cs", [T, H])

    st = ExitStack()
    P = Prog(nc, st)

    _uid = [0]

    def sb(name, shape, dt=F32, stack=None):
        _uid[0] += 1
        return (stack or st).enter_context(nc.sbuf_tensor("t%d_%s" % (_uid[0], name), list(shape), dt))

    ps = [st.enter_context(nc.psum_tensor("ps%d" % i, [128, 512], F32)) for i in range(8)]

    pvec = sb("pvec", [128, NPV])
    cst = sb("cstt", [128, 6, 128])
    identb = sb("identb", [128, 128], BF16)
    idx = sb("idxt", [128, 32], I32)
    cneg = sb("cneg", [128, 2, 8]); cnegh = sb("cnegh", [128, 2, 8]); cneg2 = sb("cneg2", [128, 2, 8])
    bgah = sb("bgah", [128, 2, 8]); bgxh = sb("bgxh", [128, 2, 8])
    bfB = sb("bfB", [128, H + 2])
    scst = sb("scst", [128, 3, 256])
    kTn = sb("kTn", [128, 8, 256], BF16)
    vn1 = sb("vn1", [128, 8, 129], BF16)
    negE = sb("negE", [128, H])
    ptT2 = sb("ptT2", [128, NS], I32)
    idxK = sb("idxK", [128, 2, 4, 8], I32)
    idxL = sb("idxL", [128, 4, 8], I32)
    offs = sb("offs", [128, 3, 8])
    ck = sb("ck", [128, NT, H])
    lacc = sb("lacc", [128, H])
    stat = sb("stat", [128, 16])

    ident = cst[:, 0, :]; tri = cst[:, 1, :]; ones = cst[:, 2, :]
    mask_a = cst[:, 3, :]; mask_b = cst[:, 4, :]

    def pv(nm, l, n=None):
        c = PV[(nm, l)]
        return pvec[:, c:c + 8] if n is None else pvec[:, c + n:c + n + 1]

    P.dma("sp", lambda e: e.dma_start(out=pvec[:], in_=pvec_d), w=["pvec"])
    P.dma("sp", lambda e: e.dma_start(out=cst[:], in_=cst_d), w=["cst"])
    P.dma("sp", lambda e: e.dma_start(out=idx[:], in_=idx_d), w=["idx"])
    P.dma("sp", lambda e: e.dma_start(out=bfB[:], in_=rowv_d[4:5, 0:H + 2].partition_broadcast(128)), w=["bfB"])
    P.dma("sp", lambda e: e.dma_start(out=scst[:], in_=scst_d), w=["scst"])
    P.dma("sp", lambda e: e.dma_start(out=ptT2[:], in_=ptT2_d), w=["ptT2"])
    P.op("pool", lambda e: e.memset(vn1[:, :, 128:129], 1.0), w=["vn1"])
    for hp in range(8):
        for half in range(2):
            P.op("pool", lambda e, half=half, hp=hp: e.tensor_scalar(
                out=offs[:, half, hp:hp + 1], in0=scst[:, 0, 129 + half:130 + half], scalar1=float(hp * npool * 4), scalar2=0.0,
                op0=ALU.add, op1=ALU.add), r=["scst"], w=["offs"])
        P.op("pool", lambda e, hp=hp: e.tensor_scalar(
            out=offs[:, 2, hp:hp + 1], in0=scst[:, 0, 128:129], scalar1=float(hp * npool * 2), scalar2=0.0,
            op0=ALU.add, op1=ALU.add), r=["scst"], w=["offs"])
    for hp in range(8):
        for half in range(2):
            P.op("pool", lambda e, half=half, hp=hp: e.tensor_scalar(
                out=idxK[:, half, :, hp], in0=ptT2[:, 0:4], scalar1=4.0, scalar2=offs[:, half, hp:hp + 1],
                op0=ALU.mult, op1=ALU.add), r=["ptT2", "offs"], w=["idxK"])
        P.op("pool", lambda e, hp=hp: e.tensor_scalar(
            out=idxL[:, :, hp], in0=ptT2[:, 0:4], scalar1=2.0, scalar2=offs[:, 2, hp:hp + 1],
            op0=ALU.mult, op1=ALU.add), r=["ptT2", "offs"], w=["idxL"])
    P.op("dve", lambda e: e.tensor_copy(out=identb[:], in_=ident), r=["cst"], w=["identb"])
    P.op("dve", lambda e: e.memset(lacc[:], 0.0), w=["lacc"])
    for l in range(2):
        P.op("act", lambda e, l=l: e.activation(out=cneg[:, l, :], in_=pv("lam", l), func=AF.Exp, scale=-1.0),
             r=["pvec"], w=["cneg"])
        P.op("act", lambda e, l=l: e.activation(out=cneg[:, l, :], in_=cneg[:, l, :], func=AF.Ln, bias=1.0),
             r=["cneg"], w=["cneg"])
        P.op("dve", lambda e, l=l: e.tensor_scalar(out=cnegh[:, l, :], in0=cneg[:, l, :], scalar1=-4.0, scalar2=None,
                                                   op0=ALU.mult), r=["cneg"], w=["cnegh"])
        P.op("dve", lambda e, l=l: e.tensor_scalar(out=cneg2[:, l, :], in0=cneg[:, l, :], scalar1=-8.0, scalar2=None,
                                                   op0=ALU.mult), r=["cneg"], w=["cneg2"])
        P.op("dve", lambda e, l=l: e.tensor_scalar(out=bgah[:, l, :], in0=pv("bga", l), scalar1=0.5, scalar2=None,
                                                   op0=ALU.mult), r=["pvec"], w=["bgah"])
        P.op("dve", lambda e, l=l: e.tensor_scalar(out=bgxh[:, l, :], in0=pv("bgx", l), scalar1=0.5, scalar2=None,
                                                   op0=ALU.mult), r=["pvec"], w=["bgxh"])

    def rms_to_featmajor(x_ap, xkey, ntt, gain_l, xh, xnT, tpb, pfx):
        for tt in range(ntt):
            c0 = tt * 2
            P.op("act", lambda e, tt=tt, c0=c0: e.activation(out=xh[:, tt, :], in_=x_ap(tt), func=AF.Square,
                                                             accum_out=stat[:, c0:c0 + 1]),
                 r=[xkey(tt)], w=[pfx + "xh", "stat"])
            P.op("act", lambda e, c0=c0: e.activation(out=stat[:, c0 + 1:c0 + 2], in_=stat[:, c0:c0 + 1], func=AF.Sqrt,
                                                      scale=1.0 / D, bias=epsb[:, 0:1]), r=["stat", "epsb"], w=["stat"])
            P.op("dve", lambda e, c0=c0: e.reciprocal(out=stat[:, c0 + 1:c0 + 2], in_=stat[:, c0 + 1:c0 + 2]),
                 r=["stat"], w=["stat"])
            P.op("pool", lambda e, tt=tt, c0=c0: e.tensor_scalar(out=xh[:, tt, :], in0=x_ap(tt), scalar1=stat[:, c0 + 1:c0 + 2],
                                                                 scalar2=0.0, op0=ALU.mult, op1=ALU.add),
                 r=[xkey(tt), "stat"], w=[pfx + "xh"])
        W = ntt * 128
        per = 1024 // W
        for bi in range(8 // per):
            bank = tpb[bi]
            bv = bank[:].bitcast(BF16)
            for kk in range(per):
                kc = bi * per + kk
                for tt in range(ntt):
                    P.op("pe", lambda e, kc=kc, kk=kk, tt=tt, bv=bv: e.transpose(
                        out=bv[:, kk * W + tt * 128: kk * W + (tt + 1) * 128], in_=xh[:, tt, kc * 128:(kc + 1) * 128],
                        identity=identb[:]), r=[pfx + "xh", "identb"], w=[("ps", id(bank))])
            for kk in range(per):
                kc = bi * per + kk
                if kc % 2 == 0:
                    P.op("dve", lambda e, kc=kc, kk=kk, bv=bv: e.tensor_scalar(
                        out=xnT[:, kc, 0:W], in0=bv[:, kk * W:(kk + 1) * W], scalar1=gain_l(kc), scalar2=None, op0=ALU.mult),
                        r=[("ps", id(bank)), "pvec"], w=[pfx + "xnT"])
                else:
                    P.op("act", lambda e, kc=kc, kk=kk, bv=bv: e.activation(
                        out=xnT[:, kc, 0:W], in_=bv[:, kk * W:(kk + 1) * W], func=AF.Copy, scale=gain_l(kc)),
                        r=[("ps", id(bank)), "pvec"], w=[pfx + "xnT"])

    def post_norm_residual(ybanks, x_tile_ap, xkey, gB, gkey, tmp):
        for hb in range(2):
            P.op("act", lambda e, hb=hb: e.activation(out=tmp[:, hb * 512:(hb + 1) * 512], in_=ybanks[hb][:], func=AF.Square,
                                                      accum_out=stat[:, 8 + hb:9 + hb]),
                 r=[("ps", id(ybanks[hb]))], w=["ytmp", "stat"])
        P.op("dve", lambda e: e.tensor_tensor(out=stat[:, 10:11], in0=stat[:, 8:9], in1=stat[:, 9:10], op=ALU.add),
             r=["stat"], w=["stat"])
        P.op("act", lambda e: e.activation(out=stat[:, 11:12], in_=stat[:, 10:11], func=AF.Sqrt, scale=1.0 / D,
                                           bias=epsb[:, 0:1]), r=["stat", "epsb"], w=["stat"])
        P.op("dve", lambda e: e.reciprocal(out=stat[:, 11:12], in_=stat[:, 11:12]), r=["stat"], w=["stat"])
        for hb in range(2):
            P.op("dve", lambda e, hb=hb: e.scalar_tensor_tensor(
                out=tmp[:, hb * 512:(hb + 1) * 512], in0=ybanks[hb][:], scalar=stat[:, 11:12],
                in1=gB[:, hb * 512:(hb + 1) * 512], op0=ALU.mult, op1=ALU.mult),
                r=[("ps", id(ybanks[hb])), "stat", gkey], w=["ytmp"])
        P.op("pool", lambda e: e.tensor_tensor(out=x_tile_ap, in0=x_tile_ap, in1=tmp[:], op=ALU.add),
             r=["ytmp", xkey], w=[xkey])

    epsb = sb("epsb", [128, 1])
    P.op("dve", lambda e: e.memset(epsb[:], EPS), w=["epsb"])

    def a_pass(l):
        with ExitStack() as ph:
            w_in = sb("aw_in", [128, 8, 2 * D], BF16, ph)
            w_out = sb("aw_out", [128, 8, D], BF16, ph)
            wga = sb("awga", [128, 8, 128], BF16, ph)
            wgx = sb("awgx", [128, 8, 128], BF16, ph)
            gB = sb("agB", [128, D], F32, ph)
            xt = sb("axt", [128, 2, 2, D], F32, ph)
            xh = sb("axh", [128, 2, D], BF16, ph)
            xnT = sb("axnT", [128, 8, CH], BF16, ph)
            ubuf = sb("aubuf", [128, 8, 352], F32, ph)
            hstS = sb("ahstS", [128, 8, NS], F32, ph)
            hlast = sb("ahlast", [128, 8, NS], F32, ph)
            ubv = lambda oc: ubuf[:, oc, 0:352].rearrange("p (s j) -> p s j", j=11)
            v3 = lambda ap: ap.rearrange("p (s t) -> p s t", t=8)
            uc = sb("auc", [128, 8, CH], F32, ph)
            ucb = sb("aucb", [128, 8, CH], BF16, ph)
            rbuf = sb("arbuf", [128, 8, CH], F32, ph)
            ibuf = sb("aibuf", [128, 8, CH], F32, ph)
            sbuf_ = sb("asbuf", [128, 8, CH], F32, ph)
            thg = sb("athg", [128, 8, CH], BF16, ph)
            hbuf = sbuf_
            mT = sb("amT", [128, 8, CH], BF16, ph)
            hst = sb("ahst", [128, 8], F32, ph)
            ytmp = sb("aytmp", [128, D], F32, ph)
            if l == 1:
                wkv = sb("awkv", [128, 8, 2 * D + H], BF16, ph)
                kvtok = sb("akvtok", [128, 2, D], F32, ph)
                vbf = sb("avbf", [128, 2, D], BF16, ph)
                kTc = sb("akTc", [128, 8, CH], BF16, ph)
                lft = sb("alft", [128, H], F32, ph)
                lfe = sb("alfe", [128, H], F32, ph)

            P.dma("pool", lambda e: e.dma_start(out=w_in[:], in_=a_w_in[l].rearrange("(kc p) n -> p kc n", p=128)), w=["aw_in"])
            P.dma("pool", lambda e: e.dma_start(out=w_out[:], in_=a_w_out[l].rearrange("(kc p) n -> p kc n", p=128)), w=["aw_out"])
            P.dma("pool", lambda e: e.dma_start(out=wga[:], in_=a_w_ga[l].rearrange("n c d -> c n d")), w=["awga"])
            P.dma("pool", lambda e: e.dma_start(out=wgx[:], in_=a_w_gx[l].rearrange("n c d -> c n d")), w=["awgx"])
            P.dma("sp", lambda e: e.dma_start(out=gB[:], in_=rowv_d[l:l + 1, :].partition_broadcast(128)), w=["agB"])
            if l == 1:
                P.dma("pool", lambda e: e.dma_start(out=wkv[:], in_=w_kv.rearrange("(kc p) n -> p kc n", p=128)), w=["awkv"])
            P.op("dve", lambda e: e.memset(ubuf[:, :, 0:3], 0.0), w=[("ubuf", o_) for o_ in range(8)])
            P.op("dve", lambda e: e.memset(hst[:], 0.0), w=["hst"])

            src = xp if l == 0 else x1s
            dst = x1s if l == 0 else x2s
            tpb = [ps[0], ps[1]]
            chunks = (list(range(NCH)) if do_prompt else []) + (["S"] if do_sample else [])
            def front(ch):
                sample = (ch == "S")
                slot = 0 if sample else ch % 2
                xkey = lambda tt, slot=slot: ("axt", slot)
                x_ap = lambda tt, slot=slot: xt[:, slot, tt, :]
                if sample:
                    if do_prompt:
                        P.dma("pool", lambda e: e.dma_start(out=rnn_out[l], in_=hst[:]), r=["hst"], w=[("rnn_out", l)])
                        P.dma("pool", lambda e: e.dma_start(out=conv_out[l], in_=ubuf[:, :, 0:3]), r=[("ubuf", o_) for o_ in range(8)], w=[("conv_out", l)])
                    ssrc = xs_d if l == 0 else xs1
                    P.dma("sp", lambda e, slot=slot: e.dma_start(
                        out=xt[:, slot, :, :], in_=ssrc.rearrange("(t p) f -> p t f", p=128)), w=[("axt", slot)])
                    for oc in range(8):
                        P.dma("sp", lambda e, oc=oc: e.dma_start(out=ubv(oc)[:, :, 0:3], in_=st_conv_d[l, :, oc]),
                              r=[], w=[("ubuf", oc)])
                    P.dma("sp", lambda e: e.dma_start(out=hstS[:], in_=st_rnn_d[l]), w=["hstS"])
                else:
                    P.dma("sp", lambda e, ch=ch, slot=slot: e.dma_start(
                        out=xt[:, slot, :, :], in_=src[ch * CH:(ch + 1) * CH, :].rearrange("(t p) f -> p t f", p=128)),
                        w=[("axt", slot)])
                rms_to_featmajor(x_ap, xkey, 2, lambda kc: pv("apre", l, kc), xh, xnT, tpb, "a")
                for oc in range(16):
                    bank = ps[2 + oc % 2]
                    for kc in range(8):
                        P.op("pe", lambda e, oc=oc, kc=kc, bank=bank: e.matmul(
                            out=bank[:, 0:CH], lhsT=w_in[:, kc, oc * 128:(oc + 1) * 128], rhs=xnT[:, kc, :],
                            start=(kc == 0), stop=(kc == 7)), r=["aw_in", "axnT"], w=[("ps", id(bank))])
                    if oc < 8:
                        if sample:
                            P.op("act", lambda e, oc=oc, bank=bank: e.activation(out=ubv(oc)[:, :, 3:11], in_=v3(bank[:, 0:CH]),
                                                                                 func=AF.Copy),
                                 r=[("ps", id(bank))], w=[("ubuf", oc)])
                        else:
                            P.op("act", lambda e, oc=oc, bank=bank: e.activation(out=ubuf[:, oc, 3:3 + CH], in_=bank[:, 0:CH],
                                                                                 func=AF.Copy),
                                 r=[("ps", id(bank))], w=[("ubuf", oc)])
                    else:
                        g = oc - 8
                        P.op("act", lambda e, g=g, bank=bank: e.activation(out=thg[:, g, :], in_=bank[:, 0:CH], func=AF.Tanh,
                                                                           scale=0.5),
                             r=[("ps", id(bank))], w=[("thg", g)])
                        P.op("dve", lambda e, g=g, bank=bank: e.scalar_tensor_tensor(
                            out=thg[:, g, :], in0=thg[:, g, :], scalar=1.0, in1=bank[:, 0:CH], op0=ALU.add, op1=ALU.mult),
                            r=[("ps", id(bank)), ("thg", g)], w=[("thg", g)])

            def mid(ch):
                sample = (ch == "S")
                slot = 0 if sample else ch % 2
                xkey = lambda tt, slot=slot: ("axt", slot)
                x_ap = lambda tt, slot=slot: xt[:, slot, tt, :]
                for oc in range(8):
                    if sample:
                        uin = lambda oc, j: ubv(oc)[:, :, j:j + 8]
                        uco = lambda oc: v3(uc[:, oc, :])
                    else:
                        uin = lambda oc, j: ubuf[:, oc, j:j + CH]
                        uco = lambda oc: uc[:, oc, :]
                    P.op("pool", lambda e, oc=oc, uin=uin, uco=uco: e.tensor_scalar(
                        out=uco(oc), in0=uin(oc, 3), scalar1=pv("cw3", l, oc), scalar2=pv("cb", l, oc),
                        op0=ALU.mult, op1=ALU.add), r=[("ubuf", oc), "pvec"], w=[("uc", oc)])
                    for j in range(3):
                        P.op("dve", lambda e, oc=oc, j=j, uin=uin, uco=uco: e.scalar_tensor_tensor(
                            out=uco(oc), in0=uin(oc, j), scalar=pv("cw%d" % j, l, oc), in1=uco(oc),
                            op0=ALU.mult, op1=ALU.add), r=[("ubuf", oc), ("uc", oc), "pvec"], w=[("uc", oc)])
                for oc in range(8):
                    P.op("act", lambda e, oc=oc: e.activation(out=ucb[:, oc, :], in_=uc[:, oc, :], func=AF.Copy),
                         r=[("uc", oc)], w=[("ucb", oc)])
                if sample:
                    for oc in range(8):
                        P.dma("pool", lambda e, oc=oc: e.dma_start(out=convs_out[l, :, oc], in_=ubv(oc)[:, :, 8:11]),
                              r=[("ubuf", oc)], w=[("convs_out", l, oc)])
                else:
                    P.op("pool", lambda e: e.tensor_copy(out=ubuf[:, :, 0:3], in_=ubuf[:, :, CH:CH + 3]), r=[("ubuf", o_) for o_ in range(8)], w=[("ubuf", o_) for o_ in range(8)])
                for oc in range(8):
                    bank = ps[4 + oc % 2]
                    P.op("pe", lambda e, oc=oc, bank=bank: e.matmul(out=bank[:, 0:CH], lhsT=wga[:, oc, :], rhs=ucb[:, oc, :],
                                                                    start=True, stop=True),
                         r=["awga", ("ucb", oc)], w=[("ps", id(bank))])
                    P.op("pe", lambda e, oc=oc, bank=bank: e.matmul(out=bank[:, CH:2 * CH], lhsT=wgx[:, oc, :], rhs=ucb[:, oc, :],
                                                                    start=True, stop=True),
                         r=["awgx", ("ucb", oc)], w=[("ps", id(bank))])
                    P.op("act", lambda e, oc=oc, bank=bank: e.activation(out=rbuf[:, oc, :], in_=bank[:, 0:CH], func=AF.Tanh,
                                                                         scale=0.5, bias=bgah[:, l, oc:oc + 1]),
                         r=[("ps", id(bank)), "bgah"], w=[("rbuf", oc)])
                    P.op("act", lambda e, oc=oc, bank=bank: e.activation(out=ibuf[:, oc, :], in_=bank[:, CH:2 * CH], func=AF.Tanh,
                                                                         scale=0.5, bias=bgxh[:, l, oc:oc + 1]),
                         r=[("ps", id(bank)), "bgxh"], w=[("ibuf", oc)])
                for oc in range(8):
                    P.op("act", lambda e, oc=oc: e.activation(out=sbuf_[:, oc, :], in_=rbuf[:, oc, :], func=AF.Exp,
                                                              scale=cneg2[:, l, oc:oc + 1], bias=cneg2[:, l, oc:oc + 1]),
                         r=[("rbuf", oc), "cneg2"], w=[("sbuf", oc)])
                    P.op("act", lambda e, oc=oc: e.activation(out=rbuf[:, oc, :], in_=rbuf[:, oc, :], func=AF.Exp,
                                                              scale=cnegh[:, l, oc:oc + 1], bias=cnegh[:, l, oc:oc + 1]),
                         r=[("rbuf", oc), "cnegh"], w=[("rbuf", oc)])
                for oc in range(8):
                    P.op("act", lambda e, oc=oc: e.activation(out=sbuf_[:, oc, :], in_=sbuf_[:, oc, :], func=AF.Sqrt, scale=-1.0,
                                                              bias=oneb[:, 0:1]),
                         r=[("sbuf", oc), "oneb"], w=[("sbuf", oc)])
                for oc in range(8):
                    P.op("dve", lambda e, oc=oc: e.scalar_tensor_tensor(
                        out=ibuf[:, oc, :], in0=ibuf[:, oc, :], scalar=1.0, in1=uc[:, oc, :], op0=ALU.add, op1=ALU.mult),
                        r=[("ibuf", oc), ("uc", oc)], w=[("ibuf", oc)])
                    P.op("dve", lambda e, oc=oc: e.scalar_tensor_tensor(
                        out=ibuf[:, oc, :], in0=ibuf[:, oc, :], scalar=0.5, in1=sbuf_[:, oc, :], op0=ALU.mult, op1=ALU.mult),
                        r=[("ibuf", oc), ("sbuf", oc)], w=[("ibuf", oc)])
                    if sample:
                        for sq in range(NS):
                            P.op("dve", lambda e, oc=oc, sq=sq: e.tensor_tensor_scan(
                                out=hbuf[:, oc, sq * 8:(sq + 1) * 8], data0=rbuf[:, oc, sq * 8:(sq + 1) * 8],
                                data1=ibuf[:, oc, sq * 8:(sq + 1) * 8], initial=hstS[:, oc, sq:sq + 1],
                                op0=ALU.mult, op1=ALU.add), r=[("rbuf", oc), ("ibuf", oc), "hstS"], w=[("sbuf", oc)])
                    else:
                        P.op("dve", lambda e, oc=oc: e.tensor_tensor_scan(
                            out=hbuf[:, oc, :], data0=rbuf[:, oc, :], data1=ibuf[:, oc, :], initial=hst[:, oc:oc + 1],
                            op0=ALU.mult, op1=ALU.add), r=[("rbuf", oc), ("ibuf", oc), "hst"], w=[("sbuf", oc)])
                    P.op("dve", lambda e, oc=oc: e.scalar_tensor_tensor(
                        out=mT[:, oc, :], in0=hbuf[:, oc, :], scalar=0.5, in1=thg[:, oc, :], op0=ALU.mult, op1=ALU.mult),
                        r=[("sbuf", oc), ("thg", oc)], w=[("amT", oc)])
                if sample:
                    P.op("pool", lambda e: e.tensor_copy(
                        out=hlast[:], in_=hbuf[:].rearrange("p o (s t) -> p o s t", t=8)[:, :, :, 7]), r=[("sbuf", o_) for o_ in range(8)], w=["hlast"])
                    P.dma("pool", lambda e: e.dma_start(out=rnns_out[l], in_=hlast[:]), r=["hlast"], w=[("rnns_out", l)])
                else:
                    P.op("pool", lambda e: e.tensor_copy(out=hst[:], in_=hbuf[:, :, CH - 1]), r=[("sbuf", o_) for o_ in range(8)], w=["hst"])

            def tail(ch):
                sample = (ch == "S")
                slot = 0 if sample else ch % 2
                xkey = lambda tt, slot=slot: ("axt", slot)
                x_ap = lambda tt, slot=slot: xt[:, slot, tt, :]
                for tt in range(2):
                    yb = [ps[6], ps[7]]
                    for fc in range(2):
                        for kc in range(8):
                            P.op("pe", lambda e, tt=tt, fc=fc, kc=kc: e.matmul(
                                out=yb[fc][:], lhsT=mT[:, kc, tt * 128:(tt + 1) * 128], rhs=w_out[:, kc, fc * 512:(fc + 1) * 512],
                                start=(kc == 0), stop=(kc == 7)), r=[("amT", kc), "aw_out"], w=[("ps", id(yb[fc]))])
                    post_norm_residual(yb, xt[:, slot, tt, :], ("axt", slot), gB, "agB", ytmp)
                if sample:
                    sdst = xs1 if l == 0 else xs2
                    P.dma("pool", lambda e, slot=slot: e.dma_start(
                        out=sdst.rearrange("(t p) f -> p t f", p=128), in_=xt[:, slot, :, :]),
                        r=[("axt", slot)], w=[("sdst", l)])
                else:
                    P.dma("pool", lambda e, ch=ch, slot=slot: e.dma_start(
                        out=dst[ch * CH:(ch + 1) * CH, :].rearrange("(t p) f -> p t f", p=128), in_=xt[:, slot, :, :]),
                        r=[("axt", slot)], w=[("dst", ch)])
                if l == 1:
                    rms_to_featmajor(x_ap, xkey, 2, lambda kc: pv("kvn", 0, kc), xh, xnT, tpb, "a")
                    for oc in range(8):
                        bank = ps[2 + oc % 2]
                        for kc in range(8):
                            P.op("pe", lambda e, oc=oc, kc=kc, bank=bank: e.matmul(
                                out=bank[:, 0:CH], lhsT=wkv[:, kc, oc * 128:(oc + 1) * 128], rhs=xnT[:, kc, :],
                                start=(kc == 0), stop=(kc == 7)), r=["awkv", "axnT"], w=[("ps", id(bank))])
                        P.op("act", lambda e, oc=oc, bank=bank: e.activation(out=kTc[:, oc, :], in_=bank[:, 0:CH], func=AF.Copy),
                             r=[("ps", id(bank))], w=["kTc"])
                    if sample:
                        P.op("pool", lambda e: e.tensor_copy(out=kTn[:], in_=kTc[:]), r=["kTc"], w=["kTn"])
                    else:
                        P.dma("pool", lambda e, ch=ch: e.dma_start(out=kTs[:, :, ch * CH:(ch + 1) * CH].rearrange("h p t -> p h t"),
                                                                   in_=kTc[:]), r=["kTc"], w=[("kTs", ch)])
                    ko, vo, lo = (ks_out, vs_out, lfs_out) if sample else (k_out, v_out, lf_out)
                    for tt in range(2):
                        tg = tt if sample else ch * 2 + tt
                        for part in range(2):
                            for fc in range(2):
                                bank = ps[4 + fc]
                                c0 = part * D + fc * 512
                                for kc in range(8):
                                    P.op("pe", lambda e, tt=tt, kc=kc, bank=bank, c0=c0: e.matmul(
                                        out=bank[:], lhsT=xnT[:, kc, tt * 128:(tt + 1) * 128], rhs=wkv[:, kc, c0:c0 + 512],
                                        start=(kc == 0), stop=(kc == 7)), r=["awkv", "axnT"], w=[("ps", id(bank))])
                                if fc == 0:
                                    P.op("act", lambda e, part=part, fc=fc, bank=bank: e.activation(
                                        out=kvtok[:, part, fc * 512:(fc + 1) * 512], in_=bank[:], func=AF.Copy),
                                        r=[("ps", id(bank))], w=["kvtok"])
                                else:
                                    P.op("dve", lambda e, part=part, fc=fc, bank=bank: e.tensor_copy(
                                        out=kvtok[:, part, fc * 512:(fc + 1) * 512], in_=bank[:]),
                                        r=[("ps", id(bank))], w=["kvtok"])
                                if part == 1:
                                    P.op("pool", lambda e, fc=fc, tt=tt: e.tensor_copy(
                                        out=vbf[:, tt, fc * 512:(fc + 1) * 512], in_=kvtok[:, 1, fc * 512:(fc + 1) * 512]),
                                        r=["kvtok"], w=["vbf"])
                        P.dma("pool", lambda e, tg=tg, ko=ko: e.dma_start(out=ko[tg * 128:(tg + 1) * 128, :], in_=kvtok[:, 0, :]),
                              r=["kvtok"], w=[("k_out", sample, tg)])
                        P.dma("pool", lambda e, tg=tg, vo=vo: e.dma_start(out=vo[tg * 128:(tg + 1) * 128, :], in_=kvtok[:, 1, :]),
                              r=["kvtok"], w=[("v_out", sample, tg)])
                        bank = ps[6]
                        for kc in range(8):
                            P.op("pe", lambda e, tt=tt, kc=kc, bank=bank: e.matmul(
                                out=bank[:, 0:H], lhsT=xnT[:, kc, tt * 128:(tt + 1) * 128], rhs=wkv[:, kc, 2 * D:2 * D + H],
                                start=(kc == 0), stop=(kc == 7)), r=["awkv", "axnT"], w=[("ps", id(bank))])
                        P.op("dve", lambda e, bank=bank: e.tensor_tensor(out=lfe[:], in0=bank[:, 0:H], in1=bfB[:, 0:H], op=ALU.add),
                             r=[("ps", id(bank)), "bfB"], w=["lfe"])
                        P.op("act", lambda e: e.activation(out=lfe[:], in_=lfe[:], func=AF.Exp, scale=-1.0), r=["lfe"], w=["lfe"])
                        P.op("act", lambda e: e.activation(out=lfe[:], in_=lfe[:], func=AF.Ln, bias=1.0), r=["lfe"], w=["lfe"])
                        P.op("dve", lambda e: e.tensor_scalar(out=lft[:], in0=lfe[:], scalar1=-1.0, scalar2=None, op0=ALU.mult),
                             r=["lfe"], w=["lft"])
                        P.dma("pool", lambda e, tg=tg, lo=lo: e.dma_start(out=lo[tg * 128:(tg + 1) * 128, :], in_=lft[:]),
                              r=["lft"], w=[("lf_out", sample, tg)])
                        if sample:
                            if tt == 0:
                                P.op("pool", lambda e: e.tensor_copy(
                                    out=vn1[:, :, 0:128], in_=vbf[:, 0, :].rearrange("p (h d) -> p h d", h=8)),
                                    r=["vbf"], w=["vn1"])
                                bank2 = ps[7]
                                P.op("pe", lambda e, bank2=bank2: e.matmul(out=bank2[:, 0:H], lhsT=scst[:, 2, 0:128], rhs=lft[:],
                                                                           start=True, stop=True),
                                     r=["scst", "lft"], w=[("ps", id(bank2))])
                                P.op("dve", lambda e, bank2=bank2: e.tensor_scalar(out=negE[:], in0=bank2[:, 0:H], scalar1=-1.0,
                                                                                  scalar2=None, op0=ALU.mult),
                                     r=[("ps", id(bank2))], w=["negE"])
                            continue
                        bank = ps[7]
                        P.op("pe", lambda e, bank=bank: e.matmul(out=bank[:, 0:H], lhsT=tri, rhs=lft[:], start=True, stop=False),
                             r=["cst", "lft"], w=[("ps", id(bank))])
                        P.op("pe", lambda e, bank=bank: e.matmul(out=bank[:, 0:H], lhsT=ones, rhs=lacc[:], start=False, stop=True),
                             r=["cst", "lacc"], w=[("ps", id(bank))])
                        P.op("dve", lambda e, tg=tg, bank=bank: e.tensor_copy(out=ck[:, tg, :], in_=bank[:, 0:H]),
                             r=[("ps", id(bank))], w=["ck"])
                        P.op("dve", lambda e: e.tensor_tensor(out=lacc[:], in0=lacc[:], in1=lft[:], op=ALU.add),
                             r=["lacc", "lft"], w=["lacc"])
                    for tt in range(2 if not sample else 0):
                        P.dma("pool", lambda e, ch=ch, tt=tt: e.dma_start(
                            out=vs[:, :, ch * 2 + tt, :].rearrange("h p d -> p h d"),
                            in_=vbf[:, tt, :].rearrange("p (h d) -> p h d", h=8)), r=["vbf"], w=[("vs", ch, tt)])

            for ci, ch in enumerate(chunks):
                if ci == 0:
                    front(ch)
                mid(ch)
                if ci + 1 < len(chunks):
                    front(chunks[ci + 1])
                tail(ch)
            if do_prompt and not do_sample:
                P.dma("pool", lambda e: e.dma_start(out=rnn_out[l], in_=hst[:]), r=["hst"], w=[("rnn_out", l)])
                P.dma("pool", lambda e: e.dma_start(out=conv_out[l], in_=ubuf[:, :, 0:3]), r=[("ubuf", o_) for o_ in range(8)], w=[("conv_out", l)])
            if l == 1 and do_prompt:
                P.dma("pool", lambda e: e.dma_start(out=cs.rearrange("(t p) h -> p t h", p=128), in_=ck[:]), r=["ck"], w=["cs"])
            P.drain("sp")
            P.flush()

    oneb = sb("oneb", [128, 1])
    P.op("dve", lambda e: e.memset(oneb[:], 1.0), w=["oneb"])

    a_pass(0)
    a_pass(1)

    def b_phase():
        with ExitStack() as ph:
            xo = sb("bxo", [128, 16, D], F32, ph)
            RB = sb("bRB", [128, 16, H], F32, ph)
            QT = sb("bQT", [128, 8, 2048], BF16, ph)
            sgT = sb("bsgT", [128, 8, 2048], BF16, ph)
            gB = sb("bgB", [128, D], F32, ph)
            ytmp = sb("bytmp", [128, D], F32, ph)
            for j in range(16):
                P.dma("pool", lambda e, j=j: e.indirect_dma_start(
                    out=xo[:, j, :], out_offset=None, in_=x2s,
                    in_offset=bass.IndirectOffsetOnAxis(ap=idx[:, j:j + 1], axis=0)), r=["idx"], w=[("bxo", j)])
                P.dma("pool", lambda e, j=j: e.indirect_dma_start(
                    out=RB[:, j, :], out_offset=None, in_=cs,
                    in_offset=bass.IndirectOffsetOnAxis(ap=idx[:, 16 + j:17 + j], axis=0)), r=["idx"], w=["bRB"])
            for l in range(2):
                with ExitStack() as p1:
                    w_in = sb("bw_in", [128, 8, 2 * D], BF16, p1)
                    xh = sb("bxh", [128, 4, D], BF16, p1)
                    xnT = sb("bxnT", [128, 8, 512], BF16, p1)
                    thg = sb("bthg", [128, 2, 512], F32, p1)
                    P.dma("pool", lambda e: e.dma_start(out=w_in[:], in_=b_w_in[l].rearrange("(kc p) n -> p kc n", p=128)),
                          w=["bw_in"])
                    P.dma("sp", lambda e: e.dma_start(out=gB[:], in_=rowv_d[2 + l:3 + l, :].partition_broadcast(128)), w=["bgB"])
                    for grp in range(4):
                        x_ap = lambda tt, grp=grp: xo[:, grp * 4 + tt, :]
                        xkey = lambda tt, grp=grp: ("bxo", grp * 4 + tt)
                        rms_to_featmajor(x_ap, xkey, 4, lambda kc: pv("bpre", l, kc), xh, xnT, [ps[0], ps[1], ps[2], ps[3]], "b")
                        for oc in range(16):
                            bank = ps[4 + oc % 2]
                            for kc in range(8):
                                P.op("pe", lambda e, oc=oc, kc=kc, bank=bank: e.matmul(
                                    out=bank[:], lhsT=w_in[:, kc, oc * 128:(oc + 1) * 128], rhs=xnT[:, kc, :],
                                    start=(kc == 0), stop=(kc == 7)), r=["bw_in", "bxnT"], w=[("ps", id(bank))])
                            if oc < 8:
                                P.op("act", lambda e, oc=oc, bank=bank, grp=grp: e.activation(
                                    out=QT[:, oc, grp * 512:(grp + 1) * 512], in_=bank[:], func=AF.Copy),
                                    r=[("ps", id(bank))], w=["bQT"])
                            else:
                                g = oc - 8
                                ts_ = g % 2
                                P.op("act", lambda e, bank=bank, ts_=ts_: e.activation(out=thg[:, ts_, :], in_=bank[:], func=AF.Tanh,
                                                                                       scale=0.5),
                                     r=[("ps", id(bank))], w=[("bthg", ts_)])
                                P.op("dve", lambda e, bank=bank, ts_=ts_: e.scalar_tensor_tensor(
                                    out=thg[:, ts_, :], in0=thg[:, ts_, :], scalar=1.0, in1=bank[:], op0=ALU.add, op1=ALU.mult),
                                    r=[("ps", id(bank)), ("bthg", ts_)], w=[("bthg", ts_)])
                                P.op("pool", lambda e, g=g, grp=grp, ts_=ts_: e.tensor_scalar(
                                    out=sgT[:, g, grp * 512:(grp + 1) * 512], in0=thg[:, ts_, :], scalar1=0.5, scalar2=0.0,
                                    op0=ALU.mult, op1=ALU.add), r=[("bthg", ts_)], w=["bsgT"])
                    P.drain("sp")
                    P.flush()
                with ExitStack() as p2:
                    kT = sb("bkT", [128, 2, T], BF16, p2)
                    vv = sb("bvv", [128, 2, NT, 2, 65], BF16, p2)
                    bias = sb("bbias", [128, 2, NT, H], F32, p2)
                    sm = sb("bsm", [128, 2, 128], F32, p2)
                    pT = sb("bpT", [128, 4, 128], BF16, p2)
                    on = sb("bon", [128, 2, 128], BF16, p2)
                    rden = sb("brden", [128, 2, 2], F32, p2)
                    P.op("pool", lambda e: e.memset(vv[:, :, :, :, 64:65], 1.0), w=[("bvv", 0), ("bvv", 1)])
                    for hp in range(8):
                        sl = hp % 2
                        P.dma("sp", lambda e, hp=hp, sl=sl: e.dma_start(out=kT[:, sl, :], in_=kTs[hp]),
                              w=[("bkT", sl)])
                        P.dma("sp", lambda e, hp=hp, sl=sl: e.dma_start(
                            out=vv[:, sl, :, :, 0:64], in_=vs[hp].rearrange("p t (e d) -> p t e d", e=2)),
                            w=[("bvv", sl)])
                        for j in range(16):
                            nk = 2 * j + 2
                            bs = j % 2
                            P.op("dve", lambda e, j=j, nk=nk, bs=bs: e.tensor_tensor(
                                out=bias[:, bs, 0:nk, :], in0=RB[:, j:j + 1, :].to_broadcast([128, nk, H]), in1=ck[:, 0:nk, :],
                                op=ALU.subtract), r=["bRB", "ck"], w=[("bbias", bs)])
                            obs = [ps[6], ps[7]]
                            items = [(ee, kt) for kt in range(nk) for ee in range(2)]

                            def emit_s(n, j=j, hp=hp, sl=sl, nk=nk, bs=bs):
                                ee, kt = items[n]
                                h = hp * 2 + ee
                                sbk = ps[n % 4]
                                pslot = n % 4
                                P.op("pe", lambda e: e.matmul(
                                    out=sbk[:, 0:128], lhsT=kT[ee * 64:(ee + 1) * 64, sl, kt * 128:(kt + 1) * 128],
                                    rhs=QT[ee * 64:(ee + 1) * 64, hp, j * 128:(j + 1) * 128], start=True, stop=True),
                                    r=[("bkT", sl), "bQT"], w=[("ps", id(sbk))])
                                if kt >= nk - 2:
                                    mk = mask_a if kt == nk - 2 else mask_b
                                    ms = kt - (nk - 2)
                                    P.op("dve", lambda e: e.scalar_tensor_tensor(
                                        out=sm[:, ms, :], in0=sbk[:, 0:128], scalar=0.125, in1=mk, op0=ALU.mult, op1=ALU.add),
                                        r=[("ps", id(sbk)), "cst"], w=[("bsm", ms)])
                                    P.op("act", lambda e: e.activation(
                                        out=pT[:, pslot, :], in_=sm[:, ms, :], func=AF.Exp, bias=bias[:, bs, kt, h:h + 1]),
                                        r=[("bsm", ms), ("bbias", bs)], w=[("bpT", pslot)])
                                else:
                                    P.op("act", lambda e: e.activation(
                                        out=pT[:, pslot, :], in_=sbk[:, 0:128], func=AF.Exp, scale=0.125,
                                        bias=bias[:, bs, kt, h:h + 1]),
                                        r=[("ps", id(sbk)), ("bbias", bs)], w=[("bpT", pslot)])

                            def emit_pv(n, sl=sl, nk=nk, obs=obs):
                                ee, kt = items[n]
                                pslot = n % 4
                                ob = obs[ee]
                                P.op("pe", lambda e: e.matmul(
                                    out=ob[:, 0:65], lhsT=pT[:, pslot, :], rhs=vv[:, sl, kt, ee, :],
                                    start=(kt == 0), stop=(kt == nk - 1)),
                                    r=[("bpT", pslot), ("bvv", sl)], w=[("ps", id(ob))])
                            LAG = 2
                            for n in range(len(items) + LAG):
                                if n < len(items):
                                    emit_s(n)
                                if n >= LAG:
                                    emit_pv(n - LAG)
                            osl = j % 2
                            for ee in range(2):
                                P.op("dve", lambda e, ob=obs[ee], osl=osl, ee=ee: e.reciprocal(
                                    out=rden[:, osl, ee:ee + 1], in_=ob[:, 64:65]),
                                    r=[("ps", id(obs[ee]))], w=[("brden", osl, ee)])
                                P.op("dve", lambda e, ob=obs[ee], osl=osl, ee=ee: e.tensor_scalar(
                                    out=on[:, osl, ee * 64:(ee + 1) * 64], in0=ob[:, 0:64],
                                    scalar1=rden[:, osl, ee:ee + 1], scalar2=None, op0=ALU.mult),
                                    r=[("ps", id(obs[ee])), ("brden", osl, ee)], w=[("bon", osl)])
                            tb = ps[4 + j % 2]
                            tbv = tb[:].bitcast(BF16)
                            P.op("pe", lambda e, osl=osl, tbv=tbv: e.transpose(out=tbv[:, 0:128], in_=on[:, osl, :], identity=identb[:]),
                                 r=[("bon", osl), "identb"], w=[("ps", id(tb))])
                            P.op("dve", lambda e, tbv=tbv, hp=hp, j=j: e.tensor_tensor(
                                out=sgT[:, hp, j * 128:(j + 1) * 128], in0=tbv[:, 0:128], in1=sgT[:, hp, j * 128:(j + 1) * 128],
                                op=ALU.mult), r=[("ps", id(tb)), "bsgT"], w=["bsgT"])
                    P.drain("sp")
                    P.flush()
                with ExitStack() as p3:
                    w_out = sb("bw_out", [128, 8, D], BF16, p3)
                    P.dma("pool", lambda e: e.dma_start(out=w_out[:], in_=b_w_out[l].rearrange("(kc p) n -> p kc n", p=128)),
                          w=["bw_out"])
                    for j in range(16):
                        yb = [ps[6], ps[7]]
                        for fc in range(2):
                            for kc in range(8):
                                P.op("pe", lambda e, j=j, fc=fc, kc=kc, yb=yb: e.matmul(
                                    out=yb[fc][:], lhsT=sgT[:, kc, j * 128:(j + 1) * 128], rhs=w_out[:, kc, fc * 512:(fc + 1) * 512],
                                    start=(kc == 0), stop=(kc == 7)), r=["bsgT", "bw_out"], w=[("ps", id(yb[fc]))])
                        post_norm_residual(yb, xo[:, j, :], ("bxo", j), gB, "bgB", ytmp)
                        if l == 1:
                            P.dma("sp", lambda e, j=j: e.dma_start(out=y_own[j], in_=xo[:, j, :]), r=[("bxo", j)],
                                  w=[("y_own", j)])
                    P.drain("sp")
                    P.flush()

    def s_phase():
        RG = [list(range(8))]
        with ExitStack() as ph:
            xs_t = sb("sxs", [128, 2, D], F32, ph)
            Dall = sb("sDall", [128, NS, 64, 2], F32, ph)
            P.op("dve", lambda e: e.memset(ytmp[:], 0.0), w=["ytmp"])
            for i_ in range(2):
                for t_ in range(2):
                    P.dma("sp", lambda e, i_=i_, t_=t_: e.dma_start(out=o_scr[i_][t_ * 128:(t_ + 1) * 128, :], in_=ytmp[:]),
                          r=["ytmp"], w=[("o_scr", i_)])
            gB = sb("sgB", [128, D], F32, ph)
            ytmp = sb("sytmp", [128, D], F32, ph)
            onesf = sb("sonesf", [128, 64], F32, ph)
            P.dma("sp", lambda e: e.dma_start(out=xs_t[:], in_=xs2.rearrange("(t p) f -> p t f", p=128)), w=["sxs"])
            P.op("dve", lambda e: e.memset(onesf[:], 1.0), w=["sonesf"])
            with ExitStack() as p0:
                Lg = sb("sLg", [128, 2, 2, 64], F32, p0)
                pre = sb("spre", [128, 2, 2, 64], F32, p0)
                LT = sb("sLT", [128, 2, 2], F32, p0)
                TT = sb("sTT", [128, 2, 2], F32, p0)
                zc = sb("szc", [128, 1], F32, p0)
                P.op("dve", lambda e: e.memset(zc[:], 0.0), w=["szc"])
                for sq in range(NS):
                    sl = sq % 2
                    P.dma("pool", lambda e, sq=sq, sl=sl: e.indirect_dma_start(
                        out=Lg[:, sl, :, :].rearrange("p a b -> p (a b)"), out_offset=None, in_=clf_full,
                        in_offset=bass.IndirectOffsetOnAxis(ap=idxL[:, sq // 8, sq % 8:sq % 8 + 1], axis=0)), r=["idxL"], w=[("sLg", sl)])
                    for ee in range(2):
                        P.op("dve", lambda e, sl=sl, ee=ee: e.tensor_tensor_scan(
                            out=pre[:, sl, ee, :], data0=onesf[:], data1=Lg[:, sl, ee, :], initial=zc[:, 0:1],
                            op0=ALU.mult, op1=ALU.add), r=[("sLg", sl), "sonesf", "szc"], w=[("spre", sl)])
                    bank = ps[sq % 2]
                    P.op("dve", lambda e, sl=sl: e.tensor_copy(out=TT[:, sl, :], in_=pre[:, sl, :, 63]),
                         r=[("spre", sl)], w=[("sTT", sl)])
                    P.op("pe", lambda e, sl=sl, bank=bank: e.matmul(out=bank[:, 0:2], lhsT=scst[:, 0, 0:128], rhs=TT[:, sl, :],
                                                                    start=True, stop=True),
                         r=["scst", ("sTT", sl)], w=[("ps", id(bank))])
                    P.op("dve", lambda e, sl=sl, bank=bank: e.tensor_tensor(out=LT[:, sl, :], in0=bank[:, 0:2], in1=TT[:, sl, :],
                                                                            op=ALU.add),
                         r=[("ps", id(bank)), ("sTT", sl)], w=[("sLT", sl)])
                    for ee in range(2):
                        P.op("dve", lambda e, sl=sl, ee=ee, sq=sq: e.tensor_scalar(
                            out=Dall[:, sq, :, ee], in0=pre[:, sl, ee, :], scalar1=-1.0, scalar2=LT[:, sl, ee:ee + 1],
                            op0=ALU.mult, op1=ALU.add), r=[("spre", sl), ("sLT", sl)], w=["sDall"])
                if debug:
                    P.dma("sp", lambda e: e.dma_start(out=dbg[:, 2836:2964], in_=Lg[:, 1].rearrange("p a b -> p (a b)")),
                          r=[("sLg", 1)], w=["dbg5"])
                    P.dma("sp", lambda e: e.dma_start(out=dbg[:, 3092:3220], in_=pre[:, 1].rearrange("p a b -> p (a b)")),
                          r=[("spre", 1)], w=["dbg6"])
                    P.dma("sp", lambda e: e.dma_start(out=dbg[:, 3348:3352], in_=LT[:].rearrange("p a b -> p (a b)")),
                          r=[("sLT", 0), ("sLT", 1)], w=["dbg7"])
                    P.dma("sp", lambda e: e.dma_start(out=dbg[:, 3352:3356], in_=TT[:].rearrange("p a b -> p (a b)")),
                          r=[("sTT", 0), ("sTT", 1)], w=["dbg8"])
                    P.dma("sp", lambda e: e.dma_start(out=dbg[:, 3356:3388].bitcast(I32), in_=idxL[:].rearrange("p a b -> p (a b)")),
                          r=["idxL"], w=["dbg9"])
                P.drain("sp")
                P.flush()
            for l in range(2):
                with ExitStack() as p1:
                    wg = sb("swg", [128, 8, D], BF16, p1)
                    wq = sb("swq", [128, 8, D], BF16, p1)
                    w_out = sb("sw_out", [128, 8, D], BF16, p1)
                    xh = sb("sxh", [128, 2, D], BF16, p1)
                    xnT = sb("sxnT", [128, 8, 256], BF16, p1)
                    sg = sb("ssg", [128, 2, D], BF16, p1)
                    thg = sb("sthg", [128, 512], F32, p1)
                    Qbd = sb("sQbd", [128, NS, 16], BF16, p1)
                    Kst = sb("sKst", [128, 2, 32, 128], F32, p1)
                    Vst = sb("sVst", [128, 1, 32, 128], F32, p1)
                    Vbf = sb("sVbf", [128, 2, 32, 129], BF16, p1)
                    KT = sb("sKT", [128, 2, 32, 128], BF16, p1)
                    Ssb = sb("sSsb", [128, 2, 512], F32, p1)
                    Psb = sb("sPsb", [128, 2, 512], BF16, p1)
                    Sn = sb("sSn", [128, 16], F32, p1)
                    Pn = sb("sPn", [128, 16], BF16, p1)
                    Osb = sb("sOsb", [16, 2, 130], F32, p1)
                    ofull = Kst[:, 0, 0:16, :].rearrange("p (a b) c -> p a (b c)", a=2)
                    mtok = sb("smtok", [128, 2, D], BF16, p1)
                    mT = sb("smT", [128, 8, 256], BF16, p1)
                    P.dma("pool", lambda e: e.dma_start(out=wg[:], in_=b_w_in[l][:, D:2 * D].rearrange("(kc p) n -> p kc n", p=128)),
                          w=["swg"])
                    P.dma("pool", lambda e: e.dma_start(out=wq[:], in_=b_w_in[l][:, 0:D].rearrange("(kc p) n -> p kc n", p=128)),
                          w=["swq"])
                    P.dma("pool", lambda e: e.dma_start(out=w_out[:], in_=b_w_out[l].rearrange("(kc p) n -> p kc n", p=128)),
                          w=["sw_out"])
                    P.dma("sp", lambda e: e.dma_start(out=gB[:], in_=rowv_d[2 + l:3 + l, :].partition_broadcast(128)), w=["sgB"])
                    P.op("pool", lambda e: e.memset(Vbf[:, :, :, 128:129], 1.0), w=[("sVbf", 0), ("sVbf", 1)])
                    P.op("pool", lambda e: e.memset(Qbd[:], 0.0), w=["sQbd"])
                    x_ap = lambda tt: xs_t[:, tt, :]
                    xkey = lambda tt: "sxs"
                    rms_to_featmajor(x_ap, xkey, 2, lambda kc: pv("bpre", l, kc), xh, xnT, [ps[0], ps[1]], "s")
                    Qv = Qbd[:].rearrange("p (s h) c -> p s h c", h=8)
                    for hp in range(8):
                        bank = ps[2 + hp % 2]
                        for kc in range(8):
                            P.op("pe", lambda e, kc=kc, hp=hp, bank=bank: e.matmul(
                                out=bank[:, 0:256], lhsT=wq[:, kc, hp * 128:(hp + 1) * 128], rhs=xnT[:, kc, :],
                                start=(kc == 0), stop=(kc == 7)), r=["swq", "sxnT"], w=[("ps", id(bank))])
                        for ee in range(2):
                            P.op("dve", lambda e, ee=ee, hp=hp, bank=bank: e.tensor_scalar(
                                out=Qv[ee * 64:(ee + 1) * 64, :, hp, ee * 8:(ee + 1) * 8],
                                in0=bank[ee * 64:(ee + 1) * 64, 0:32].rearrange("p (s t) -> p s t", t=8),
                                scalar1=0.125, scalar2=None, op0=ALU.mult), r=[("ps", id(bank))], w=["sQbd"])
                    for tt in range(2):
                        for fc in range(2):
                            bank = ps[4 + fc]
                            for kc in range(8):
                                P.op("pe", lambda e, kc=kc, tt=tt, fc=fc, bank=bank: e.matmul(
                                    out=bank[:], lhsT=xnT[:, kc, tt * 128:(tt + 1) * 128], rhs=wg[:, kc, fc * 512:(fc + 1) * 512],
                                    start=(kc == 0), stop=(kc == 7)), r=["swg", "sxnT"], w=[("ps", id(bank))])
                            P.op("act", lambda e, bank=bank: e.activation(out=thg[:], in_=bank[:], func=AF.Tanh, scale=0.5),
                                 r=[("ps", id(bank))], w=["sthg"])
                            P.op("dve", lambda e, bank=bank: e.scalar_tensor_tensor(
                                out=thg[:], in0=thg[:], scalar=1.0, in1=bank[:], op0=ALU.add, op1=ALU.mult),
                                r=[("ps", id(bank)), "sthg"], w=["sthg"])
                            P.op("pool", lambda e, tt=tt, fc=fc: e.tensor_scalar(
                                out=sg[:, tt, fc * 512:(fc + 1) * 512], in0=thg[:], scalar1=0.5, scalar2=0.0,
                                op0=ALU.mult, op1=ALU.add), r=["sthg"], w=["ssg"])
                    for sq in range(NS):
                        ob = ps[6 + sq % 2]
                        osl = sq % 2
                        for half in range(2):
                            bs = (sq * 2 + half) % 2
                            P.dma("pool", lambda e, sq=sq, half=half, bs=bs: e.indirect_dma_start(
                                out=Kst[:, bs, :, :].rearrange("p a b -> p (a b)"), out_offset=None, in_=ck_full,
                                in_offset=bass.IndirectOffsetOnAxis(ap=idxK[:, half, sq // 8, sq % 8:sq % 8 + 1], axis=0)),
                                r=["idxK"], w=[("sKst", bs)])
                            P.dma("pool", lambda e, sq=sq, half=half, bs=bs: e.indirect_dma_start(
                                out=Vst[:, 0, :, :].rearrange("p a b -> p (a b)"), out_offset=None, in_=cv_full,
                                in_offset=bass.IndirectOffsetOnAxis(ap=idxK[:, half, sq // 8, sq % 8:sq % 8 + 1], axis=0)),
                                r=["idxK"], w=[("sVst", 0)])
                            P.op("act", lambda e, bs=bs: e.activation(out=Vbf[:, bs, :, 0:128], in_=Vst[:, 0, :, :], func=AF.Copy),
                                 r=[("sVst", 0)], w=[("sVbf", bs)])
                            for g4 in range(8):
                                tb = ps[g4 % 2]
                                for k4 in range(4):
                                    t = g4 * 4 + k4
                                    P.op("pe", lambda e, bs=bs, t=t, k4=k4, tb=tb: e.transpose(
                                        out=tb[:, k4 * 128:(k4 + 1) * 128], in_=Kst[:, bs, t, :], identity=ident),
                                        r=[("sKst", bs), "cst"], w=[("ps", id(tb))])
                                if g4 % 2 == 0:
                                    P.op("act", lambda e, bs=bs, g4=g4, tb=tb: e.activation(
                                        out=KT[:, bs, g4 * 4:(g4 + 1) * 4, :], in_=tb[:].rearrange("p (a b) -> p a b", b=128),
                                        func=AF.Copy), r=[("ps", id(tb))], w=[("sKT", bs)])
                                else:
                                    P.op("dve", lambda e, bs=bs, g4=g4, tb=tb: e.tensor_copy(
                                        out=KT[:, bs, g4 * 4:(g4 + 1) * 4, :], in_=tb[:].rearrange("p (a b) -> p a b", b=128)),
                                        r=[("ps", id(tb))], w=[("sKT", bs)])
                            sbk = ps[2 + bs]
                            for t in range(32):
                                P.op("pe", lambda e, bs=bs, t=t, sq=sq, sbk=sbk: e.matmul(
                                    out=sbk[:, t * 16:(t + 1) * 16], lhsT=KT[:, bs, t, :], rhs=Qbd[:, sq, :], start=True, stop=True),
                                    r=[("sKT", bs), "sQbd"], w=[("ps", id(sbk))])
                            P.op("dve", lambda e, bs=bs, sq=sq, half=half, sbk=sbk: e.tensor_tensor(
                                out=Ssb[:, bs, :].rearrange("p (a q) -> p a q", q=8), in0=sbk[:].rearrange("p (a q) -> p a q", q=8),
                                in1=Dall[:, sq, half * 32:(half + 1) * 32, :].rearrange("p t e -> p (t e)").unsqueeze(2).to_broadcast([128, 64, 8]),
                                op=ALU.add), r=[("ps", id(sbk)), "sDall"], w=[("sSsb", bs)])
                            P.op("act", lambda e, bs=bs: e.activation(out=Psb[:, bs, :], in_=Ssb[:, bs, :], func=AF.Exp),
                                 r=[("sSsb", bs)], w=[("sPsb", bs)])
                            for t in range(32):
                                P.op("pe", lambda e, bs=bs, t=t, half=half, ob=ob: e.matmul(
                                    out=ob[0:16, 0:129], lhsT=Psb[:, bs, t * 16:(t + 1) * 16], rhs=Vbf[:, bs, t, :],
                                    start=(half == 0 and t == 0), stop=False),
                                    r=[("sPsb", bs), ("sVbf", bs)], w=[("ps", id(ob))])
                        sl_, hp_ = sq // 8, sq % 8
                        nb = ps[4]
                        P.op("pe", lambda e, hp_=hp_, sq=sq, nb=nb: e.matmul(out=nb[:, 0:16], lhsT=kTn[:, hp_, 0:128],
                                                                            rhs=Qbd[:, sq, :], start=True, stop=True),
                             r=["kTn", "sQbd"], w=[("ps", id(nb))])
                        P.op("dve", lambda e, sl_=sl_, nb=nb: e.tensor_tensor(
                            out=Sn[:], in0=nb[:, 0:16], in1=scst[:, 1, sl_ * 16:(sl_ + 1) * 16], op=ALU.add),
                            r=[("ps", id(nb)), "scst"], w=["sSn"])
                        P.op("dve", lambda e, hp_=hp_: e.tensor_tensor(
                            out=Sn[:].rearrange("p (e q) -> p e q", q=8), in0=Sn[:].rearrange("p (e q) -> p e q", q=8),
                            in1=negE[:, 2 * hp_:2 * hp_ + 2].unsqueeze(2).to_broadcast([128, 2, 8]), op=ALU.add),
                            r=["sSn", "negE"], w=["sSn"])
                        P.op("act", lambda e: e.activation(out=Pn[:], in_=Sn[:], func=AF.Exp), r=["sSn"], w=["sPn"])
                        P.op("pe", lambda e, hp_=hp_, ob=ob: e.matmul(out=ob[0:16, 0:129], lhsT=Pn[:], rhs=vn1[:, hp_, :],
                                                                      start=False, stop=True),
                             r=["sPn", "vn1"], w=[("ps", id(ob))])
                        P.op("dve", lambda e, ob=ob, osl=osl: e.reciprocal(out=Osb[:, osl, 129:130], in_=ob[0:16, 128:129]),
                             r=[("ps", id(ob))], w=[("sOsb", osl)])
                        P.op("dve", lambda e, ob=ob, osl=osl: e.tensor_scalar(
                            out=Osb[:, osl, 0:128], in0=ob[0:16, 0:128], scalar1=Osb[:, osl, 129:130], scalar2=None, op0=ALU.mult),
                            r=[("ps", id(ob)), ("sOsb", osl)], w=[("sOsb", osl)])
                        for ee in range(2):
                            P.dma("sp", lambda e, sq=sq, ee=ee, osl=osl: e.dma_start(
                                out=o_scr[l][(sq // 8) * 8:(sq // 8 + 1) * 8, (sq % 8) * 128 + ee * 64:(sq % 8) * 128 + (ee + 1) * 64],
                                in_=Osb[ee * 8:(ee + 1) * 8, osl, ee * 64:(ee + 1) * 64]),
                                r=[("sOsb", osl), ("o_scr", l)], w=[("o_scrw", l)])
                    P.dma("sp", lambda e: e.dma_start(out=ofull, in_=o_scr[l].rearrange("(t p) f -> p t f", p=128)),
                          r=[("o_scrw", l), ("o_scr", l)], w=[("sKst", 0)])
                    if debug and l == 0:
                        P.dma("sp", lambda e: e.dma_start(out=dbg[:, 0:512], in_=Dall[:, 0:4, :, :].rearrange("p a b c -> p (a b c)")),
                              r=["sDall"], w=["dbg0"])
                        P.dma("sp", lambda e: e.dma_start(out=dbg[:, 512:1536], in_=ofull[:, 0, :]), r=[("sKst", 0)], w=["dbg1"])
                        P.dma("sp", lambda e: e.dma_start(out=dbg[:, 1536:2560], in_=Ssb[:].rearrange("p a b -> p (a b)")),
                              r=[("sSsb", 0), ("sSsb", 1)], w=["dbg2"])
                        P.dma("sp", lambda e: e.dma_start(out=dbg[0:16, 2560:2816].rearrange("p (a b) -> p a b", a=2), in_=Osb[:, :, 0:128]),
                              r=[("sOsb", 0), ("sOsb", 1)], w=["dbg3"])
                        P.dma("sp", lambda e: e.dma_start(out=dbg[:, 2820:2836], in_=Sn[:]), r=["sSn"], w=["dbg4"])
                    for tt in range(2):
                        P.op("dve", lambda e, tt=tt: e.tensor_tensor(out=mtok[:, tt, :], in0=ofull[:, tt, :], in1=sg[:, tt, :],
                                                                     op=ALU.mult), r=[("sKst", 0), "ssg"], w=["smtok"])
                    for bi in range(2):
                        bank = ps[bi]
                        bv = bank[:].bitcast(BF16)
                        for kk in range(4):
                            kc = bi * 4 + kk
                            for tt in range(2):
                                P.op("pe", lambda e, kc=kc, kk=kk, tt=tt, bv=bv: e.transpose(
                                    out=bv[:, kk * 256 + tt * 128: kk * 256 + (tt + 1) * 128],
                                    in_=mtok[:, tt, kc * 128:(kc + 1) * 128], identity=identb[:]),
                                    r=["smtok", "identb"], w=[("ps", id(bank))])
                        P.op("dve", lambda e, bi=bi, bv=bv: e.tensor_copy(
                            out=mT[:, bi * 4:(bi + 1) * 4, :], in_=bv[:].rearrange("p (a b) -> p a b", b=256)),
                            r=[("ps", id(bank))], w=["smT"])
                    for tt in range(2):
                        yb = [ps[6], ps[7]]
                        for fc in range(2):
                            for kc in range(8):
                                P.op("pe", lambda e, tt=tt, fc=fc, kc=kc, yb=yb: e.matmul(
                                    out=yb[fc][:], lhsT=mT[:, kc, tt * 128:(tt + 1) * 128], rhs=w_out[:, kc, fc * 512:(fc + 1) * 512],
                                    start=(kc == 0), stop=(kc == 7)), r=["smT", "sw_out"], w=[("ps", id(yb[fc]))])
                        post_norm_residual(yb, xs_t[:, tt, :], "sxs", gB, "sgB", ytmp)
                    if l == 1:
                        P.dma("sp", lambda e: e.dma_start(out=ys_out.rearrange("(t p) f -> p t f", p=128), in_=xs_t[:]),
                              r=["sxs"], w=["ys_out"])
                    P.drain("sp")
                    P.flush()

    if do_prompt:
        b_phase()
    if do_sample:
        s_phase()

    P.drain("sp")
    P.flush()
    st.close()
    return nc


_NC_CACHE = {}


def kernel(x_prompt, x_sample, cache_k, cache_v, cache_logf, state_rnn, state_conv, page_table,
           a_pre_norm, a_post_norm, a_w_in, a_conv_w, a_conv_b, a_w_ga, a_b_ga, a_w_gx, a_b_gx,
           a_lambda, a_w_out, kv_norm, w_kv, b_f, b_pre_norm, b_post_norm, b_w_in, b_w_out,
           _trace=False, _do_prompt=True, _debug=False):
    f32 = np.float32
    A = lambda v: np.ascontiguousarray(np.asarray(v, f32))
    npool = int(np.asarray(cache_k).shape[0])
    key = ("prog", npool, _do_prompt, _debug)
    if key not in _NC_CACHE:
        _NC_CACHE[key] = build_program(_do_prompt, True, npool=npool, debug=_debug)
    nc = _NC_CACHE[key]

    pvec = np.zeros((128, NPV), f32)
    for l in range(2):
        pvec[:, PV[("apre", l)]:PV[("apre", l)] + 8] = fm(a_pre_norm[l])
        for j in range(4):
            pvec[:, PV[("cw%d" % j, l)]:PV[("cw%d" % j, l)] + 8] = fm(np.asarray(a_conv_w)[l, j])
        pvec[:, PV[("cb", l)]:PV[("cb", l)] + 8] = fm(a_conv_b[l])
        pvec[:, PV[("bga", l)]:PV[("bga", l)] + 8] = fm(a_b_ga[l])
        pvec[:, PV[("bgx", l)]:PV[("bgx", l)] + 8] = fm(a_b_gx[l])
        pvec[:, PV[("lam", l)]:PV[("lam", l)] + 8] = fm(a_lambda[l])
        pvec[:, PV[("bpre", l)]:PV[("bpre", l)] + 8] = fm(b_pre_norm[l])
    pvec[:, PV[("kvn", 0)]:PV[("kvn", 0)] + 8] = fm(kv_norm)
    rowv = np.zeros((5, D), f32)
    rowv[0:2] = A(a_post_norm); rowv[2:4] = A(b_post_norm); rowv[4, 0:H] = A(b_f)

    ii = np.arange(128)
    ident = np.eye(128, dtype=f32)
    tri = (ii[:, None] <= ii[None, :]).astype(f32)
    ones = np.ones((128, 128), f32)
    causal = np.where(ii[:, None] <= ii[None, :], 0.0, NEG).astype(f32)
    full_ok = np.zeros((128, 128), f32)
    full_no = np.full((128, 128), NEG, f32)

    scst = np.zeros((128, 3, 256), f32)
    order = (ii % 64) * 2 + ii // 64
    scst[:, 0, 0:128] = (order[:, None] > order[None, :]).astype(f32)
    scst[:, 0, 128] = ii // 64
    scst[:, 0, 129] = 2 * (ii // 64)
    scst[:, 0, 130] = 2 * (ii // 64) + 1
    ms = np.full((128, 16, 2, 8), NEG, f32)
    for p_ in range(128):
        s_, t_ = p_ // 8, p_ % 8
        ms[p_, s_, :, t_:] = 0.0
    scst[:, 1, :] = ms.reshape(128, 256)
    scst[:, 2, 0:128] = ((ii[:, None] // 8 == ii[None, :] // 8) & (ii[:, None] <= ii[None, :])).astype(f32)

    ck_full = np.ascontiguousarray(
        np.asarray(cache_k, f32).reshape(npool, 128, 8, 128).transpose(2, 0, 1, 3)).reshape(8 * npool * 4, 4096)
    cv_full = np.ascontiguousarray(
        np.asarray(cache_v, f32).reshape(npool, 128, 8, 128).transpose(2, 0, 1, 3)).reshape(8 * npool * 4, 4096)
    clf_full = np.ascontiguousarray(
        np.asarray(cache_logf, f32).reshape(npool, 2, 64, 8, 2).transpose(3, 0, 1, 4, 2)).reshape(8 * npool * 2, 128)
    pt = np.asarray(page_table).astype(np.int32)
    xs_all = A(x_sample)
    srnn = A(state_rnn); sconv = A(state_conv)

    xp_all = A(x_prompt)
    shared = {"a_w_in": A(a_w_in), "a_w_out": A(a_w_out), "a_w_ga": A(a_w_ga), "a_w_gx": A(a_w_gx), "w_kv": A(w_kv),
              "b_w_in": A(b_w_in), "b_w_out": A(b_w_out), "pvec": pvec, "scst": scst, "rowv": rowv,
              "ck_full": ck_full, "cv_full": cv_full, "clf_full": clf_full}
    in_maps = []
    for c in range(8):
        b, p = c // 2, c % 2
        cst = np.zeros((128, 6, 128), f32)
        cst[:, 0] = ident; cst[:, 1] = tri; cst[:, 2] = ones
        cst[:, 3] = causal if p == 0 else full_ok
        cst[:, 4] = full_no if p == 0 else causal
        idx = np.zeros((128, 32), np.int32)
        for j in range(16):
            idx[:, j] = (2 * j + p) * 128 + ii
            idx[:, 16 + j] = (2 * j + p) * 128 + 63
        xs = np.zeros((256, D), f32); xs[0:32] = xs_all[4 * c:4 * c + 4].reshape(32, D)
        st_rnn = np.zeros((2, NS, D), f32); st_rnn[:, 0:4] = srnn[:, 4 * c:4 * c + 4]
        st_conv = np.zeros((2, NS, 3, D), f32); st_conv[:, 0:4] = sconv[:, 4 * c:4 * c + 4]
        ptT2 = np.zeros((128, NS), np.int32)
        ptT2[:, 0:4] = np.tile(pt[4 * c:4 * c + 4].T, (2, 1))
        m = dict(shared)
        m.update({"xp": xp_all[b], "cst": cst, "idx": idx, "xs": xs, "ptT2": ptT2,
                  "st_rnn": np.ascontiguousarray(st_rnn.reshape(2, NS, 8, 128).transpose(0, 3, 2, 1)),
                  "st_conv": np.ascontiguousarray(st_conv.reshape(2, NS, 3, 8, 128).transpose(0, 4, 3, 1, 2))})
        in_maps.append(m)

    res = run_bass_kernel_spmd(nc, in_maps, core_ids=list(range(8)), trace=_trace)
    R = res.results
    y_prompt = np.zeros((4, T, D), f32)
    k_p = np.zeros((4, T, H, 64), f32); v_p = np.zeros((4, T, H, 64), f32); lf_p = np.zeros((4, T, H), f32)
    rnn_p = np.zeros((2, 4, D), f32); conv_p = np.zeros((2, 4, 3, D), f32)
    y_s = np.zeros((NS, 8, D), f32); k_s = np.zeros((NS, 8, H, 64), f32); v_s = np.zeros((NS, 8, H, 64), f32)
    lf_s = np.zeros((NS, 8, H), f32); rnn_s = np.zeros((2, NS, D), f32); conv_s = np.zeros((2, NS, 3, D), f32)
    for c in range(8):
        b, p = c // 2, c % 2
        r = R[c]
        if _do_prompt:
            y_prompt[b].reshape(16, 2, 128, D)[:, p] = np.asarray(r["y_own"])
            if p == 0:
                k_p[b] = np.asarray(r["k_out"]).reshape(T, H, 64)
                v_p[b] = np.asarray(r["v_out"]).reshape(T, H, 64)
                lf_p[b] = np.asarray(r["lf_out"])
                rnn_p[:, b] = np.asarray(r["rnn_out"]).transpose(0, 2, 1).reshape(2, D)
                conv_p[:, b] = np.asarray(r["conv_out"]).transpose(0, 3, 2, 1).reshape(2, 3, D)
        sl = slice(4 * c, 4 * c + 4)
        y_s[sl] = np.asarray(r["ys_out"], f32)[0:32].reshape(4, 8, D)
        k_s[sl] = np.asarray(r["ks_out"], f32)[0:32].reshape(4, 8, H, 64)
        v_s[sl] = np.asarray(r["vs_out"], f32)[0:32].reshape(4, 8, H, 64)
        lf_s[sl] = np.asarray(r["lfs_out"], f32)[0:32].reshape(4, 8, H)
        rnn_s[:, sl] = np.asarray(r["rnns_out"], f32).transpose(0, 3, 2, 1).reshape(2, NS, D)[:, 0:4]
        conv_s[:, sl] = np.asarray(r["convs_out"], f32).transpose(0, 3, 4, 2, 1).reshape(2, NS, 3, D)[:, 0:4]
    if _trace:
        kernel.last_exec_ns = res.exec_time_ns
    if _debug:
        kernel.last_dbg = np.asarray(R[0]["dbg"])
    return (y_prompt, y_s, k_p, v_p, lf_p, rnn_p, conv_p, k_s, v_s, lf_s, rnn_s, conv_s)
```

```python
import numpy as np
from contextlib import ExitStack
import concourse.bass as bass
import concourse.mybir as mybir
from concourse.bass_utils import run_bass_kernel_spmd

F32, BF16, I32 = mybir.dt.float32, mybir.dt.bfloat16, mybir.dt.int32
ALU = mybir.AluOpType
AF = mybir.ActivationFunctionType

D = 1024
T = 4096
NT = T // 128
CH = 256
NCH = T // CH
H = 16
NPOOL = 2560
NPG = 64
NS = 32
EPS = 1e-6
NEG = -30000.0


class Prog:
    ENG = ("pe", "act", "dve", "pool", "sp")
    NDS = 6

    def __init__(self, nc, stack):
        self.nc = nc
        self.stack = stack
        self.streams = {e: [] for e in self.ENG}
        self.cnt = {e: 0 for e in self.ENG}
        self.sem = {}
        self.known = {e: {} for e in self.ENG}
        self.lastw = {}
        self.readers = {}
        self.dsem = {}
        self.dcnt = {}
        self.nsem = 0
        self.semobj = {}
        for q in ("sp", "pool", "act"):
            self.dsem[q] = [self._newsem() for _ in range(self.NDS)]
            self.dcnt[q] = 0

    def _newsem(self):
        s = self.stack.enter_context(self.nc.semaphore("s%d" % self.nsem))
        self.semobj[self.nsem] = s
        self.nsem += 1
        return self.nsem - 1

    def _deps(self, eng, r, w):
        need = {}
        def add(ev):
            sid, val, src = ev
            if src == "pe" and eng == "pe":
                return
            if need.get(sid, 0) < val:
                need[sid] = val
        for k in r:
            if k in self.lastw:
                add(self.lastw[k])
        for k in w:
            if k in self.lastw:
                add(self.lastw[k])
            for ev in self.readers.get(k, {}).values():
                add(ev)
        waits = []
        kn = self.known[eng]
        for sid, val in need.items():
            if kn.get(sid, 0) < val:
                kn[sid] = val
                waits.append((sid, val))
        return waits

    def _commit(self, ev, r, w):
        for k in r:
            self.readers.setdefault(k, {})[ev[0]] = ev
        for k in w:
            self.lastw[k] = ev
            self.readers[k] = {}

    def op(self, eng, fn, r=(), w=()):
        waits = self._deps(eng, r, w)
        if eng not in self.sem or self.cnt[eng] >= 6000:
            self.sem[eng] = self._newsem()
            self.cnt[eng] = 0
        self.cnt[eng] += 1
        ev = (self.sem[eng], self.cnt[eng], eng)
        self.streams[eng].append((waits, fn, (self.sem[eng], 1)))
        self._commit(ev, r, w)

    def dma(self, q, fn, r=(), w=()):
        waits = self._deps(q, r, w)
        i = self.dcnt[q]
        self.dcnt[q] += 1
        sid = self.dsem[q][i % self.NDS]
        prev = 16 * (i // self.NDS)
        if prev > 0 and self.known[q].get(sid, 0) < prev:
            self.known[q][sid] = prev
            waits.append((sid, prev))
        self.streams[q].append((waits, fn, (sid, 16)))
        self._commit((sid, prev + 16, "dma"), r, w)

    def drain(self, eng="sp"):
        waits = []
        for q in self.dsem:
            n = self.dcnt[q]
            for j, sid in enumerate(self.dsem[q]):
                cntj = (n - j + self.NDS - 1) // self.NDS if n > j else 0
                if cntj > 0 and self.known[eng].get(sid, 0) < 16 * cntj:
                    self.known[eng][sid] = 16 * cntj
                    waits.append((sid, 16 * cntj))
        self.streams[eng].append((waits, None, None))

    def flush(self):
        nc = self.nc
        streams = self.streams
        semobj = self.semobj
        self.streams = {e: [] for e in self.ENG}

        def replay(name):
            def f(e):
                for waits, fn, inc in streams[name]:
                    if fn is None:
                        for sid, val in waits:
                            e.wait_ge(semobj[sid], val)
                        continue
                    for sid, val in waits[:-1]:
                        e.wait_ge(semobj[sid], val)
                    inst = fn(e)
                    if waits:
                        inst._wait_ge(semobj[waits[-1][0]], waits[-1][1])
                    inst.then_inc(semobj[inc[0]], inc[1])
            return f
        with nc.Block() as block:
            if streams["pe"]:
                block.tensor(replay("pe"))
            if streams["act"]:
                block.scalar(replay("act"))
            if streams["dve"]:
                block.vector(replay("dve"))
            if streams["pool"]:
                block.gpsimd(replay("pool"))
            if streams["sp"]:
                block.sync(replay("sp"))


def fm(v):
    return np.ascontiguousarray(np.asarray(v, np.float32).reshape(8, 128).T)


PV = {}
def _pv_layout():
    c = 0
    for l in range(2):
        for nm in ("apre", "cw0", "cw1", "cw2", "cw3", "cb", "bga", "bgx", "lam"):
            PV[(nm, l)] = c; c += 8
    PV[("kvn", 0)] = c; c += 8
    for l in range(2):
        PV[("bpre", l)] = c; c += 8
    return c
NPV = _pv_layout()


def build_program(do_prompt=True, do_sample=True, npool=NPOOL, debug=False):
    nc = bass.Bass("TRN2", target_bir_lowering=False)
    dram = {}

    def din(name, shape, dt=F32):
        dram[name] = nc.dram_tensor(name, list(shape), dt, kind="ExternalInput").ap()
        return dram[name]

    def dout(name, shape, dt=F32):
        dram[name] = nc.dram_tensor(name, list(shape), dt, kind="ExternalOutput").ap()
        return dram[name]

    def dscr(name, shape, dt=F32):
        dram[name] = nc.dram_tensor(name, list(shape), dt, kind="Internal").ap()
        return dram[name]

    xp = din("xp", [T, D])
    a_w_in = din("a_w_in", [2, D, 2 * D]); a_w_out = din("a_w_out", [2, D, D])
    a_w_ga = din("a_w_ga", [2, 8, 128, 128]); a_w_gx = din("a_w_gx", [2, 8, 128, 128])
    w_kv = din("w_kv", [D, 2 * D + H])
    b_w_in = din("b_w_in", [2, D, 2 * D]); b_w_out = din("b_w_out", [2, D, D])
    pvec_d = din("pvec", [128, NPV])
    rowv_d = din("rowv", [5, D])
    cst_d = din("cst", [128, 6, 128])
    idx_d = din("idx", [128, 32], I32)
    xs_d = din("xs", [256, D])
    st_rnn_d = din("st_rnn", [2, 128, 8, NS]); st_conv_d = din("st_conv", [2, 128, 8, NS, 3])
    ck_full = din("ck_full", [8 * npool * 4, 4096])
    cv_full = din("cv_full", [8 * npool * 4, 4096])
    clf_full = din("clf_full", [8 * npool * 2, 128])
    ptT2_d = din("ptT2", [128, NS], I32)
    scst_d = din("scst", [128, 3, 256])
    ys_out = dout("ys_out", [256, D]); ks_out = dout("ks_out", [256, D]); vs_out = dout("vs_out", [256, D])
    lfs_out = dout("lfs_out", [256, H])
    rnns_out = dout("rnns_out", [2, 128, 8, NS]); convs_out = dout("convs_out", [2, 128, 8, NS, 3])
    xs1 = dscr("xs1", [256, D]); xs2 = dscr("xs2", [256, D])
    o_scr = [dscr("o_scr%d" % i, [256, D]) for i in range(2)]
    dbg = dout("dbg", [128, 4096]) if debug else None
    y_own = dout("y_own", [16, 128, D])
    k_out = dout("k_out", [T, D]); v_out = dout("v_out", [T, D]); lf_out = dout("lf_out", [T, H])
    rnn_out = dout("rnn_out", [2, 128, 8]); conv_out = dout("conv_out", [2, 128, 8, 3])
    x1s = dscr("x1s", [T, D]); x2s = dscr("x2s", [T, D])
    kTs = dscr("kTs", [8, 128, T], BF16)
    vs = dscr("vs", [8, 128, NT, 128], BF16)
    cs = dscr("cs", [T, H])

    st = ExitStack()
    P = Prog(nc, st)

    _uid = [0]

    def sb(name, shape, dt=F32, stack=None):
        _uid[0] += 1
        return (stack or st).enter_context(nc.sbuf_tensor("t%d_%s" % (_uid[0], name), list(shape), dt))

    ps = [st.enter_context(nc.psum_tensor("ps%d" % i, [128, 512], F32)) for i in range(8)]

    pvec = sb("pvec", [128, NPV])
    cst = sb("cstt", [128, 6, 128])
    identb = sb("identb", [128, 128], BF16)
    idx = sb("idxt", [128, 32], I32)
    cneg = sb("cneg", [128, 2, 8]); cnegh = sb("cnegh", [128, 2, 8]); cneg2 = sb("cneg2", [128, 2, 8])
    bgah = sb("bgah", [128, 2, 8]); bgxh = sb("bgxh", [128, 2, 8])
    bfB = sb("bfB", [128, H + 2])
    scst = sb("scst", [128, 3, 256])
    kTn = sb("kTn", [128, 8, 256], BF16)
    vn1 = sb("vn1", [128, 8, 129], BF16)
    negE = sb("negE", [128, H])
    ptT2 = sb("ptT2", [128, NS], I32)
    idxK = sb("idxK", [128, 2, 4, 8], I32)
    idxL = sb("idxL", [128, 4, 8], I32)
    offs = sb("offs", [128, 3, 8])
    ck = sb("ck", [128, NT, H])
    lacc = sb("lacc", [128, H])
    stat = sb("stat", [128, 16])

    ident = cst[:, 0, :]; tri = cst[:, 1, :]; ones = cst[:, 2, :]
    mask_a = cst[:, 3, :]; mask_b = cst[:, 4, :]

    def pv(nm, l, n=None):
        c = PV[(nm, l)]
        return pvec[:, c:c + 8] if n is None else pvec[:, c + n:c + n + 1]

    P.dma("sp", lambda e: e.dma_start(out=pvec[:], in_=pvec_d), w=["pvec"])
    P.dma("sp", lambda e: e.dma_start(out=cst[:], in_=cst_d), w=["cst"])
    P.dma("sp", lambda e: e.dma_start(out=idx[:], in_=idx_d), w=["idx"])
    P.dma("sp", lambda e: e.dma_start(out=bfB[:], in_=rowv_d[4:5, 0:H + 2].partition_broadcast(128)), w=["bfB"])
    P.dma("sp", lambda e: e.dma_start(out=scst[:], in_=scst_d), w=["scst"])
    P.dma("sp", lambda e: e.dma_start(out=ptT2[:], in_=ptT2_d), w=["ptT2"])
    P.op("pool", lambda e: e.memset(vn1[:, :, 128:129], 1.0), w=["vn1"])
    for hp in range(8):
        for half in range(2):
            P.op("pool", lambda e, half=half, hp=hp: e.tensor_scalar(
                out=offs[:, half, hp:hp + 1], in0=scst[:, 0, 129 + half:130 + half], scalar1=float(hp * npool * 4), scalar2=0.0,
                op0=ALU.add, op1=ALU.add), r=["scst"], w=["offs"])
        P.op("pool", lambda e, hp=hp: e.tensor_scalar(
            out=offs[:, 2, hp:hp + 1], in0=scst[:, 0, 128:129], scalar1=float(hp * npool * 2), scalar2=0.0,
            op0=ALU.add, op1=ALU.add), r=["scst"], w=["offs"])
    for hp in range(8):
        for half in range(2):
            P.op("pool", lambda e, half=half, hp=hp: e.tensor_scalar(
                out=idxK[:, half, :, hp], in0=ptT2[:, 0:4], scalar1=4.0, scalar2=offs[:, half, hp:hp + 1],
                op0=ALU.mult, op1=ALU.add), r=["ptT2", "offs"], w=["idxK"])
        P.op("pool", lambda e, hp=hp: e.tensor_scalar(
            out=idxL[:, :, hp], in0=ptT2[:, 0:4], scalar1=2.0, scalar2=offs[:, 2, hp:hp + 1],
            op0=ALU.mult, op1=ALU.add), r=["ptT2", "offs"], w=["idxL"])
    P.op("dve", lambda e: e.tensor_copy(out=identb[:], in_=ident), r=["cst"], w=["identb"])
    P.op("dve", lambda e: e.memset(lacc[:], 0.0), w=["lacc"])
    for l in range(2):
        P.op("act", lambda e, l=l: e.activation(out=cneg[:, l, :], in_=pv("lam", l), func=AF.Exp, scale=-1.0),
             r=["pvec"], w=["cneg"])
        P.op("act", lambda e, l=l: e.activation(out=cneg[:, l, :], in_=cneg[:, l, :], func=AF.Ln, bias=1.0),
             r=["cneg"], w=["cneg"])
        P.op("dve", lambda e, l=l: e.tensor_scalar(out=cnegh[:, l, :], in0=cneg[:, l, :], scalar1=-4.0, scalar2=None,
                                                   op0=ALU.mult), r=["cneg"], w=["cnegh"])
        P.op("dve", lambda e, l=l: e.tensor_scalar(out=cneg2[:, l, :], in0=cneg[:, l, :], scalar1=-8.0, scalar2=None,
                                                   op0=ALU.mult), r=["cneg"], w=["cneg2"])
        P.op("dve", lambda e, l=l: e.tensor_scalar(out=bgah[:, l, :], in0=pv("bga", l), scalar1=0.5, scalar2=None,
                                                   op0=ALU.mult), r=["pvec"], w=["bgah"])
        P.op("dve", lambda e, l=l: e.tensor_scalar(out=bgxh[:, l, :], in0=pv("bgx", l), scalar1=0.5, scalar2=None,
                                                   op0=ALU.mult), r=["pvec"], w=["bgxh"])

    def rms_to_featmajor(x_ap, xkey, ntt, gain_l, xh, xnT, tpb, pfx):
        for tt in range(ntt):
            c0 = tt * 2
            P.op("act", lambda e, tt=tt, c0=c0: e.activation(out=xh[:, tt, :], in_=x_ap(tt), func=AF.Square,
                                                             accum_out=stat[:, c0:c0 + 1]),
                 r=[xkey(tt)], w=[pfx + "xh", ("stat", c0)])
            P.op("act", lambda e, c0=c0: e.activation(out=stat[:, c0 + 1:c0 + 2], in_=stat[:, c0:c0 + 1], func=AF.Sqrt,
                                                      scale=1.0 / D, bias=epsb[:, 0:1]), r=[("stat", c0), "epsb"], w=[("stat", c0)])
            P.op("dve", lambda e, c0=c0: e.reciprocal(out=stat[:, c0 + 1:c0 + 2], in_=stat[:, c0 + 1:c0 + 2]),
                 r=[("stat", c0)], w=[("stat", c0)])
            P.op("pool", lambda e, tt=tt, c0=c0: e.tensor_scalar(out=xh[:, tt, :], in0=x_ap(tt), scalar1=stat[:, c0 + 1:c0 + 2],
                                                                 scalar2=0.0, op0=ALU.mult, op1=ALU.add),
                 r=[xkey(tt), ("stat", c0)], w=[pfx + "xh"])
        W = ntt * 128
        per = 1024 // W
        for bi in range(8 // per):
            bank = tpb[bi]
            bv = bank[:].bitcast(BF16)
            for kk in range(per):
                kc = bi * per + kk
                for tt in range(ntt):
                    P.op("pe", lambda e, kc=kc, kk=kk, tt=tt, bv=bv: e.transpose(
                        out=bv[:, kk * W + tt * 128: kk * W + (tt + 1) * 128], in_=xh[:, tt, kc * 128:(kc + 1) * 128],
                        identity=identb[:]), r=[pfx + "xh", "identb"], w=[("ps", id(bank))])
            for kk in range(per):
                kc = bi * per + kk
                if kc % 2 == 0:
                    P.op("dve", lambda e, kc=kc, kk=kk, bv=bv: e.tensor_scalar(
                        out=xnT[:, kc, 0:W], in0=bv[:, kk * W:(kk + 1) * W], scalar1=gain_l(kc), scalar2=None, op0=ALU.mult),
                        r=[("ps", id(bank)), "pvec"], w=[pfx + "xnT"])
                else:
                    P.op("act", lambda e, kc=kc, kk=kk, bv=bv: e.activation(
                        out=xnT[:, kc, 0:W], in_=bv[:, kk * W:(kk + 1) * W], func=AF.Copy, scale=gain_l(kc)),
                        r=[("ps", id(bank)), "pvec"], w=[pfx + "xnT"])

    def post_norm_residual(ybanks, x_tile_ap, xkey, gB, gkey, tmp):
        for hb in range(2):
            P.op("act", lambda e, hb=hb: e.activation(out=tmp[:, hb * 512:(hb + 1) * 512], in_=ybanks[hb][:], func=AF.Square,
                                                      accum_out=stat[:, 8 + hb:9 + hb]),
                 r=[("ps", id(ybanks[hb]))], w=["ytmp", "stat"])
        P.op("dve", lambda e: e.tensor_tensor(out=stat[:, 10:11], in0=stat[:, 8:9], in1=stat[:, 9:10], op=ALU.add),
             r=["stat"], w=["stat"])
        P.op("act", lambda e: e.activation(out=stat[:, 11:12], in_=stat[:, 10:11], func=AF.Sqrt, scale=1.0 / D,
                                           bias=epsb[:, 0:1]), r=["stat", "epsb"], w=["stat"])
        P.op("dve", lambda e: e.reciprocal(out=stat[:, 11:12], in_=stat[:, 11:12]), r=["stat"], w=["stat"])
        for hb in range(2):
            P.op("dve", lambda e, hb=hb: e.scalar_tensor_tensor(
                out=tmp[:, hb * 512:(hb + 1) * 512], in0=ybanks[hb][:], scalar=stat[:, 11:12],
                in1=gB[:, hb * 512:(hb + 1) * 512], op0=ALU.mult, op1=ALU.mult),
                r=[("ps", id(ybanks[hb])), "stat", gkey], w=["ytmp"])
        P.op("pool", lambda e: e.tensor_tensor(out=x_tile_ap, in0=x_tile_ap, in1=tmp[:], op=ALU.add),
             r=["ytmp", xkey], w=[xkey])

    epsb = sb("epsb", [128, 1])
    P.op("dve", lambda e: e.memset(epsb[:], EPS), w=["epsb"])

    def a_pass(l):
        with ExitStack() as ph:
            w_in = sb("aw_in", [128, 8, 2 * D], BF16, ph)
            w_out = sb("aw_out", [128, 8, D], BF16, ph)
            wga = sb("awga", [128, 8, 128], BF16, ph)
            wgx = sb("awgx", [128, 8, 128], BF16, ph)
            gB = sb("agB", [128, D], F32, ph)
            xt = sb("axt", [128, 2, 2, D], F32, ph)
            xh = sb("axh", [128, 2, D], BF16, ph)
            xnT = sb("axnT", [128, 8, CH], BF16, ph)
            ubuf = sb("aubuf", [128, 8, 352], F32, ph)
            hstS = sb("ahstS", [128, 8, NS], F32, ph)
            hlast = sb("ahlast", [128, 8, NS], F32, ph)
            ubv = lambda oc: ubuf[:, oc, 0:352].rearrange("p (s j) -> p s j", j=11)
            v3 = lambda ap: ap.rearrange("p (s t) -> p s t", t=8)
            uc = sb("auc", [128, 8, CH], F32, ph)
            ucb = sb("aucb", [128, 8, CH], BF16, ph)
            rbuf = sb("arbuf", [128, 8, CH], F32, ph)
            ibuf = sb("aibuf", [128, 8, CH], F32, ph)
            sbuf_ = sb("asbuf", [128, 8, CH], F32, ph)
            thg = sb("athg", [128, 8, CH], BF16, ph)
            hbuf = sbuf_
            mT = sb("amT", [128, 8, CH], BF16, ph)
            hst = sb("ahst", [128, 8], F32, ph)
            ytmp = sb("aytmp", [128, D], F32, ph)
            if l == 1:
                wkv = sb("awkv", [128, 8, 2 * D + H], BF16, ph)
                kvtok = sb("akvtok", [128, 2, D], F32, ph)
                vbf = sb("avbf", [128, 2, D], BF16, ph)
                kTc = sb("akTc", [128, 8, CH], BF16, ph)
                lft = sb("alft", [128, H], F32, ph)
                lfe = sb("alfe", [128, H], F32, ph)

            P.dma("pool", lambda e: e.dma_start(out=w_in[:], in_=a_w_in[l].rearrange("(kc p) n -> p kc n", p=128)), w=["aw_in"])
            P.dma("pool", lambda e: e.dma_start(out=w_out[:], in_=a_w_out[l].rearrange("(kc p) n -> p kc n", p=128)), w=["aw_out"])
            P.dma("pool", lambda e: e.dma_start(out=wga[:], in_=a_w_ga[l].rearrange("n c d -> c n d")), w=["awga"])
            P.dma("pool", lambda e: e.dma_start(out=wgx[:], in_=a_w_gx[l].rearrange("n c d -> c n d")), w=["awgx"])
            P.dma("sp", lambda e: e.dma_start(out=gB[:], in_=rowv_d[l:l + 1, :].partition_broadcast(128)), w=["agB"])
            if l == 1:
                P.dma("pool", lambda e: e.dma_start(out=wkv[:], in_=w_kv.rearrange("(kc p) n -> p kc n", p=128)), w=["awkv"])
            P.op("dve", lambda e: e.memset(ubuf[:, :, 0:3], 0.0), w=[("ubuf", o_) for o_ in range(8)])
            P.op("dve", lambda e: e.memset(hst[:], 0.0), w=["hst"])

            src = xp if l == 0 else x1s
            dst = x1s if l == 0 else x2s
            tpb = [ps[0], ps[1]]
            chunks = (list(range(NCH)) if do_prompt else []) + (["S"] if do_sample else [])
            def front(ch):
                sample = (ch == "S")
                slot = 0 if sample else ch % 2
                xkey = lambda tt, slot=slot: ("axt", slot)
                x_ap = lambda tt, slot=slot: xt[:, slot, tt, :]
                if sample:
                    if do_prompt:
                        P.dma("pool", lambda e: e.dma_start(out=rnn_out[l], in_=hst[:]), r=["hst"], w=[("rnn_out", l)])
                        P.dma("pool", lambda e: e.dma_start(out=conv_out[l], in_=ubuf[:, :, 0:3]), r=[("ubuf", o_) for o_ in range(8)], w=[("conv_out", l)])
                    ssrc = xs_d if l == 0 else xs1
                    P.dma("sp", lambda e, slot=slot: e.dma_start(
                        out=xt[:, slot, :, :], in_=ssrc.rearrange("(t p) f -> p t f", p=128)), w=[("axt", slot)])
                    for oc in range(8):
                        P.dma("sp", lambda e, oc=oc: e.dma_start(out=ubv(oc)[:, :, 0:3], in_=st_conv_d[l, :, oc]),
                              r=[], w=[("ubuf", oc)])
                    P.dma("sp", lambda e: e.dma_start(out=hstS[:], in_=st_rnn_d[l]), w=["hstS"])
                else:
                    P.dma("sp", lambda e, ch=ch, slot=slot: e.dma_start(
                        out=xt[:, slot, :, :], in_=src[ch * CH:(ch + 1) * CH, :].rearrange("(t p) f -> p t f", p=128)),
                        w=[("axt", slot)])
                rms_to_featmajor(x_ap, xkey, 2, lambda kc: pv("apre", l, kc), xh, xnT, tpb, "a")
                for oc in range(16):
                    bank = ps[2 + oc % 2]
                    for kc in range(8):
                        P.op("pe", lambda e, oc=oc, kc=kc, bank=bank: e.matmul(
                            out=bank[:, 0:CH], lhsT=w_in[:, kc, oc * 128:(oc + 1) * 128], rhs=xnT[:, kc, :],
                            start=(kc == 0), stop=(kc == 7)), r=["aw_in", "axnT"], w=[("ps", id(bank))])
                    if oc < 8:
                        if sample:
                            P.op("act", lambda e, oc=oc, bank=bank: e.activation(out=ubv(oc)[:, :, 3:11], in_=v3(bank[:, 0:CH]),
                                                                                 func=AF.Copy),
                                 r=[("ps", id(bank))], w=[("ubuf", oc)])
                        else:
                            P.op("act", lambda e, oc=oc, bank=bank: e.activation(out=ubuf[:, oc, 3:3 + CH], in_=bank[:, 0:CH],
                                                                                 func=AF.Copy),
                                 r=[("ps", id(bank))], w=[("ubuf", oc)])
                    else:
                        g = oc - 8
                        P.op("act", lambda e, g=g, bank=bank: e.activation(out=thg[:, g, :], in_=bank[:, 0:CH], func=AF.Tanh,
                                                                           scale=0.5),
                             r=[("ps", id(bank))], w=[("thg", g)])
                        P.op("dve", lambda e, g=g, bank=bank: e.scalar_tensor_tensor(
                            out=thg[:, g, :], in0=thg[:, g, :], scalar=1.0, in1=bank[:, 0:CH], op0=ALU.add, op1=ALU.mult),
                            r=[("ps", id(bank)), ("thg", g)], w=[("thg", g)])

            def mid(ch):
                sample = (ch == "S")
                slot = 0 if sample else ch % 2
                xkey = lambda tt, slot=slot: ("axt", slot)
                x_ap = lambda tt, slot=slot: xt[:, slot, tt, :]
                for oc in range(8):
                    if sample:
                        uin = lambda oc, j: ubv(oc)[:, :, j:j + 8]
                        uco = lambda oc: v3(uc[:, oc, :])
                    else:
                        uin = lambda oc, j: ubuf[:, oc, j:j + CH]
                        uco = lambda oc: uc[:, oc, :]
                    P.op("pool", lambda e, oc=oc, uin=uin, uco=uco: e.tensor_scalar(
                        out=uco(oc), in0=uin(oc, 3), scalar1=pv("cw3", l, oc), scalar2=pv("cb", l, oc),
                        op0=ALU.mult, op1=ALU.add), r=[("ubuf", oc), "pvec"], w=[("uc", oc)])
                    for j in range(3):
                        P.op("dve", lambda e, oc=oc, j=j, uin=uin, uco=uco: e.scalar_tensor_tensor(
                            out=uco(oc), in0=uin(oc, j), scalar=pv("cw%d" % j, l, oc), in1=uco(oc),
                            op0=ALU.mult, op1=ALU.add), r=[("ubuf", oc), ("uc", oc), "pvec"], w=[("uc", oc)])
                for oc in range(8):
                    P.op("act", lambda e, oc=oc: e.activation(out=ucb[:, oc, :], in_=uc[:, oc, :], func=AF.Copy),
                         r=[("uc", oc)], w=[("ucb", oc)])
                if sample:
                    for oc in range(8):
                        P.dma("pool", lambda e, oc=oc: e.dma_start(out=convs_out[l, :, oc], in_=ubv(oc)[:, :, 8:11]),
                              r=[("ubuf", oc)], w=[("convs_out", l, oc)])
                else:
                    P.op("pool", lambda e: e.tensor_copy(out=ubuf[:, :, 0:3], in_=ubuf[:, :, CH:CH + 3]), r=[("ubuf", o_) for o_ in range(8)], w=[("ubuf", o_) for o_ in range(8)])
                for oc in range(8):
                    bank = ps[4 + oc % 2]
                    P.op("pe", lambda e, oc=oc, bank=bank: e.matmul(out=bank[:, 0:CH], lhsT=wga[:, oc, :], rhs=ucb[:, oc, :],
                                                                    start=True, stop=True),
                         r=["awga", ("ucb", oc)], w=[("ps", id(bank))])
                    P.op("pe", lambda e, oc=oc, bank=bank: e.matmul(out=bank[:, CH:2 * CH], lhsT=wgx[:, oc, :], rhs=ucb[:, oc, :],
                                                                    start=True, stop=True),
                         r=["awgx", ("ucb", oc)], w=[("ps", id(bank))])
                    P.op("act", lambda e, oc=oc, bank=bank: e.activation(out=rbuf[:, oc, :], in_=bank[:, 0:CH], func=AF.Tanh,
                                                                         scale=0.5, bias=bgah[:, l, oc:oc + 1]),
                         r=[("ps", id(bank)), "bgah"], w=[("rbuf", oc)])
                    P.op("act", lambda e, oc=oc, bank=bank: e.activation(out=ibuf[:, oc, :], in_=bank[:, CH:2 * CH], func=AF.Tanh,
                                                                         scale=0.5, bias=bgxh[:, l, oc:oc + 1]),
                         r=[("ps", id(bank)), "bgxh"], w=[("ibuf", oc)])
                for oc in range(8):
                    P.op("act", lambda e, oc=oc: e.activation(out=sbuf_[:, oc, :], in_=rbuf[:, oc, :], func=AF.Exp,
                                                              scale=cneg2[:, l, oc:oc + 1], bias=cneg2[:, l, oc:oc + 1]),
                         r=[("rbuf", oc), "cneg2"], w=[("sbuf", oc)])
                    P.op("act", lambda e, oc=oc: e.activation(out=rbuf[:, oc, :], in_=rbuf[:, oc, :], func=AF.Exp,
                                                              scale=cnegh[:, l, oc:oc + 1], bias=cnegh[:, l, oc:oc + 1]),
                         r=[("rbuf", oc), "cnegh"], w=[("rbuf", oc)])
                for oc in range(8):
                    P.op("act", lambda e, oc=oc: e.activation(out=sbuf_[:, oc, :], in_=sbuf_[:, oc, :], func=AF.Sqrt, scale=-1.0,
                                                              bias=oneb[:, 0:1]),
                         r=[("sbuf", oc), "oneb"], w=[("sbuf", oc)])
                for oc in range(8):
                    P.op("dve", lambda e, oc=oc: e.scalar_tensor_tensor(
                        out=ibuf[:, oc, :], in0=ibuf[:, oc, :], scalar=1.0, in1=uc[:, oc, :], op0=ALU.add, op1=ALU.mult),
                        r=[("ibuf", oc), ("uc", oc)], w=[("ibuf", oc)])
                    P.op("dve", lambda e, oc=oc: e.scalar_tensor_tensor(
                        out=ibuf[:, oc, :], in0=ibuf[:, oc, :], scalar=0.5, in1=sbuf_[:, oc, :], op0=ALU.mult, op1=ALU.mult),
                        r=[("ibuf", oc), ("sbuf", oc)], w=[("ibuf", oc)])
                    if sample:
                        for sq in range(NS):
                            P.op("dve", lambda e, oc=oc, sq=sq: e.tensor_tensor_scan(
                                out=hbuf[:, oc, sq * 8:(sq + 1) * 8], data0=rbuf[:, oc, sq * 8:(sq + 1) * 8],
                                data1=ibuf[:, oc, sq * 8:(sq + 1) * 8], initial=hstS[:, oc, sq:sq + 1],
                                op0=ALU.mult, op1=ALU.add), r=[("rbuf", oc), ("ibuf", oc), "hstS"], w=[("sbuf", oc)])
                    else:
                        P.op("dve", lambda e, oc=oc: e.tensor_tensor_scan(
                            out=hbuf[:, oc, :], data0=rbuf[:, oc, :], data1=ibuf[:, oc, :], initial=hst[:, oc:oc + 1],
                            op0=ALU.mult, op1=ALU.add), r=[("rbuf", oc), ("ibuf", oc), "hst"], w=[("sbuf", oc)])
                    P.op("dve", lambda e, oc=oc: e.scalar_tensor_tensor(
                        out=mT[:, oc, :], in0=hbuf[:, oc, :], scalar=0.5, in1=thg[:, oc, :], op0=ALU.mult, op1=ALU.mult),
                        r=[("sbuf", oc), ("thg", oc)], w=[("amT", oc)])
                if sample:
                    P.op("pool", lambda e: e.tensor_copy(
                        out=hlast[:], in_=hbuf[:].rearrange("p o (s t) -> p o s t", t=8)[:, :, :, 7]), r=[("sbuf", o_) for o_ in range(8)], w=["hlast"])
                    P.dma("pool", lambda e: e.dma_start(out=rnns_out[l], in_=hlast[:]), r=["hlast"], w=[("rnns_out", l)])
                else:
                    P.op("pool", lambda e: e.tensor_copy(out=hst[:], in_=hbuf[:, :, CH - 1]), r=[("sbuf", o_) for o_ in range(8)], w=["hst"])

            def tail(ch):
                sample = (ch == "S")
                slot = 0 if sample else ch % 2
                xkey = lambda tt, slot=slot: ("axt", slot)
                x_ap = lambda tt, slot=slot: xt[:, slot, tt, :]
                for tt in range(2):
                    yb = [ps[6], ps[7]]
                    for fc in range(2):
                        for kc in range(8):
                            P.op("pe", lambda e, tt=tt, fc=fc, kc=kc: e.matmul(
                                out=yb[fc][:], lhsT=mT[:, kc, tt * 128:(tt + 1) * 128], rhs=w_out[:, kc, fc * 512:(fc + 1) * 512],
                                start=(kc == 0), stop=(kc == 7)), r=[("amT", kc), "aw_out"], w=[("ps", id(yb[fc]))])
                    post_norm_residual(yb, xt[:, slot, tt, :], ("axt", slot), gB, "agB", ytmp)
                if sample:
                    sdst = xs1 if l == 0 else xs2
                    P.dma("pool", lambda e, slot=slot: e.dma_start(
                        out=sdst.rearrange("(t p) f -> p t f", p=128), in_=xt[:, slot, :, :]),
                        r=[("axt", slot)], w=[("sdst", l)])
                else:
                    P.dma("pool", lambda e, ch=ch, slot=slot: e.dma_start(
                        out=dst[ch * CH:(ch + 1) * CH, :].rearrange("(t p) f -> p t f", p=128), in_=xt[:, slot, :, :]),
                        r=[("axt", slot)], w=[("dst", ch)])
                if l == 1:
                    rms_to_featmajor(x_ap, xkey, 2, lambda kc: pv("kvn", 0, kc), xh, xnT, tpb, "a")
                    for oc in range(8):
                        bank = ps[2 + oc % 2]
                        for kc in range(8):
                            P.op("pe", lambda e, oc=oc, kc=kc, bank=bank: e.matmul(
                                out=bank[:, 0:CH], lhsT=wkv[:, kc, oc * 128:(oc + 1) * 128], rhs=xnT[:, kc, :],
                                start=(kc == 0), stop=(kc == 7)), r=["awkv", "axnT"], w=[("ps", id(bank))])
                        P.op("act", lambda e, oc=oc, bank=bank: e.activation(out=kTc[:, oc, :], in_=bank[:, 0:CH], func=AF.Copy),
                             r=[("ps", id(bank))], w=["kTc"])
                    if sample:
                        P.op("pool", lambda e: e.tensor_copy(out=kTn[:], in_=kTc[:]), r=["kTc"], w=["kTn"])
                    else:
                        P.dma("pool", lambda e, ch=ch: e.dma_start(out=kTs[:, :, ch * CH:(ch + 1) * CH].rearrange("h p t -> p h t"),
                                                                   in_=kTc[:]), r=["kTc"], w=[("kTs", ch)])
                    ko, vo, lo = (ks_out, vs_out, lfs_out) if sample else (k_out, v_out, lf_out)
                    for tt in range(2):
                        tg = tt if sample else ch * 2 + tt
                        for part in range(2):
                            for fc in range(2):
                                bank = ps[4 + fc]
                                c0 = part * D + fc * 512
                                for kc in range(8):
                                    P.op("pe", lambda e, tt=tt, kc=kc, bank=bank, c0=c0: e.matmul(
                                        out=bank[:], lhsT=xnT[:, kc, tt * 128:(tt + 1) * 128], rhs=wkv[:, kc, c0:c0 + 512],
                                        start=(kc == 0), stop=(kc == 7)), r=["awkv", "axnT"], w=[("ps", id(bank))])
                                if fc == 0:
                                    P.op("act", lambda e, part=part, fc=fc, bank=bank: e.activation(
                                        out=kvtok[:, part, fc * 512:(fc + 1) * 512], in_=bank[:], func=AF.Copy),
                                        r=[("ps", id(bank))], w=["kvtok"])
                                else:
                                    P.op("dve", lambda e, part=part, fc=fc, bank=bank: e.tensor_copy(
                                        out=kvtok[:, part, fc * 512:(fc + 1) * 512], in_=bank[:]),
                                        r=[("ps", id(bank))], w=["kvtok"])
                                if part == 1:
                                    P.op("pool", lambda e, fc=fc, tt=tt: e.tensor_copy(
                                        out=vbf[:, tt, fc * 512:(fc + 1) * 512], in_=kvtok[:, 1, fc * 512:(fc + 1) * 512]),
                                        r=["kvtok"], w=["vbf"])
                        P.dma("pool", lambda e, tg=tg, ko=ko: e.dma_start(out=ko[tg * 128:(tg + 1) * 128, :], in_=kvtok[:, 0, :]),
                              r=["kvtok"], w=[("k_out", sample, tg)])
                        P.dma("pool", lambda e, tg=tg, vo=vo: e.dma_start(out=vo[tg * 128:(tg + 1) * 128, :], in_=kvtok[:, 1, :]),
                              r=["kvtok"], w=[("v_out", sample, tg)])
                        bank = ps[6]
                        for kc in range(8):
                            P.op("pe", lambda e, tt=tt, kc=kc, bank=bank: e.matmul(
                                out=bank[:, 0:H], lhsT=xnT[:, kc, tt * 128:(tt + 1) * 128], rhs=wkv[:, kc, 2 * D:2 * D + H],
                                start=(kc == 0), stop=(kc == 7)), r=["awkv", "axnT"], w=[("ps", id(bank))])
                        P.op("dve", lambda e, bank=bank: e.tensor_tensor(out=lfe[:], in0=bank[:, 0:H], in1=bfB[:, 0:H], op=ALU.add),
                             r=[("ps", id(bank)), "bfB"], w=["lfe"])
                        P.op("act", lambda e: e.activation(out=lfe[:], in_=lfe[:], func=AF.Exp, scale=-1.0), r=["lfe"], w=["lfe"])
                        P.op("act", lambda e: e.activation(out=lfe[:], in_=lfe[:], func=AF.Ln, bias=1.0), r=["lfe"], w=["lfe"])
                        P.op("dve", lambda e: e.tensor_scalar(out=lft[:], in0=lfe[:], scalar1=-1.0, scalar2=None, op0=ALU.mult),
                             r=["lfe"], w=["lft"])
                        P.dma("pool", lambda e, tg=tg, lo=lo: e.dma_start(out=lo[tg * 128:(tg + 1) * 128, :], in_=lft[:]),
                              r=["lft"], w=[("lf_out", sample, tg)])
                        if sample:
                            if tt == 0:
                                P.op("pool", lambda e: e.tensor_copy(
                                    out=vn1[:, :, 0:128], in_=vbf[:, 0, :].rearrange("p (h d) -> p h d", h=8)),
                                    r=["vbf"], w=["vn1"])
                                bank2 = ps[7]
                                P.op("pe", lambda e, bank2=bank2: e.matmul(out=bank2[:, 0:H], lhsT=scst[:, 2, 0:128], rhs=lft[:],
                                                                           start=True, stop=True),
                                     r=["scst", "lft"], w=[("ps", id(bank2))])
                                P.op("dve", lambda e, bank2=bank2: e.tensor_scalar(out=negE[:], in0=bank2[:, 0:H], scalar1=-1.0,
                                                                                  scalar2=None, op0=ALU.mult),
                                     r=[("ps", id(bank2))], w=["negE"])
                            continue
                        bank = ps[7]
                        P.op("pe", lambda e, bank=bank: e.matmul(out=bank[:, 0:H], lhsT=tri, rhs=lft[:], start=True, stop=False),
                             r=["cst", "lft"], w=[("ps", id(bank))])
                        P.op("pe", lambda e, bank=bank: e.matmul(out=bank[:, 0:H], lhsT=ones, rhs=lacc[:], start=False, stop=True),
                             r=["cst", "lacc"], w=[("ps", id(bank))])
                        P.op("dve", lambda e, tg=tg, bank=bank: e.tensor_copy(out=ck[:, tg, :], in_=bank[:, 0:H]),
                             r=[("ps", id(bank))], w=["ck"])
                        P.op("dve", lambda e: e.tensor_tensor(out=lacc[:], in0=lacc[:], in1=lft[:], op=ALU.add),
                             r=["lacc", "lft"], w=["lacc"])
                    for tt in range(2 if not sample else 0):
                        P.dma("pool", lambda e, ch=ch, tt=tt: e.dma_start(
                            out=vs[:, :, ch * 2 + tt, :].rearrange("h p d -> p h d"),
                            in_=vbf[:, tt, :].rearrange("p (h d) -> p h d", h=8)), r=["vbf"], w=[("vs", ch, tt)])

            for ci, ch in enumerate(chunks):
                if ci == 0:
                    front(ch)
                mid(ch)
                if ci + 1 < len(chunks):
                    front(chunks[ci + 1])
                tail(ch)
            if do_prompt and not do_sample:
                P.dma("pool", lambda e: e.dma_start(out=rnn_out[l], in_=hst[:]), r=["hst"], w=[("rnn_out", l)])
                P.dma("pool", lambda e: e.dma_start(out=conv_out[l], in_=ubuf[:, :, 0:3]), r=[("ubuf", o_) for o_ in range(8)], w=[("conv_out", l)])
            if l == 1 and do_prompt:
                P.dma("pool", lambda e: e.dma_start(out=cs.rearrange("(t p) h -> p t h", p=128), in_=ck[:]), r=["ck"], w=["cs"])
            P.drain("sp")
            P.flush()

    oneb = sb("oneb", [128, 1])
    P.op("dve", lambda e: e.memset(oneb[:], 1.0), w=["oneb"])

    a_pass(0)
    a_pass(1)

    def b_phase():
        with ExitStack() as ph:
            xo = sb("bxo", [128, 16, D], F32, ph)
            RB = sb("bRB", [128, 16, H], F32, ph)
            QT = sb("bQT", [128, 8, 2048], BF16, ph)
            sgT = sb("bsgT", [128, 8, 2048], BF16, ph)
            gB = sb("bgB", [128, D], F32, ph)
            ytmp = sb("bytmp", [128, D], F32, ph)
            for j in range(16):
                P.dma("pool", lambda e, j=j: e.indirect_dma_start(
                    out=xo[:, j, :], out_offset=None, in_=x2s,
                    in_offset=bass.IndirectOffsetOnAxis(ap=idx[:, j:j + 1], axis=0)), r=["idx"], w=[("bxo", j)])
                P.dma("pool", lambda e, j=j: e.indirect_dma_start(
                    out=RB[:, j, :], out_offset=None, in_=cs,
                    in_offset=bass.IndirectOffsetOnAxis(ap=idx[:, 16 + j:17 + j], axis=0)), r=["idx"], w=["bRB"])
            for l in range(2):
                with ExitStack() as p1:
                    w_in = sb("bw_in", [128, 8, 2 * D], BF16, p1)
                    xh = sb("bxh", [128, 4, D], BF16, p1)
                    xnT = sb("bxnT", [128, 8, 512], BF16, p1)
                    thg = sb("bthg", [128, 2, 512], F32, p1)
                    P.dma("pool", lambda e: e.dma_start(out=w_in[:], in_=b_w_in[l].rearrange("(kc p) n -> p kc n", p=128)),
                          w=["bw_in"])
                    P.dma("sp", lambda e: e.dma_start(out=gB[:], in_=rowv_d[2 + l:3 + l, :].partition_broadcast(128)), w=["bgB"])
                    for grp in range(4):
                        x_ap = lambda tt, grp=grp: xo[:, grp * 4 + tt, :]
                        xkey = lambda tt, grp=grp: ("bxo", grp * 4 + tt)
                        rms_to_featmajor(x_ap, xkey, 4, lambda kc: pv("bpre", l, kc), xh, xnT, [ps[0], ps[1], ps[2], ps[3]], "b")
                        for oc in range(16):
                            bank = ps[4 + oc % 2]
                            for kc in range(8):
                                P.op("pe", lambda e, oc=oc, kc=kc, bank=bank: e.matmul(
                                    out=bank[:], lhsT=w_in[:, kc, oc * 128:(oc + 1) * 128], rhs=xnT[:, kc, :],
                                    start=(kc == 0), stop=(kc == 7)), r=["bw_in", "bxnT"], w=[("ps", id(bank))])
                            if oc < 8:
                                P.op("act", lambda e, oc=oc, bank=bank, grp=grp: e.activation(
                                    out=QT[:, oc, grp * 512:(grp + 1) * 512], in_=bank[:], func=AF.Copy),
                                    r=[("ps", id(bank))], w=["bQT"])
                            else:
                                g = oc - 8
                                ts_ = g % 2
                                P.op("act", lambda e, bank=bank, ts_=ts_: e.activation(out=thg[:, ts_, :], in_=bank[:], func=AF.Tanh,
                                                                                       scale=0.5),
                                     r=[("ps", id(bank))], w=[("bthg", ts_)])
                                P.op("dve", lambda e, bank=bank, ts_=ts_: e.scalar_tensor_tensor(
                                    out=thg[:, ts_, :], in0=thg[:, ts_, :], scalar=1.0, in1=bank[:], op0=ALU.add, op1=ALU.mult),
                                    r=[("ps", id(bank)), ("bthg", ts_)], w=[("bthg", ts_)])
                                P.op("pool", lambda e, g=g, grp=grp, ts_=ts_: e.tensor_scalar(
                                    out=sgT[:, g, grp * 512:(grp + 1) * 512], in0=thg[:, ts_, :], scalar1=0.5, scalar2=0.0,
                                    op0=ALU.mult, op1=ALU.add), r=[("bthg", ts_)], w=["bsgT"])
                    P.drain("sp")
                    P.flush()
                with ExitStack() as p2:
                    kT = sb("bkT", [128, 2, T], BF16, p2)
                    vv = sb("bvv", [128, 2, NT, 2, 65], BF16, p2)
                    bias = sb("bbias", [128, 2, NT, H], F32, p2)
                    sm = sb("bsm", [128, 2, 128], F32, p2)
                    pT = sb("bpT", [128, 4, 128], BF16, p2)
                    on = sb("bon", [128, 2, 128], BF16, p2)
                    rden = sb("brden", [128, 2, 2], F32, p2)
                    P.op("pool", lambda e: e.memset(vv[:, :, :, :, 64:65], 1.0), w=[("bvv", 0), ("bvv", 1)])
                    for hp in range(8):
                        sl = hp % 2
                        P.dma("sp", lambda e, hp=hp, sl=sl: e.dma_start(out=kT[:, sl, :], in_=kTs[hp]),
                              w=[("bkT", sl)])
                        P.dma("sp", lambda e, hp=hp, sl=sl: e.dma_start(
                            out=vv[:, sl, :, :, 0:64], in_=vs[hp].rearrange("p t (e d) -> p t e d", e=2)),
                            w=[("bvv", sl)])
                        for j in range(16):
                            nk = 2 * j + 2
                            bs = j % 2
                            P.op("dve", lambda e, j=j, nk=nk, bs=bs: e.tensor_tensor(
                                out=bias[:, bs, 0:nk, :], in0=RB[:, j:j + 1, :].to_broadcast([128, nk, H]), in1=ck[:, 0:nk, :],
                                op=ALU.subtract), r=["bRB", "ck"], w=[("bbias", bs)])
                            obs = [ps[6], ps[7]]
                            items = [(ee, kt) for kt in range(nk) for ee in range(2)]

                            def emit_s(n, j=j, hp=hp, sl=sl, nk=nk, bs=bs):
                                ee, kt = items[n]
                                h = hp * 2 + ee
                                sbk = ps[n % 4]
                                pslot = n % 4
                                P.op("pe", lambda e: e.matmul(
                                    out=sbk[:, 0:128], lhsT=kT[ee * 64:(ee + 1) * 64, sl, kt * 128:(kt + 1) * 128],
                                    rhs=QT[ee * 64:(ee + 1) * 64, hp, j * 128:(j + 1) * 128], start=True, stop=True),
                                    r=[("bkT", sl), "bQT"], w=[("ps", id(sbk))])
                                if kt >= nk - 2:
                                    mk = mask_a if kt == nk - 2 else mask_b
                                    ms = kt - (nk - 2)
                                    P.op("dve", lambda e: e.scalar_tensor_tensor(
                                        out=sm[:, ms, :], in0=sbk[:, 0:128], scalar=0.125, in1=mk, op0=ALU.mult, op1=ALU.add),
                                        r=[("ps", id(sbk)), "cst"], w=[("bsm", ms)])
                                    P.op("act", lambda e: e.activation(
                                        out=pT[:, pslot, :], in_=sm[:, ms, :], func=AF.Exp, bias=bias[:, bs, kt, h:h + 1]),
                                        r=[("bsm", ms), ("bbias", bs)], w=[("bpT", pslot)])
                                else:
                                    P.op("act", lambda e: e.activation(
                                        out=pT[:, pslot, :], in_=sbk[:, 0:128], func=AF.Exp, scale=0.125,
                                        bias=bias[:, bs, kt, h:h + 1]),
                                        r=[("ps", id(sbk)), ("bbias", bs)], w=[("bpT", pslot)])

                            def emit_pv(n, sl=sl, nk=nk, obs=obs):
                                ee, kt = items[n]
                                pslot = n % 4
                                ob = obs[ee]
                                P.op("pe", lambda e: e.matmul(
                                    out=ob[:, 0:65], lhsT=pT[:, pslot, :], rhs=vv[:, sl, kt, ee, :],
                                    start=(kt == 0), stop=(kt == nk - 1)),
                                    r=[("bpT", pslot), ("bvv", sl)], w=[("ps", id(ob))])
                            for k in range(nk + 1):
                                if k < nk:
                                    emit_s(2 * k)
                                    emit_s(2 * k + 1)
                                if k >= 1:
                                    emit_pv(2 * k - 2)
                                    emit_pv(2 * k - 1)
                            osl = j % 2
                            for ee in range(2):
                                P.op("dve", lambda e, ob=obs[ee], osl=osl, ee=ee: e.reciprocal(
                                    out=rden[:, osl, ee:ee + 1], in_=ob[:, 64:65]),
                                    r=[("ps", id(obs[ee]))], w=[("brden", osl, ee)])
                                P.op("dve", lambda e, ob=obs[ee], osl=osl, ee=ee: e.tensor_scalar(
                                    out=on[:, osl, ee * 64:(ee + 1) * 64], in0=ob[:, 0:64],
                                    scalar1=rden[:, osl, ee:ee + 1], scalar2=None, op0=ALU.mult),
                                    r=[("ps", id(obs[ee])), ("brden", osl, ee)], w=[("bon", osl)])
                            tb = ps[4 + j % 2]
                            tbv = tb[:].bitcast(BF16)
                            P.op("pe", lambda e, osl=osl, tbv=tbv: e.transpose(out=tbv[:, 0:128], in_=on[:, osl, :], identity=identb[:]),
                                 r=[("bon", osl), "identb"], w=[("ps", id(tb))])
                            P.op("dve", lambda e, tbv=tbv, hp=hp, j=j: e.tensor_tensor(
                                out=sgT[:, hp, j * 128:(j + 1) * 128], in0=tbv[:, 0:128], in1=sgT[:, hp, j * 128:(j + 1) * 128],
                                op=ALU.mult), r=[("ps", id(tb)), "bsgT"], w=["bsgT"])
                    P.drain("sp")
                    P.flush()
                with ExitStack() as p3:
                    w_out = sb("bw_out", [128, 8, D], BF16, p3)
                    P.dma("pool", lambda e: e.dma_start(out=w_out[:], in_=b_w_out[l].rearrange("(kc p) n -> p kc n", p=128)),
                          w=["bw_out"])
                    for j in range(16):
                        yb = [ps[6], ps[7]]
                        for fc in range(2):
                            for kc in range(8):
                                P.op("pe", lambda e, j=j, fc=fc, kc=kc, yb=yb: e.matmul(
                                    out=yb[fc][:], lhsT=sgT[:, kc, j * 128:(j + 1) * 128], rhs=w_out[:, kc, fc * 512:(fc + 1) * 512],
                                    start=(kc == 0), stop=(kc == 7)), r=["bsgT", "bw_out"], w=[("ps", id(yb[fc]))])
                        post_norm_residual(yb, xo[:, j, :], ("bxo", j), gB, "bgB", ytmp)
                        if l == 1:
                            P.dma("sp", lambda e, j=j: e.dma_start(out=y_own[j], in_=xo[:, j, :]), r=[("bxo", j)],
                                  w=[("y_own", j)])
                    P.drain("sp")
                    P.flush()

    def s_phase():
        RG = [list(range(8))]
        with ExitStack() as ph:
            xs_t = sb("sxs", [128, 2, D], F32, ph)
            Dall = sb("sDall", [128, NS, 64, 2], F32, ph)
            P.op("dve", lambda e: e.memset(ytmp[:], 0.0), w=["ytmp"])
            for i_ in range(2):
                for t_ in range(2):
                    P.dma("sp", lambda e, i_=i_, t_=t_: e.dma_start(out=o_scr[i_][t_ * 128:(t_ + 1) * 128, :], in_=ytmp[:]),
                          r=["ytmp"], w=[("o_scr", i_)])
            gB = sb("sgB", [128, D], F32, ph)
            ytmp = sb("sytmp", [128, D], F32, ph)
            onesf = sb("sonesf", [128, 64], F32, ph)
            P.dma("sp", lambda e: e.dma_start(out=xs_t[:], in_=xs2.rearrange("(t p) f -> p t f", p=128)), w=["sxs"])
            P.op("dve", lambda e: e.memset(onesf[:], 1.0), w=["sonesf"])
            with ExitStack() as p0:
                Lg = sb("sLg", [128, 2, 2, 64], F32, p0)
                pre = sb("spre", [128, 2, 2, 64], F32, p0)
                LT = sb("sLT", [128, 2, 2], F32, p0)
                TT = sb("sTT", [128, 2, 2], F32, p0)
                zc = sb("szc", [128, 1], F32, p0)
                P.op("dve", lambda e: e.memset(zc[:], 0.0), w=["szc"])
                for sq in range(NS):
                    sl = sq % 2
                    P.dma("pool", lambda e, sq=sq, sl=sl: e.indirect_dma_start(
                        out=Lg[:, sl, :, :].rearrange("p a b -> p (a b)"), out_offset=None, in_=clf_full,
                        in_offset=bass.IndirectOffsetOnAxis(ap=idxL[:, sq // 8, sq % 8:sq % 8 + 1], axis=0)), r=["idxL"], w=[("sLg", sl)])
                    for ee in range(2):
                        P.op("dve", lambda e, sl=sl, ee=ee: e.tensor_tensor_scan(
                            out=pre[:, sl, ee, :], data0=onesf[:], data1=Lg[:, sl, ee, :], initial=zc[:, 0:1],
                            op0=ALU.mult, op1=ALU.add), r=[("sLg", sl), "sonesf", "szc"], w=[("spre", sl)])
                    bank = ps[sq % 2]
                    P.op("dve", lambda e, sl=sl: e.tensor_copy(out=TT[:, sl, :], in_=pre[:, sl, :, 63]),
                         r=[("spre", sl)], w=[("sTT", sl)])
                    P.op("pe", lambda e, sl=sl, bank=bank: e.matmul(out=bank[:, 0:2], lhsT=scst[:, 0, 0:128], rhs=TT[:, sl, :],
                                                                    start=True, stop=True),
                         r=["scst", ("sTT", sl)], w=[("ps", id(bank))])
                    P.op("dve", lambda e, sl=sl, bank=bank: e.tensor_tensor(out=LT[:, sl, :], in0=bank[:, 0:2], in1=TT[:, sl, :],
                                                                            op=ALU.add),
                         r=[("ps", id(bank)), ("sTT", sl)], w=[("sLT", sl)])
                    for ee in range(2):
                        P.op("dve", lambda e, sl=sl, ee=ee, sq=sq: e.tensor_scalar(
                            out=Dall[:, sq, :, ee], in0=pre[:, sl, ee, :], scalar1=-1.0, scalar2=LT[:, sl, ee:ee + 1],
                            op0=ALU.mult, op1=ALU.add), r=[("spre", sl), ("sLT", sl)], w=["sDall"])
                if debug:
                    P.dma("sp", lambda e: e.dma_start(out=dbg[:, 2836:2964], in_=Lg[:, 1].rearrange("p a b -> p (a b)")),
                          r=[("sLg", 1)], w=["dbg5"])
                    P.dma("sp", lambda e: e.dma_start(out=dbg[:, 3092:3220], in_=pre[:, 1].rearrange("p a b -> p (a b)")),
                          r=[("spre", 1)], w=["dbg6"])
                    P.dma("sp", lambda e: e.dma_start(out=dbg[:, 3348:3352], in_=LT[:].rearrange("p a b -> p (a b)")),
                          r=[("sLT", 0), ("sLT", 1)], w=["dbg7"])
                    P.dma("sp", lambda e: e.dma_start(out=dbg[:, 3352:3356], in_=TT[:].rearrange("p a b -> p (a b)")),
                          r=[("sTT", 0), ("sTT", 1)], w=["dbg8"])
                    P.dma("sp", lambda e: e.dma_start(out=dbg[:, 3356:3388].bitcast(I32), in_=idxL[:].rearrange("p a b -> p (a b)")),
                          r=["idxL"], w=["dbg9"])
                P.drain("sp")
                P.flush()
            for l in range(2):
                with ExitStack() as p1:
                    wg = sb("swg", [128, 8, D], BF16, p1)
                    wq = sb("swq", [128, 8, D], BF16, p1)
                    w_out = sb("sw_out", [128, 8, D], BF16, p1)
                    xh = sb("sxh", [128, 2, D], BF16, p1)
                    xnT = sb("sxnT", [128, 8, 256], BF16, p1)
                    sg = sb("ssg", [128, 2, D], BF16, p1)
                    thg = sb("sthg", [128, 512], F32, p1)
                    Qbd = sb("sQbd", [128, NS, 16], BF16, p1)
                    Kst = sb("sKst", [128, 2, 32, 128], F32, p1)
                    Vst = sb("sVst", [128, 1, 32, 128], F32, p1)
                    Vbf = sb("sVbf", [128, 2, 32, 129], BF16, p1)
                    KT = sb("sKT", [128, 2, 32, 128], BF16, p1)
                    Ssb = sb("sSsb", [128, 2, 512], F32, p1)
                    Psb = sb("sPsb", [128, 2, 512], BF16, p1)
                    Sn = sb("sSn", [128, 16], F32, p1)
                    Pn = sb("sPn", [128, 16], BF16, p1)
                    Osb = sb("sOsb", [16, 2, 130], F32, p1)
                    ofull = Kst[:, 0, 0:16, :].rearrange("p (a b) c -> p a (b c)", a=2)
                    mtok = sb("smtok", [128, 2, D], BF16, p1)
                    mT = sb("smT", [128, 8, 256], BF16, p1)
                    P.dma("pool", lambda e: e.dma_start(out=wg[:], in_=b_w_in[l][:, D:2 * D].rearrange("(kc p) n -> p kc n", p=128)),
                          w=["swg"])
                    P.dma("pool", lambda e: e.dma_start(out=wq[:], in_=b_w_in[l][:, 0:D].rearrange("(kc p) n -> p kc n", p=128)),
                          w=["swq"])
                    P.dma("pool", lambda e: e.dma_start(out=w_out[:], in_=b_w_out[l].rearrange("(kc p) n -> p kc n", p=128)),
                          w=["sw_out"])
                    P.dma("sp", lambda e: e.dma_start(out=gB[:], in_=rowv_d[2 + l:3 + l, :].partition_broadcast(128)), w=["sgB"])
                    P.op("pool", lambda e: e.memset(Vbf[:, :, :, 128:129], 1.0), w=[("sVbf", 0), ("sVbf", 1)])
                    P.op("pool", lambda e: e.memset(Qbd[:], 0.0), w=["sQbd"])
                    x_ap = lambda tt: xs_t[:, tt, :]
                    xkey = lambda tt: "sxs"
                    rms_to_featmajor(x_ap, xkey, 2, lambda kc: pv("bpre", l, kc), xh, xnT, [ps[0], ps[1]], "s")
                    Qv = Qbd[:].rearrange("p (s h) c -> p s h c", h=8)
                    for hp in range(8):
                        bank = ps[2 + hp % 2]
                        for kc in range(8):
                            P.op("pe", lambda e, kc=kc, hp=hp, bank=bank: e.matmul(
                                out=bank[:, 0:256], lhsT=wq[:, kc, hp * 128:(hp + 1) * 128], rhs=xnT[:, kc, :],
                                start=(kc == 0), stop=(kc == 7)), r=["swq", "sxnT"], w=[("ps", id(bank))])
                        for ee in range(2):
                            P.op("dve", lambda e, ee=ee, hp=hp, bank=bank: e.tensor_scalar(
                                out=Qv[ee * 64:(ee + 1) * 64, :, hp, ee * 8:(ee + 1) * 8],
                                in0=bank[ee * 64:(ee + 1) * 64, 0:32].rearrange("p (s t) -> p s t", t=8),
                                scalar1=0.125, scalar2=None, op0=ALU.mult), r=[("ps", id(bank))], w=["sQbd"])
                    for tt in range(2):
                        for fc in range(2):
                            bank = ps[4 + fc]
                            for kc in range(8):
                                P.op("pe", lambda e, kc=kc, tt=tt, fc=fc, bank=bank: e.matmul(
                                    out=bank[:], lhsT=xnT[:, kc, tt * 128:(tt + 1) * 128], rhs=wg[:, kc, fc * 512:(fc + 1) * 512],
                                    start=(kc == 0), stop=(kc == 7)), r=["swg", "sxnT"], w=[("ps", id(bank))])
                            P.op("act", lambda e, bank=bank: e.activation(out=thg[:], in_=bank[:], func=AF.Tanh, scale=0.5),
                                 r=[("ps", id(bank))], w=["sthg"])
                            P.op("dve", lambda e, bank=bank: e.scalar_tensor_tensor(
                                out=thg[:], in0=thg[:], scalar=1.0, in1=bank[:], op0=ALU.add, op1=ALU.mult),
                                r=[("ps", id(bank)), "sthg"], w=["sthg"])
                            P.op("pool", lambda e, tt=tt, fc=fc: e.tensor_scalar(
                                out=sg[:, tt, fc * 512:(fc + 1) * 512], in0=thg[:], scalar1=0.5, scalar2=0.0,
                                op0=ALU.mult, op1=ALU.add), r=["sthg"], w=["ssg"])
                    for sq in range(NS):
                        ob = ps[6 + sq % 2]
                        osl = sq % 2
                        for half in range(2):
                            bs = (sq * 2 + half) % 2
                            P.dma("pool", lambda e, sq=sq, half=half, bs=bs: e.indirect_dma_start(
                                out=Kst[:, bs, :, :].rearrange("p a b -> p (a b)"), out_offset=None, in_=ck_full,
                                in_offset=bass.IndirectOffsetOnAxis(ap=idxK[:, half, sq // 8, sq % 8:sq % 8 + 1], axis=0)),
                                r=["idxK"], w=[("sKst", bs)])
                            P.dma("pool", lambda e, sq=sq, half=half, bs=bs: e.indirect_dma_start(
                                out=Vst[:, 0, :, :].rearrange("p a b -> p (a b)"), out_offset=None, in_=cv_full,
                                in_offset=bass.IndirectOffsetOnAxis(ap=idxK[:, half, sq // 8, sq % 8:sq % 8 + 1], axis=0)),
                                r=["idxK"], w=[("sVst", 0)])
                            P.op("act", lambda e, bs=bs: e.activation(out=Vbf[:, bs, :, 0:128], in_=Vst[:, 0, :, :], func=AF.Copy),
                                 r=[("sVst", 0)], w=[("sVbf", bs)])
                            for g4 in range(8):
                                tb = ps[g4 % 2]
                                for k4 in range(4):
                                    t = g4 * 4 + k4
                                    P.op("pe", lambda e, bs=bs, t=t, k4=k4, tb=tb: e.transpose(
                                        out=tb[:, k4 * 128:(k4 + 1) * 128], in_=Kst[:, bs, t, :], identity=ident),
                                        r=[("sKst", bs), "cst"], w=[("ps", id(tb))])
                                if g4 % 2 == 0:
                                    P.op("act", lambda e, bs=bs, g4=g4, tb=tb: e.activation(
                                        out=KT[:, bs, g4 * 4:(g4 + 1) * 4, :], in_=tb[:].rearrange("p (a b) -> p a b", b=128),
                                        func=AF.Copy), r=[("ps", id(tb))], w=[("sKT", bs)])
                                else:
                                    P.op("dve", lambda e, bs=bs, g4=g4, tb=tb: e.tensor_copy(
                                        out=KT[:, bs, g4 * 4:(g4 + 1) * 4, :], in_=tb[:].rearrange("p (a b) -> p a b", b=128)),
                                        r=[("ps", id(tb))], w=[("sKT", bs)])
                            sbk = ps[2 + bs]
                            for t in range(32):
                                P.op("pe", lambda e, bs=bs, t=t, sq=sq, sbk=sbk: e.matmul(
                                    out=sbk[:, t * 16:(t + 1) * 16], lhsT=KT[:, bs, t, :], rhs=Qbd[:, sq, :], start=True, stop=True),
                                    r=[("sKT", bs), "sQbd"], w=[("ps", id(sbk))])
                            P.op("dve", lambda e, bs=bs, sq=sq, half=half, sbk=sbk: e.tensor_tensor(
                                out=Ssb[:, bs, :].rearrange("p (a q) -> p a q", q=8), in0=sbk[:].rearrange("p (a q) -> p a q", q=8),
                                in1=Dall[:, sq, half * 32:(half + 1) * 32, :].rearrange("p t e -> p (t e)").unsqueeze(2).to_broadcast([128, 64, 8]),
                                op=ALU.add), r=[("ps", id(sbk)), "sDall"], w=[("sSsb", bs)])
                            P.op("act", lambda e, bs=bs: e.activation(out=Psb[:, bs, :], in_=Ssb[:, bs, :], func=AF.Exp),
                                 r=[("sSsb", bs)], w=[("sPsb", bs)])
                            for t in range(32):
                                P.op("pe", lambda e, bs=bs, t=t, half=half, ob=ob: e.matmul(
                                    out=ob[0:16, 0:129], lhsT=Psb[:, bs, t * 16:(t + 1) * 16], rhs=Vbf[:, bs, t, :],
                                    start=(half == 0 and t == 0), stop=False),
                                    r=[("sPsb", bs), ("sVbf", bs)], w=[("ps", id(ob))])
                        sl_, hp_ = sq // 8, sq % 8
                        nb = ps[4]
                        P.op("pe", lambda e, hp_=hp_, sq=sq, nb=nb: e.matmul(out=nb[:, 0:16], lhsT=kTn[:, hp_, 0:128],
                                                                            rhs=Qbd[:, sq, :], start=True, stop=True),
                             r=["kTn", "sQbd"], w=[("ps", id(nb))])
                        P.op("dve", lambda e, sl_=sl_, nb=nb: e.tensor_tensor(
                            out=Sn[:], in0=nb[:, 0:16], in1=scst[:, 1, sl_ * 16:(sl_ + 1) * 16], op=ALU.add),
                            r=[("ps", id(nb)), "scst"], w=["sSn"])
                        P.op("dve", lambda e, hp_=hp_: e.tensor_tensor(
                            out=Sn[:].rearrange("p (e q) -> p e q", q=8), in0=Sn[:].rearrange("p (e q) -> p e q", q=8),
                            in1=negE[:, 2 * hp_:2 * hp_ + 2].unsqueeze(2).to_broadcast([128, 2, 8]), op=ALU.add),
                            r=["sSn", "negE"], w=["sSn"])
                        P.op("act", lambda e: e.activation(out=Pn[:], in_=Sn[:], func=AF.Exp), r=["sSn"], w=["sPn"])
                        P.op("pe", lambda e, hp_=hp_, ob=ob: e.matmul(out=ob[0:16, 0:129], lhsT=Pn[:], rhs=vn1[:, hp_, :],
                                                                      start=False, stop=True),
                             r=["sPn", "vn1"], w=[("ps", id(ob))])
                        P.op("dve", lambda e, ob=ob, osl=osl: e.reciprocal(out=Osb[:, osl, 129:130], in_=ob[0:16, 128:129]),
                             r=[("ps", id(ob))], w=[("sOsb", osl)])
                        P.op("dve", lambda e, ob=ob, osl=osl: e.tensor_scalar(
                            out=Osb[:, osl, 0:128], in0=ob[0:16, 0:128], scalar1=Osb[:, osl, 129:130], scalar2=None, op0=ALU.mult),
                            r=[("ps", id(ob)), ("sOsb", osl)], w=[("sOsb", osl)])
                        for ee in range(2):
                            P.dma("sp", lambda e, sq=sq, ee=ee, osl=osl: e.dma_start(
                                out=o_scr[l][(sq // 8) * 8:(sq // 8 + 1) * 8, (sq % 8) * 128 + ee * 64:(sq % 8) * 128 + (ee + 1) * 64],
                                in_=Osb[ee * 8:(ee + 1) * 8, osl, ee * 64:(ee + 1) * 64]),
                                r=[("sOsb", osl), ("o_scr", l)], w=[("o_scrw", l)])
                    P.dma("sp", lambda e: e.dma_start(out=ofull, in_=o_scr[l].rearrange("(t p) f -> p t f", p=128)),
                          r=[("o_scrw", l), ("o_scr", l)], w=[("sKst", 0)])
                    if debug and l == 0:
                        P.dma("sp", lambda e: e.dma_start(out=dbg[:, 0:512], in_=Dall[:, 0:4, :, :].rearrange("p a b c -> p (a b c)")),
                              r=["sDall"], w=["dbg0"])
                        P.dma("sp", lambda e: e.dma_start(out=dbg[:, 512:1536], in_=ofull[:, 0, :]), r=[("sKst", 0)], w=["dbg1"])
                        P.dma("sp", lambda e: e.dma_start(out=dbg[:, 1536:2560], in_=Ssb[:].rearrange("p a b -> p (a b)")),
                              r=[("sSsb", 0), ("sSsb", 1)], w=["dbg2"])
                        P.dma("sp", lambda e: e.dma_start(out=dbg[0:16, 2560:2816].rearrange("p (a b) -> p a b", a=2), in_=Osb[:, :, 0:128]),
                              r=[("sOsb", 0), ("sOsb", 1)], w=["dbg3"])
                        P.dma("sp", lambda e: e.dma_start(out=dbg[:, 2820:2836], in_=Sn[:]), r=["sSn"], w=["dbg4"])
                    for tt in range(2):
                        P.op("dve", lambda e, tt=tt: e.tensor_tensor(out=mtok[:, tt, :], in0=ofull[:, tt, :], in1=sg[:, tt, :],
                                                                     op=ALU.mult), r=[("sKst", 0), "ssg"], w=["smtok"])
                    for bi in range(2):
                        bank = ps[bi]
                        bv = bank[:].bitcast(BF16)
                        for kk in range(4):
                            kc = bi * 4 + kk
                            for tt in range(2):
                                P.op("pe", lambda e, kc=kc, kk=kk, tt=tt, bv=bv: e.transpose(
                                    out=bv[:, kk * 256 + tt * 128: kk * 256 + (tt + 1) * 128],
                                    in_=mtok[:, tt, kc * 128:(kc + 1) * 128], identity=identb[:]),
                                    r=["smtok", "identb"], w=[("ps", id(bank))])
                        P.op("dve", lambda e, bi=bi, bv=bv: e.tensor_copy(
                            out=mT[:, bi * 4:(bi + 1) * 4, :], in_=bv[:].rearrange("p (a b) -> p a b", b=256)),
                            r=[("ps", id(bank))], w=["smT"])
                    for tt in range(2):
                        yb = [ps[6], ps[7]]
                        for fc in range(2):
                            for kc in range(8):
                                P.op("pe", lambda e, tt=tt, fc=fc, kc=kc, yb=yb: e.matmul(
                                    out=yb[fc][:], lhsT=mT[:, kc, tt * 128:(tt + 1) * 128], rhs=w_out[:, kc, fc * 512:(fc + 1) * 512],
                                    start=(kc == 0), stop=(kc == 7)), r=["smT", "sw_out"], w=[("ps", id(yb[fc]))])
                        post_norm_residual(yb, xs_t[:, tt, :], "sxs", gB, "sgB", ytmp)
                    if l == 1:
                        P.dma("sp", lambda e: e.dma_start(out=ys_out.rearrange("(t p) f -> p t f", p=128), in_=xs_t[:]),
                              r=["sxs"], w=["ys_out"])
                    P.drain("sp")
                    P.flush()

    if do_prompt:
        b_phase()
    if do_sample:
        s_phase()

    P.drain("sp")
    P.flush()
    st.close()
    return nc


_NC_CACHE = {}


def kernel(x_prompt, x_sample, cache_k, cache_v, cache_logf, state_rnn, state_conv, page_table,
           a_pre_norm, a_post_norm, a_w_in, a_conv_w, a_conv_b, a_w_ga, a_b_ga, a_w_gx, a_b_gx,
           a_lambda, a_w_out, kv_norm, w_kv, b_f, b_pre_norm, b_post_norm, b_w_in, b_w_out,
           _trace=False, _do_prompt=True, _debug=False):
    f32 = np.float32
    A = lambda v: np.ascontiguousarray(np.asarray(v, f32))
    npool = int(np.asarray(cache_k).shape[0])
    key = ("prog", npool, _do_prompt, _debug)
    if key not in _NC_CACHE:
        _NC_CACHE[key] = build_program(_do_prompt, True, npool=npool, debug=_debug)
    nc = _NC_CACHE[key]

    pvec = np.zeros((128, NPV), f32)
    for l in range(2):
        pvec[:, PV[("apre", l)]:PV[("apre", l)] + 8] = fm(a_pre_norm[l])
        for j in range(4):
            pvec[:, PV[("cw%d" % j, l)]:PV[("cw%d" % j, l)] + 8] = fm(np.asarray(a_conv_w)[l, j])
        pvec[:, PV[("cb", l)]:PV[("cb", l)] + 8] = fm(a_conv_b[l])
        pvec[:, PV[("bga", l)]:PV[("bga", l)] + 8] = fm(a_b_ga[l])
        pvec[:, PV[("bgx", l)]:PV[("bgx", l)] + 8] = fm(a_b_gx[l])
        pvec[:, PV[("lam", l)]:PV[("lam", l)] + 8] = fm(a_lambda[l])
        pvec[:, PV[("bpre", l)]:PV[("bpre", l)] + 8] = fm(b_pre_norm[l])
    pvec[:, PV[("kvn", 0)]:PV[("kvn", 0)] + 8] = fm(kv_norm)
    rowv = np.zeros((5, D), f32)
    rowv[0:2] = A(a_post_norm); rowv[2:4] = A(b_post_norm); rowv[4, 0:H] = A(b_f)

    ii = np.arange(128)
    ident = np.eye(128, dtype=f32)
    tri = (ii[:, None] <= ii[None, :]).astype(f32)
    ones = np.ones((128, 128), f32)
    causal = np.where(ii[:, None] <= ii[None, :], 0.0, NEG).astype(f32)
    full_ok = np.zeros((128, 128), f32)
    full_no = np.full((128, 128), NEG, f32)

    scst = np.zeros((128, 3, 256), f32)
    order = (ii % 64) * 2 + ii // 64
    scst[:, 0, 0:128] = (order[:, None] > order[None, :]).astype(f32)
    scst[:, 0, 128] = ii // 64
    scst[:, 0, 129] = 2 * (ii // 64)
    scst[:, 0, 130] = 2 * (ii // 64) + 1
    ms = np.full((128, 16, 2, 8), NEG, f32)
    for p_ in range(128):
        s_, t_ = p_ // 8, p_ % 8
        ms[p_, s_, :, t_:] = 0.0
    scst[:, 1, :] = ms.reshape(128, 256)
    scst[:, 2, 0:128] = ((ii[:, None] // 8 == ii[None, :] // 8) & (ii[:, None] <= ii[None, :])).astype(f32)

    ck_full = np.ascontiguousarray(
        np.asarray(cache_k, f32).reshape(npool, 128, 8, 128).transpose(2, 0, 1, 3)).reshape(8 * npool * 4, 4096)
    cv_full = np.ascontiguousarray(
        np.asarray(cache_v, f32).reshape(npool, 128, 8, 128).transpose(2, 0, 1, 3)).reshape(8 * npool * 4, 4096)
    clf_full = np.ascontiguousarray(
        np.asarray(cache_logf, f32).reshape(npool, 2, 64, 8, 2).transpose(3, 0, 1, 4, 2)).reshape(8 * npool * 2, 128)
    pt = np.asarray(page_table).astype(np.int32)
    xs_all = A(x_sample)
    srnn = A(state_rnn); sconv = A(state_conv)

    xp_all = A(x_prompt)
    shared = {"a_w_in": A(a_w_in), "a_w_out": A(a_w_out), "a_w_ga": A(a_w_ga), "a_w_gx": A(a_w_gx), "w_kv": A(w_kv),
              "b_w_in": A(b_w_in), "b_w_out": A(b_w_out), "pvec": pvec, "scst": scst, "rowv": rowv,
              "ck_full": ck_full, "cv_full": cv_full, "clf_full": clf_full}
    in_maps = []
    for c in range(8):
        b, p = c // 2, c % 2
        cst = np.zeros((128, 6, 128), f32)
        cst[:, 0] = ident; cst[:, 1] = tri; cst[:, 2] = ones
        cst[:, 3] = causal if p == 0 else full_ok
        cst[:, 4] = full_no if p == 0 else causal
        idx = np.zeros((128, 32), np.int32)
        for j in range(16):
            idx[:, j] = (2 * j + p) * 128 + ii
            idx[:, 16 + j] = (2 * j + p) * 128 + 63
        xs = np.zeros((256, D), f32); xs[0:32] = xs_all[4 * c:4 * c + 4].reshape(32, D)
        st_rnn = np.zeros((2, NS, D), f32); st_rnn[:, 0:4] = srnn[:, 4 * c:4 * c + 4]
        st_conv = np.zeros((2, NS, 3, D), f32); st_conv[:, 0:4] = sconv[:, 4 * c:4 * c + 4]
        ptT2 = np.zeros((128, NS), np.int32)
        ptT2[:, 0:4] = np.tile(pt[4 * c:4 * c + 4].T, (2, 1))
        m = dict(shared)
        m.update({"xp": xp_all[b], "cst": cst, "idx": idx, "xs": xs, "ptT2": ptT2,
                  "st_rnn": np.ascontiguousarray(st_rnn.reshape(2, NS, 8, 128).transpose(0, 3, 2, 1)),
                  "st_conv": np.ascontiguousarray(st_conv.reshape(2, NS, 3, 8, 128).transpose(0, 4, 3, 1, 2))})
        in_maps.append(m)

    res = run_bass_kernel_spmd(nc, in_maps, core_ids=list(range(8)), trace=_trace)
    R = res.results
    y_prompt = np.zeros((4, T, D), f32)
    k_p = np.zeros((4, T, H, 64), f32); v_p = np.zeros((4, T, H, 64), f32); lf_p = np.zeros((4, T, H), f32)
    rnn_p = np.zeros((2, 4, D), f32); conv_p = np.zeros((2, 4, 3, D), f32)
    y_s = np.zeros((NS, 8, D), f32); k_s = np.zeros((NS, 8, H, 64), f32); v_s = np.zeros((NS, 8, H, 64), f32)
    lf_s = np.zeros((NS, 8, H), f32); rnn_s = np.zeros((2, NS, D), f32); conv_s = np.zeros((2, NS, 3, D), f32)
    for c in range(8):
        b, p = c // 2, c % 2
        r = R[c]
        if _do_prompt:
            y_prompt[b].reshape(16, 2, 128, D)[:, p] = np.asarray(r["y_own"])
            if p == 0:
                k_p[b] = np.asarray(r["k_out"]).reshape(T, H, 64)
                v_p[b] = np.asarray(r["v_out"]).reshape(T, H, 64)
                lf_p[b] = np.asarray(r["lf_out"])
                rnn_p[:, b] = np.asarray(r["rnn_out"]).transpose(0, 2, 1).reshape(2, D)
                conv_p[:, b] = np.asarray(r["conv_out"]).transpose(0, 3, 2, 1).reshape(2, 3, D)
        sl = slice(4 * c, 4 * c + 4)
        y_s[sl] = np.asarray(r["ys_out"], f32)[0:32].reshape(4, 8, D)
        k_s[sl] = np.asarray(r["ks_out"], f32)[0:32].reshape(4, 8, H, 64)
        v_s[sl] = np.asarray(r["vs_out"], f32)[0:32].reshape(4, 8, H, 64)
        lf_s[sl] = np.asarray(r["lfs_out"], f32)[0:32].reshape(4, 8, H)
        rnn_s[:, sl] = np.asarray(r["rnns_out"], f32).transpose(0, 3, 2, 1).reshape(2, NS, D)[:, 0:4]
        conv_s[:, sl] = np.asarray(r["convs_out"], f32).transpose(0, 3, 4, 2, 1).reshape(2, NS, 3, D)[:, 0:4]
    if _trace:
        kernel.last_exec_ns = res.exec_time_ns
    if _debug:
        kernel.last_dbg = np.asarray(R[0]["dbg"])
    return (y_prompt, y_s, k_p, v_p, lf_p, rnn_p, conv_p, k_s, v_s, lf_s, rnn_s, conv_s)
```

```python
import numpy as np
from contextlib import ExitStack
import concourse.bass as bass
import concourse.mybir as mybir
from concourse.bass_utils import run_bass_kernel_spmd

F32, BF16, I32 = mybir.dt.float32, mybir.dt.bfloat16, mybir.dt.int32
ALU = mybir.AluOpType
AF = mybir.ActivationFunctionType

D = 1024
T = 4096
NT = T // 128
CH = 256
NCH = T // CH
H = 16
NPOOL = 2560
NPG = 64
NS = 32
EPS = 1e-6
NEG = -30000.0


class Prog:
    ENG = ("pe", "act", "dve", "pool", "sp")
    NDS = 6

    def __init__(self, nc, stack):
        self.nc = nc
        self.stack = stack
        self.streams = {e: [] for e in self.ENG}
        self.cnt = {e: 0 for e in self.ENG}
        self.sem = {}
        self.known = {e: {} for e in self.ENG}
        self.lastw = {}
        self.readers = {}
        self.dsem = {}
        self.dcnt = {}
        self.nsem = 0
        self.semobj = {}
        for q in ("sp", "pool", "act"):
            self.dsem[q] = [self._newsem() for _ in range(self.NDS)]
            self.dcnt[q] = 0

    def _newsem(self):
        s = self.stack.enter_context(self.nc.semaphore("s%d" % self.nsem))
        self.semobj[self.nsem] = s
        self.nsem += 1
        return self.nsem - 1

    def _deps(self, eng, r, w):
        need = {}
        def add(ev):
            sid, val, src = ev
            if src == "pe" and eng == "pe":
                return
            if need.get(sid, 0) < val:
                need[sid] = val
        for k in r:
            if k in self.lastw:
                add(self.lastw[k])
        for k in w:
            if k in self.lastw:
                add(self.lastw[k])
            for ev in self.readers.get(k, {}).values():
                add(ev)
        waits = []
        kn = self.known[eng]
        for sid, val in need.items():
            if kn.get(sid, 0) < val:
                kn[sid] = val
                waits.append((sid, val))
        return waits

    def _commit(self, ev, r, w):
        for k in r:
            self.readers.setdefault(k, {})[ev[0]] = ev
        for k in w:
            self.lastw[k] = ev
            self.readers[k] = {}

    def op(self, eng, fn, r=(), w=()):
        waits = self._deps(eng, r, w)
        if eng not in self.sem or self.cnt[eng] >= 6000:
            self.sem[eng] = self._newsem()
            self.cnt[eng] = 0
        self.cnt[eng] += 1
        ev = (self.sem[eng], self.cnt[eng], eng)
        self.streams[eng].append((waits, fn, (self.sem[eng], 1)))
        self._commit(ev, r, w)

    def dma(self, q, fn, r=(), w=()):
        waits = self._deps(q, r, w)
        i = self.dcnt[q]
        self.dcnt[q] += 1
        sid = self.dsem[q][i % self.NDS]
        prev = 16 * (i // self.NDS)
        if prev > 0 and self.known[q].get(sid, 0) < prev:
            self.known[q][sid] = prev
            waits.append((sid, prev))
        self.streams[q].append((waits, fn, (sid, 16)))
        self._commit((sid, prev + 16, "dma"), r, w)

    def drain(self, eng="sp"):
        waits = []
        for q in self.dsem:
            n = self.dcnt[q]
            for j, sid in enumerate(self.dsem[q]):
                cntj = (n - j + self.NDS - 1) // self.NDS if n > j else 0
                if cntj > 0 and self.known[eng].get(sid, 0) < 16 * cntj:
                    self.known[eng][sid] = 16 * cntj
                    waits.append((sid, 16 * cntj))
        self.streams[eng].append((waits, None, None))

    def flush(self):
        nc = self.nc
        streams = self.streams
        semobj = self.semobj
        self.streams = {e: [] for e in self.ENG}

        def replay(name):
            def f(e):
                for waits, fn, inc in streams[name]:
                    if fn is None:
                        for sid, val in waits:
                            e.wait_ge(semobj[sid], val)
                        continue
                    for sid, val in waits[:-1]:
                        e.wait_ge(semobj[sid], val)
                    inst = fn(e)
                    if waits:
                        inst._wait_ge(semobj[waits[-1][0]], waits[-1][1])
                    inst.then_inc(semobj[inc[0]], inc[1])
            return f
        with nc.Block() as block:
            if streams["pe"]:
                block.tensor(replay("pe"))
            if streams["act"]:
                block.scalar(replay("act"))
            if streams["dve"]:
                block.vector(replay("dve"))
            if streams["pool"]:
                block.gpsimd(replay("pool"))
            if streams["sp"]:
                block.sync(replay("sp"))


def fm(v):
    return np.ascontiguousarray(np.asarray(v, np.float32).reshape(8, 128).T)


PV = {}
def _pv_layout():
    c = 0
    for l in range(2):
        for nm in ("apre", "cw0", "cw1", "cw2", "cw3", "cb", "bga", "bgx", "lam"):
            PV[(nm, l)] = c; c += 8
    PV[("kvn", 0)] = c; c += 8
    for l in range(2):
        PV[("bpre", l)] = c; c += 8
    return c
NPV = _pv_layout()


def build_program(do_prompt=True, do_sample=True, npool=NPOOL, debug=False):
    nc = bass.Bass("TRN2", target_bir_lowering=False)
    dram = {}

    def din(name, shape, dt=F32):
        dram[name] = nc.dram_tensor(name, list(shape), dt, kind="ExternalInput").ap()
        return dram[name]

    def dout(name, shape, dt=F32):
        dram[name] = nc.dram_tensor(name, list(shape), dt, kind="ExternalOutput").ap()
        return dram[name]

    def dscr(name, shape, dt=F32):
        dram[name] = nc.dram_tensor(name, list(shape), dt, kind="Internal").ap()
        return dram[name]

    xp = din("xp", [T, D])
    a_w_in = din("a_w_in", [2, D, 2 * D]); a_w_out = din("a_w_out", [2, D, D])
    a_w_ga = din("a_w_ga", [2, 8, 128, 128]); a_w_gx = din("a_w_gx", [2, 8, 128, 128])
    w_kv = din("w_kv", [D, 2 * D + H])
    b_w_in = din("b_w_in", [2, D, 2 * D]); b_w_out = din("b_w_out", [2, D, D])
    pvec_d = din("pvec", [128, NPV])
    rowv_d = din("rowv", [5, D])
    cst_d = din("cst", [128, 6, 128])
    idx_d = din("idx", [128, 32], I32)
    xs_d = din("xs", [256, D])
    st_rnn_d = din("st_rnn", [2, 128, 8, NS]); st_conv_d = din("st_conv", [2, 128, 8, NS, 3])
    ck_full = din("ck_full", [8 * npool * 4, 4096])
    cv_full = din("cv_full", [8 * npool * 4, 4096])
    clf_full = din("clf_full", [8 * npool * 2, 128])
    ptT2_d = din("ptT2", [128, NS], I32)
    scst_d = din("scst", [128, 3, 256])
    ys_out = dout("ys_out", [256, D]); ks_out = dout("ks_out", [256, D]); vs_out = dout("vs_out", [256, D])
    lfs_out = dout("lfs_out", [256, H])
    rnns_out = dout("rnns_out", [2, 128, 8, NS]); convs_out = dout("convs_out", [2, 128, 8, NS, 3])
    xs1 = dscr("xs1", [256, D]); xs2 = dscr("xs2", [256, D])
    o_scr = [dscr("o_scr%d" % i, [256, D]) for i in range(2)]
    dbg = dout("dbg", [128, 4096]) if debug else None
    y_own = dout("y_own", [16, 128, D])
    k_out = dout("k_out", [T, D]); v_out = dout("v_out", [T, D]); lf_out = dout("lf_out", [T, H])
    rnn_out = dout("rnn_out", [2, 128, 8]); conv_out = dout("conv_out", [2, 128, 8, 3])
    x1s = dscr("x1s", [T, D]); x2s = dscr("x2s", [T, D])
    kTs = dscr("kTs", [8, 128, T], BF16)
    vs = dscr("vs", [8, 128, NT, 128], BF16)
    cs = dscr("cs", [T, H])

    st = ExitStack()
    P = Prog(nc, st)

    _uid = [0]

    def sb(name, shape, dt=F32, stack=None):
        _uid[0] += 1
        return (stack or st).enter_context(nc.sbuf_tensor("t%d_%s" % (_uid[0], name), list(shape), dt))

    ps = [st.enter_context(nc.psum_tensor("ps%d" % i, [128, 512], F32)) for i in range(8)]

    pvec = sb("pvec", [128, NPV])
    cst = sb("cstt", [128, 6, 128])
    identb = sb("identb", [128, 128], BF16)
    idx = sb("idxt", [128, 32], I32)
    cneg = sb("cneg", [128, 2, 8]); cnegh = sb("cnegh", [128, 2, 8]); cneg2 = sb("cneg2", [128, 2, 8])
    bgah = sb("bgah", [128, 2, 8]); bgxh = sb("bgxh", [128, 2, 8])
    bfB = sb("bfB", [128, H + 2])
    scst = sb("scst", [128, 3, 256])
    kTn = sb("kTn", [128, 8, 256], BF16)
    vn1 = sb("vn1", [128, 8, 129], BF16)
    negE = sb("negE", [128, H])
    ptT2 = sb("ptT2", [128, NS], I32)
    idxK = sb("idxK", [128, 2, 4, 8], I32)
    idxL = sb("idxL", [128, 4, 8], I32)
    offs = sb("offs", [128, 3, 8])
    ck = sb("ck", [128, NT, H])
    lacc = sb("lacc", [128, H])
    stat = sb("stat", [128, 16])

    ident = cst[:, 0, :]; tri = cst[:, 1, :]; ones = cst[:, 2, :]
    mask_a = cst[:, 3, :]; mask_b = cst[:, 4, :]

    def pv(nm, l, n=None):
        c = PV[(nm, l)]
        return pvec[:, c:c + 8] if n is None else pvec[:, c + n:c + n + 1]

    P.dma("sp", lambda e: e.dma_start(out=pvec[:], in_=pvec_d), w=["pvec"])
    P.dma("sp", lambda e: e.dma_start(out=cst[:], in_=cst_d), w=["cst"])
    P.dma("sp", lambda e: e.dma_start(out=idx[:], in_=idx_d), w=["idx"])
    P.dma("sp", lambda e: e.dma_start(out=bfB[:], in_=rowv_d[4:5, 0:H + 2].partition_broadcast(128)), w=["bfB"])
    P.dma("sp", lambda e: e.dma_start(out=scst[:], in_=scst_d), w=["scst"])
    P.dma("sp", lambda e: e.dma_start(out=ptT2[:], in_=ptT2_d), w=["ptT2"])
    P.op("pool", lambda e: e.memset(vn1[:, :, 128:129], 1.0), w=["vn1"])
    for hp in range(8):
        for half in range(2):
            P.op("pool", lambda e, half=half, hp=hp: e.tensor_scalar(
                out=offs[:, half, hp:hp + 1], in0=scst[:, 0, 129 + half:130 + half], scalar1=float(hp * npool * 4), scalar2=0.0,
                op0=ALU.add, op1=ALU.add), r=["scst"], w=["offs"])
        P.op("pool", lambda e, hp=hp: e.tensor_scalar(
            out=offs[:, 2, hp:hp + 1], in0=scst[:, 0, 128:129], scalar1=float(hp * npool * 2), scalar2=0.0,
            op0=ALU.add, op1=ALU.add), r=["scst"], w=["offs"])
    for hp in range(8):
        for half in range(2):
            P.op("pool", lambda e, half=half, hp=hp: e.tensor_scalar(
                out=idxK[:, half, :, hp], in0=ptT2[:, 0:4], scalar1=4.0, scalar2=offs[:, half, hp:hp + 1],
                op0=ALU.mult, op1=ALU.add), r=["ptT2", "offs"], w=["idxK"])
        P.op("pool", lambda e, hp=hp: e.tensor_scalar(
            out=idxL[:, :, hp], in0=ptT2[:, 0:4], scalar1=2.0, scalar2=offs[:, 2, hp:hp + 1],
            op0=ALU.mult, op1=ALU.add), r=["ptT2", "offs"], w=["idxL"])
    P.op("dve", lambda e: e.tensor_copy(out=identb[:], in_=ident), r=["cst"], w=["identb"])
    P.op("dve", lambda e: e.memset(lacc[:], 0.0), w=["lacc"])
    for l in range(2):
        P.op("act", lambda e, l=l: e.activation(out=cneg[:, l, :], in_=pv("lam", l), func=AF.Exp, scale=-1.0),
             r=["pvec"], w=["cneg"])
        P.op("act", lambda e, l=l: e.activation(out=cneg[:, l, :], in_=cneg[:, l, :], func=AF.Ln, bias=1.0),
             r=["cneg"], w=["cneg"])
        P.op("dve", lambda e, l=l: e.tensor_scalar(out=cnegh[:, l, :], in0=cneg[:, l, :], scalar1=-4.0, scalar2=None,
                                                   op0=ALU.mult), r=["cneg"], w=["cnegh"])
        P.op("dve", lambda e, l=l: e.tensor_scalar(out=cneg2[:, l, :], in0=cneg[:, l, :], scalar1=-8.0, scalar2=None,
                                                   op0=ALU.mult), r=["cneg"], w=["cneg2"])
        P.op("dve", lambda e, l=l: e.tensor_scalar(out=bgah[:, l, :], in0=pv("bga", l), scalar1=0.5, scalar2=None,
                                                   op0=ALU.mult), r=["pvec"], w=["bgah"])
        P.op("dve", lambda e, l=l: e.tensor_scalar(out=bgxh[:, l, :], in0=pv("bgx", l), scalar1=0.5, scalar2=None,
                                                   op0=ALU.mult), r=["pvec"], w=["bgxh"])

    def rms_to_featmajor(x_ap, xkey, ntt, gain_l, xh, xnT, tpb, pfx):
        for tt in range(ntt):
            c0 = tt * 2
            P.op("act", lambda e, tt=tt, c0=c0: e.activation(out=xh[:, tt, :], in_=x_ap(tt), func=AF.Square,
                                                             accum_out=stat[:, c0:c0 + 1]),
                 r=[xkey(tt)], w=[pfx + "xh", ("stat", c0)])
            P.op("act", lambda e, c0=c0: e.activation(out=stat[:, c0 + 1:c0 + 2], in_=stat[:, c0:c0 + 1], func=AF.Sqrt,
                                                      scale=1.0 / D, bias=epsb[:, 0:1]), r=[("stat", c0), "epsb"], w=[("stat", c0)])
            P.op("dve", lambda e, c0=c0: e.reciprocal(out=stat[:, c0 + 1:c0 + 2], in_=stat[:, c0 + 1:c0 + 2]),
                 r=[("stat", c0)], w=[("stat", c0)])
            P.op("pool", lambda e, tt=tt, c0=c0: e.tensor_scalar(out=xh[:, tt, :], in0=x_ap(tt), scalar1=stat[:, c0 + 1:c0 + 2],
                                                                 scalar2=0.0, op0=ALU.mult, op1=ALU.add),
                 r=[xkey(tt), ("stat", c0)], w=[pfx + "xh"])
        W = ntt * 128
        per = 1024 // W
        for bi in range(8 // per):
            bank = tpb[bi]
            bv = bank[:].bitcast(BF16)
            for kk in range(per):
                kc = bi * per + kk
                for tt in range(ntt):
                    P.op("pe", lambda e, kc=kc, kk=kk, tt=tt, bv=bv: e.transpose(
                        out=bv[:, kk * W + tt * 128: kk * W + (tt + 1) * 128], in_=xh[:, tt, kc * 128:(kc + 1) * 128],
                        identity=identb[:]), r=[pfx + "xh", "identb"], w=[("ps", id(bank))])
            for kk in range(per):
                kc = bi * per + kk
                if kc % 2 == 0:
                    P.op("dve", lambda e, kc=kc, kk=kk, bv=bv: e.tensor_scalar(
                        out=xnT[:, kc, 0:W], in0=bv[:, kk * W:(kk + 1) * W], scalar1=gain_l(kc), scalar2=None, op0=ALU.mult),
                        r=[("ps", id(bank)), "pvec"], w=[pfx + "xnT"])
                else:
                    P.op("act", lambda e, kc=kc, kk=kk, bv=bv: e.activation(
                        out=xnT[:, kc, 0:W], in_=bv[:, kk * W:(kk + 1) * W], func=AF.Copy, scale=gain_l(kc)),
                        r=[("ps", id(bank)), "pvec"], w=[pfx + "xnT"])

    def post_norm_residual(ybanks, x_tile_ap, xkey, gB, gkey, tmp):
        for hb in range(2):
            P.op("act", lambda e, hb=hb: e.activation(out=tmp[:, hb * 512:(hb + 1) * 512], in_=ybanks[hb][:], func=AF.Square,
                                                      accum_out=stat[:, 8 + hb:9 + hb]),
                 r=[("ps", id(ybanks[hb]))], w=["ytmp", "stat"])
        P.op("dve", lambda e: e.tensor_tensor(out=stat[:, 10:11], in0=stat[:, 8:9], in1=stat[:, 9:10], op=ALU.add),
             r=["stat"], w=["stat"])
        P.op("act", lambda e: e.activation(out=stat[:, 11:12], in_=stat[:, 10:11], func=AF.Sqrt, scale=1.0 / D,
                                           bias=epsb[:, 0:1]), r=["stat", "epsb"], w=["stat"])
        P.op("dve", lambda e: e.reciprocal(out=stat[:, 11:12], in_=stat[:, 11:12]), r=["stat"], w=["stat"])
        for hb in range(2):
            P.op("dve", lambda e, hb=hb: e.scalar_tensor_tensor(
                out=tmp[:, hb * 512:(hb + 1) * 512], in0=ybanks[hb][:], scalar=stat[:, 11:12],
                in1=gB[:, hb * 512:(hb + 1) * 512], op0=ALU.mult, op1=ALU.mult),
                r=[("ps", id(ybanks[hb])), "stat", gkey], w=["ytmp"])
        P.op("pool", lambda e: e.tensor_tensor(out=x_tile_ap, in0=x_tile_ap, in1=tmp[:], op=ALU.add),
             r=["ytmp", xkey], w=[xkey])

    epsb = sb("epsb", [128, 1])
    P.op("dve", lambda e: e.memset(epsb[:], EPS), w=["epsb"])

    def a_pass(l):
        with ExitStack() as ph:
            w_in = sb("aw_in", [128, 8, 2 * D], BF16, ph)
            w_out = sb("aw_out", [128, 8, D], BF16, ph)
            wga = sb("awga", [128, 8, 128], BF16, ph)
            wgx = sb("awgx", [128, 8, 128], BF16, ph)
            gB = sb("agB", [128, D], F32, ph)
            xt = sb("axt", [128, 2, 2, D], F32, ph)
            xh = sb("axh", [128, 2, D], BF16, ph)
            xnT = sb("axnT", [128, 8, CH], BF16, ph)
            ubuf = sb("aubuf", [128, 8, 352], F32, ph)
            hstS = sb("ahstS", [128, 8, NS], F32, ph)
            hlast = sb("ahlast", [128, 8, NS], F32, ph)
            ubv = lambda oc: ubuf[:, oc, 0:352].rearrange("p (s j) -> p s j", j=11)
            v3 = lambda ap: ap.rearrange("p (s t) -> p s t", t=8)
            uc = sb("auc", [128, 8, CH], F32, ph)
            ucb = sb("aucb", [128, 8, CH], BF16, ph)
            rbuf = sb("arbuf", [128, 8, CH], F32, ph)
            ibuf = sb("aibuf", [128, 8, CH], F32, ph)
            sbuf_ = sb("asbuf", [128, 8, CH], F32, ph)
            thg = sb("athg", [128, 8, CH], BF16, ph)
            hbuf = sbuf_
            mT = sb("amT", [128, 8, CH], BF16, ph)
            hst = sb("ahst", [128, 8], F32, ph)
            ytmp = sb("aytmp", [128, D], F32, ph)
            if l == 1:
                wkv = sb("awkv", [128, 8, 2 * D + H], BF16, ph)
                kvtok = sb("akvtok", [128, 2, D], F32, ph)
                vbf = sb("avbf", [128, 2, D], BF16, ph)
                kTc = sb("akTc", [128, 8, CH], BF16, ph)
                lft = sb("alft", [128, H], F32, ph)
                lfe = sb("alfe", [128, H], F32, ph)

            P.dma("pool", lambda e: e.dma_start(out=w_in[:], in_=a_w_in[l].rearrange("(kc p) n -> p kc n", p=128)), w=["aw_in"])
            P.dma("pool", lambda e: e.dma_start(out=w_out[:], in_=a_w_out[l].rearrange("(kc p) n -> p kc n", p=128)), w=["aw_out"])
            P.dma("pool", lambda e: e.dma_start(out=wga[:], in_=a_w_ga[l].rearrange("n c d -> c n d")), w=["awga"])
            P.dma("pool", lambda e: e.dma_start(out=wgx[:], in_=a_w_gx[l].rearrange("n c d -> c n d")), w=["awgx"])
            P.dma("sp", lambda e: e.dma_start(out=gB[:], in_=rowv_d[l:l + 1, :].partition_broadcast(128)), w=["agB"])
            if l == 1:
                P.dma("pool", lambda e: e.dma_start(out=wkv[:], in_=w_kv.rearrange("(kc p) n -> p kc n", p=128)), w=["awkv"])
            P.op("dve", lambda e: e.memset(ubuf[:, :, 0:3], 0.0), w=[("ubuf", o_) for o_ in range(8)])
            P.op("dve", lambda e: e.memset(hst[:], 0.0), w=["hst"])

            src = xp if l == 0 else x1s
            dst = x1s if l == 0 else x2s
            tpb = [ps[0], ps[1]]
            chunks = (list(range(NCH)) if do_prompt else []) + (["S"] if do_sample else [])
            def front(ch):
                sample = (ch == "S")
                slot = 0 if sample else ch % 2
                xkey = lambda tt, slot=slot: ("axt", slot)
                x_ap = lambda tt, slot=slot: xt[:, slot, tt, :]
                if sample:
                    if do_prompt:
                        P.dma("pool", lambda e: e.dma_start(out=rnn_out[l], in_=hst[:]), r=["hst"], w=[("rnn_out", l)])
                        P.dma("pool", lambda e: e.dma_start(out=conv_out[l], in_=ubuf[:, :, 0:3]), r=[("ubuf", o_) for o_ in range(8)], w=[("conv_out", l)])
                    ssrc = xs_d if l == 0 else xs1
                    P.dma("sp", lambda e, slot=slot: e.dma_start(
                        out=xt[:, slot, :, :], in_=ssrc.rearrange("(t p) f -> p t f", p=128)), w=[("axt", slot)])
                    for oc in range(8):
                        P.dma("sp", lambda e, oc=oc: e.dma_start(out=ubv(oc)[:, :, 0:3], in_=st_conv_d[l, :, oc]),
                              r=[], w=[("ubuf", oc)])
                    P.dma("sp", lambda e: e.dma_start(out=hstS[:], in_=st_rnn_d[l]), w=["hstS"])
                else:
                    P.dma("sp", lambda e, ch=ch, slot=slot: e.dma_start(
                        out=xt[:, slot, :, :], in_=src[ch * CH:(ch + 1) * CH, :].rearrange("(t p) f -> p t f", p=128)),
                        w=[("axt", slot)])
                rms_to_featmajor(x_ap, xkey, 2, lambda kc: pv("apre", l, kc), xh, xnT, tpb, "a")
                for oc in range(16):
                    bank = ps[2 + oc % 2]
                    for kc in range(8):
                        P.op("pe", lambda e, oc=oc, kc=kc, bank=bank: e.matmul(
                            out=bank[:, 0:CH], lhsT=w_in[:, kc, oc * 128:(oc + 1) * 128], rhs=xnT[:, kc, :],
                            start=(kc == 0), stop=(kc == 7)), r=["aw_in", "axnT"], w=[("ps", id(bank))])
                    if oc < 8:
                        if sample:
                            P.op("act", lambda e, oc=oc, bank=bank: e.activation(out=ubv(oc)[:, :, 3:11], in_=v3(bank[:, 0:CH]),
                                                                                 func=AF.Copy),
                                 r=[("ps", id(bank))], w=[("ubuf", oc)])
                        else:
                            P.op("act", lambda e, oc=oc, bank=bank: e.activation(out=ubuf[:, oc, 3:3 + CH], in_=bank[:, 0:CH],
                                                                                 func=AF.Copy),
                                 r=[("ps", id(bank))], w=[("ubuf", oc)])
                    else:
                        g = oc - 8
                        P.op("act", lambda e, g=g, bank=bank: e.activation(out=thg[:, g, :], in_=bank[:, 0:CH], func=AF.Tanh,
                                                                           scale=0.5),
                             r=[("ps", id(bank))], w=[("thg", g)])
                        P.op("dve", lambda e, g=g, bank=bank: e.scalar_tensor_tensor(
                            out=thg[:, g, :], in0=thg[:, g, :], scalar=1.0, in1=bank[:, 0:CH], op0=ALU.add, op1=ALU.mult),
                            r=[("ps", id(bank)), ("thg", g)], w=[("thg", g)])

            def mid(ch):
                sample = (ch == "S")
                slot = 0 if sample else ch % 2
                xkey = lambda tt, slot=slot: ("axt", slot)
                x_ap = lambda tt, slot=slot: xt[:, slot, tt, :]
                for oc in range(8):
                    if sample:
                        uin = lambda oc, j: ubv(oc)[:, :, j:j + 8]
                        uco = lambda oc: v3(uc[:, oc, :])
                    else:
                        uin = lambda oc, j: ubuf[:, oc, j:j + CH]
                        uco = lambda oc: uc[:, oc, :]
                    P.op("pool", lambda e, oc=oc, uin=uin, uco=uco: e.tensor_scalar(
                        out=uco(oc), in0=uin(oc, 3), scalar1=pv("cw3", l, oc), scalar2=pv("cb", l, oc),
                        op0=ALU.mult, op1=ALU.add), r=[("ubuf", oc), "pvec"], w=[("uc", oc)])
                    for j in range(3):
                        P.op("dve", lambda e, oc=oc, j=j, uin=uin, uco=uco: e.scalar_tensor_tensor(
                            out=uco(oc), in0=uin(oc, j), scalar=pv("cw%d" % j, l, oc), in1=uco(oc),
                            op0=ALU.mult, op1=ALU.add), r=[("ubuf", oc), ("uc", oc), "pvec"], w=[("uc", oc)])
                for oc in range(8):
                    P.op("act", lambda e, oc=oc: e.activation(out=ucb[:, oc, :], in_=uc[:, oc, :], func=AF.Copy),
                         r=[("uc", oc)], w=[("ucb", oc)])
                if sample:
                    for oc in range(8):
                        P.dma("pool", lambda e, oc=oc: e.dma_start(out=convs_out[l, :, oc], in_=ubv(oc)[:, :, 8:11]),
                              r=[("ubuf", oc)], w=[("convs_out", l, oc)])
                else:
                    P.op("pool", lambda e: e.tensor_copy(out=ubuf[:, :, 0:3], in_=ubuf[:, :, CH:CH + 3]), r=[("ubuf", o_) for o_ in range(8)], w=[("ubuf", o_) for o_ in range(8)])
                for oc in range(8):
                    bank = ps[4 + oc % 2]
                    P.op("pe", lambda e, oc=oc, bank=bank: e.matmul(out=bank[:, 0:CH], lhsT=wga[:, oc, :], rhs=ucb[:, oc, :],
                                                                    start=True, stop=True),
                         r=["awga", ("ucb", oc)], w=[("ps", id(bank))])
                    P.op("pe", lambda e, oc=oc, bank=bank: e.matmul(out=bank[:, CH:2 * CH], lhsT=wgx[:, oc, :], rhs=ucb[:, oc, :],
                                                                    start=True, stop=True),
                         r=["awgx", ("ucb", oc)], w=[("ps", id(bank))])
                    P.op("act", lambda e, oc=oc, bank=bank: e.activation(out=rbuf[:, oc, :], in_=bank[:, 0:CH], func=AF.Tanh,
                                                                         scale=0.5, bias=bgah[:, l, oc:oc + 1]),
                         r=[("ps", id(bank)), "bgah"], w=[("rbuf", oc)])
                    P.op("act", lambda e, oc=oc, bank=bank: e.activation(out=ibuf[:, oc, :], in_=bank[:, CH:2 * CH], func=AF.Tanh,
                                                                         scale=0.5, bias=bgxh[:, l, oc:oc + 1]),
                         r=[("ps", id(bank)), "bgxh"], w=[("ibuf", oc)])
                for oc in range(8):
                    P.op("act", lambda e, oc=oc: e.activation(out=sbuf_[:, oc, :], in_=rbuf[:, oc, :], func=AF.Exp,
                                                              scale=cneg2[:, l, oc:oc + 1], bias=cneg2[:, l, oc:oc + 1]),
                         r=[("rbuf", oc), "cneg2"], w=[("sbuf", oc)])
                    P.op("act", lambda e, oc=oc: e.activation(out=rbuf[:, oc, :], in_=rbuf[:, oc, :], func=AF.Exp,
                                                              scale=cnegh[:, l, oc:oc + 1], bias=cnegh[:, l, oc:oc + 1]),
                         r=[("rbuf", oc), "cnegh"], w=[("rbuf", oc)])
                for oc in range(8):
                    P.op("act", lambda e, oc=oc: e.activation(out=sbuf_[:, oc, :], in_=sbuf_[:, oc, :], func=AF.Sqrt, scale=-1.0,
                                                              bias=oneb[:, 0:1]),
                         r=[("sbuf", oc), "oneb"], w=[("sbuf", oc)])
                for oc in range(8):
                    P.op("dve", lambda e, oc=oc: e.scalar_tensor_tensor(
                        out=ibuf[:, oc, :], in0=ibuf[:, oc, :], scalar=1.0, in1=uc[:, oc, :], op0=ALU.add, op1=ALU.mult),
                        r=[("ibuf", oc), ("uc", oc)], w=[("ibuf", oc)])
                    P.op("dve", lambda e, oc=oc: e.scalar_tensor_tensor(
                        out=ibuf[:, oc, :], in0=ibuf[:, oc, :], scalar=0.5, in1=sbuf_[:, oc, :], op0=ALU.mult, op1=ALU.mult),
                        r=[("ibuf", oc), ("sbuf", oc)], w=[("ibuf", oc)])
                    if sample:
                        for sq in range(NS):
                            P.op("dve", lambda e, oc=oc, sq=sq: e.tensor_tensor_scan(
                                out=hbuf[:, oc, sq * 8:(sq + 1) * 8], data0=rbuf[:, oc, sq * 8:(sq + 1) * 8],
                                data1=ibuf[:, oc, sq * 8:(sq + 1) * 8], initial=hstS[:, oc, sq:sq + 1],
                                op0=ALU.mult, op1=ALU.add), r=[("rbuf", oc), ("ibuf", oc), "hstS"], w=[("sbuf", oc)])
                    else:
                        P.op("dve", lambda e, oc=oc: e.tensor_tensor_scan(
                            out=hbuf[:, oc, :], data0=rbuf[:, oc, :], data1=ibuf[:, oc, :], initial=hst[:, oc:oc + 1],
                            op0=ALU.mult, op1=ALU.add), r=[("rbuf", oc), ("ibuf", oc), "hst"], w=[("sbuf", oc)])
                    P.op("dve", lambda e, oc=oc: e.scalar_tensor_tensor(
                        out=mT[:, oc, :], in0=hbuf[:, oc, :], scalar=0.5, in1=thg[:, oc, :], op0=ALU.mult, op1=ALU.mult),
                        r=[("sbuf", oc), ("thg", oc)], w=[("amT", oc)])
                if sample:
                    P.op("pool", lambda e: e.tensor_copy(
                        out=hlast[:], in_=hbuf[:].rearrange("p o (s t) -> p o s t", t=8)[:, :, :, 7]), r=[("sbuf", o_) for o_ in range(8)], w=["hlast"])
                    P.dma("pool", lambda e: e.dma_start(out=rnns_out[l], in_=hlast[:]), r=["hlast"], w=[("rnns_out", l)])
                else:
                    P.op("pool", lambda e: e.tensor_copy(out=hst[:], in_=hbuf[:, :, CH - 1]), r=[("sbuf", o_) for o_ in range(8)], w=["hst"])

            def tail(ch):
                sample = (ch == "S")
                slot = 0 if sample else ch % 2
                xkey = lambda tt, slot=slot: ("axt", slot)
                x_ap = lambda tt, slot=slot: xt[:, slot, tt, :]
                for tt in range(2):
                    yb = [ps[6], ps[7]]
                    for fc in range(2):
                        for kc in range(8):
                            P.op("pe", lambda e, tt=tt, fc=fc, kc=kc: e.matmul(
                                out=yb[fc][:], lhsT=mT[:, kc, tt * 128:(tt + 1) * 128], rhs=w_out[:, kc, fc * 512:(fc + 1) * 512],
                                start=(kc == 0), stop=(kc == 7)), r=[("amT", kc), "aw_out"], w=[("ps", id(yb[fc]))])
                    post_norm_residual(yb, xt[:, slot, tt, :], ("axt", slot), gB, "agB", ytmp)
                if sample:
                    sdst = xs1 if l == 0 else xs2
                    P.dma("pool", lambda e, slot=slot: e.dma_start(
                        out=sdst.rearrange("(t p) f -> p t f", p=128), in_=xt[:, slot, :, :]),
                        r=[("axt", slot)], w=[("sdst", l)])
                else:
                    P.dma("pool", lambda e, ch=ch, slot=slot: e.dma_start(
                        out=dst[ch * CH:(ch + 1) * CH, :].rearrange("(t p) f -> p t f", p=128), in_=xt[:, slot, :, :]),
                        r=[("axt", slot)], w=[("dst", ch)])
                if l == 1:
                    rms_to_featmajor(x_ap, xkey, 2, lambda kc: pv("kvn", 0, kc), xh, xnT, tpb, "a")
                    for oc in range(8):
                        bank = ps[2 + oc % 2]
                        for kc in range(8):
                            P.op("pe", lambda e, oc=oc, kc=kc, bank=bank: e.matmul(
                                out=bank[:, 0:CH], lhsT=wkv[:, kc, oc * 128:(oc + 1) * 128], rhs=xnT[:, kc, :],
                                start=(kc == 0), stop=(kc == 7)), r=["awkv", "axnT"], w=[("ps", id(bank))])
                        P.op("act", lambda e, oc=oc, bank=bank: e.activation(out=kTc[:, oc, :], in_=bank[:, 0:CH], func=AF.Copy),
                             r=[("ps", id(bank))], w=["kTc"])
                    if sample:
                        P.op("pool", lambda e: e.tensor_copy(out=kTn[:], in_=kTc[:]), r=["kTc"], w=["kTn"])
                    else:
                        P.dma("pool", lambda e, ch=ch: e.dma_start(out=kTs[:, :, ch * CH:(ch + 1) * CH].rearrange("h p t -> p h t"),
                                                                   in_=kTc[:]), r=["kTc"], w=[("kTs", ch)])
                    ko, vo, lo = (ks_out, vs_out, lfs_out) if sample else (k_out, v_out, lf_out)
                    for tt in range(2):
                        tg = tt if sample else ch * 2 + tt
                        for part in range(2):
                            for fc in range(2):
                                bank = ps[4 + fc]
                                c0 = part * D + fc * 512
                                for kc in range(8):
                                    P.op("pe", lambda e, tt=tt, kc=kc, bank=bank, c0=c0: e.matmul(
                                        out=bank[:], lhsT=xnT[:, kc, tt * 128:(tt + 1) * 128], rhs=wkv[:, kc, c0:c0 + 512],
                                        start=(kc == 0), stop=(kc == 7)), r=["awkv", "axnT"], w=[("ps", id(bank))])
                                if fc == 0:
                                    P.op("act", lambda e, part=part, fc=fc, bank=bank: e.activation(
                                        out=kvtok[:, part, fc * 512:(fc + 1) * 512], in_=bank[:], func=AF.Copy),
                                        r=[("ps", id(bank))], w=["kvtok"])
                                else:
                                    P.op("dve", lambda e, part=part, fc=fc, bank=bank: e.tensor_copy(
                                        out=kvtok[:, part, fc * 512:(fc + 1) * 512], in_=bank[:]),
                                        r=[("ps", id(bank))], w=["kvtok"])
                                if part == 1:
                                    P.op("pool", lambda e, fc=fc, tt=tt: e.tensor_copy(
                                        out=vbf[:, tt, fc * 512:(fc + 1) * 512], in_=kvtok[:, 1, fc * 512:(fc + 1) * 512]),
                                        r=["kvtok"], w=["vbf"])
                        P.dma("pool", lambda e, tg=tg, ko=ko: e.dma_start(out=ko[tg * 128:(tg + 1) * 128, :], in_=kvtok[:, 0, :]),
                              r=["kvtok"], w=[("k_out", sample, tg)])
                        P.dma("pool", lambda e, tg=tg, vo=vo: e.dma_start(out=vo[tg * 128:(tg + 1) * 128, :], in_=kvtok[:, 1, :]),
                              r=["kvtok"], w=[("v_out", sample, tg)])
                        bank = ps[6]
                        for kc in range(8):
                            P.op("pe", lambda e, tt=tt, kc=kc, bank=bank: e.matmul(
                                out=bank[:, 0:H], lhsT=xnT[:, kc, tt * 128:(tt + 1) * 128], rhs=wkv[:, kc, 2 * D:2 * D + H],
                                start=(kc == 0), stop=(kc == 7)), r=["awkv", "axnT"], w=[("ps", id(bank))])
                        P.op("dve", lambda e, bank=bank: e.tensor_tensor(out=lfe[:], in0=bank[:, 0:H], in1=bfB[:, 0:H], op=ALU.add),
                             r=[("ps", id(bank)), "bfB"], w=["lfe"])
                        P.op("act", lambda e: e.activation(out=lfe[:], in_=lfe[:], func=AF.Exp, scale=-1.0), r=["lfe"], w=["lfe"])
                        P.op("act", lambda e: e.activation(out=lfe[:], in_=lfe[:], func=AF.Ln, bias=1.0), r=["lfe"], w=["lfe"])
                        P.op("dve", lambda e: e.tensor_scalar(out=lft[:], in0=lfe[:], scalar1=-1.0, scalar2=None, op0=ALU.mult),
                             r=["lfe"], w=["lft"])
                        P.dma("pool", lambda e, tg=tg, lo=lo: e.dma_start(out=lo[tg * 128:(tg + 1) * 128, :], in_=lft[:]),
                              r=["lft"], w=[("lf_out", sample, tg)])
                        if sample:
                            if tt == 0:
                                P.op("pool", lambda e: e.tensor_copy(
                                    out=vn1[:, :, 0:128], in_=vbf[:, 0, :].rearrange("p (h d) -> p h d", h=8)),
                                    r=["vbf"], w=["vn1"])
                                bank2 = ps[7]
                                P.op("pe", lambda e, bank2=bank2: e.matmul(out=bank2[:, 0:H], lhsT=scst[:, 2, 0:128], rhs=lft[:],
                                                                           start=True, stop=True),
                                     r=["scst", "lft"], w=[("ps", id(bank2))])
                                P.op("dve", lambda e, bank2=bank2: e.tensor_scalar(out=negE[:], in0=bank2[:, 0:H], scalar1=-1.0,
                                                                                  scalar2=None, op0=ALU.mult),
                                     r=[("ps", id(bank2))], w=["negE"])
                            continue
                        bank = ps[7]
                        P.op("pe", lambda e, bank=bank: e.matmul(out=bank[:, 0:H], lhsT=tri, rhs=lft[:], start=True, stop=False),
                             r=["cst", "lft"], w=[("ps", id(bank))])
                        P.op("pe", lambda e, bank=bank: e.matmul(out=bank[:, 0:H], lhsT=ones, rhs=lacc[:], start=False, stop=True),
                             r=["cst", "lacc"], w=[("ps", id(bank))])
                        P.op("dve", lambda e, tg=tg, bank=bank: e.tensor_copy(out=ck[:, tg, :], in_=bank[:, 0:H]),
                             r=[("ps", id(bank))], w=["ck"])
                        P.op("dve", lambda e: e.tensor_tensor(out=lacc[:], in0=lacc[:], in1=lft[:], op=ALU.add),
                             r=["lacc", "lft"], w=["lacc"])
                    for tt in range(2 if not sample else 0):
                        P.dma("pool", lambda e, ch=ch, tt=tt: e.dma_start(
                            out=vs[:, :, ch * 2 + tt, :].rearrange("h p d -> p h d"),
                            in_=vbf[:, tt, :].rearrange("p (h d) -> p h d", h=8)), r=["vbf"], w=[("vs", ch, tt)])

            for ci, ch in enumerate(chunks):
                if ci == 0:
                    front(ch)
                mid(ch)
                if ci + 1 < len(chunks):
                    front(chunks[ci + 1])
                tail(ch)
            if do_prompt and not do_sample:
                P.dma("pool", lambda e: e.dma_start(out=rnn_out[l], in_=hst[:]), r=["hst"], w=[("rnn_out", l)])
                P.dma("pool", lambda e: e.dma_start(out=conv_out[l], in_=ubuf[:, :, 0:3]), r=[("ubuf", o_) for o_ in range(8)], w=[("conv_out", l)])
            if l == 1 and do_prompt:
                P.dma("pool", lambda e: e.dma_start(out=cs.rearrange("(t p) h -> p t h", p=128), in_=ck[:]), r=["ck"], w=["cs"])
            P.drain("sp")
            P.flush()

    oneb = sb("oneb", [128, 1])
    P.op("dve", lambda e: e.memset(oneb[:], 1.0), w=["oneb"])

    a_pass(0)
    a_pass(1)

    def b_phase():
        with ExitStack() as ph:
            xo = sb("bxo", [128, 16, D], F32, ph)
            RB = sb("bRB", [128, 16, H], F32, ph)
            QT = sb("bQT", [128, 8, 2048], BF16, ph)
            sgT = sb("bsgT", [128, 8, 2048], BF16, ph)
            gB = sb("bgB", [128, D], F32, ph)
            ytmp = sb("bytmp", [128, D], F32, ph)
            for j in range(16):
                P.dma("pool", lambda e, j=j: e.indirect_dma_start(
                    out=xo[:, j, :], out_offset=None, in_=x2s,
                    in_offset=bass.IndirectOffsetOnAxis(ap=idx[:, j:j + 1], axis=0)), r=["idx"], w=[("bxo", j)])
                P.dma("pool", lambda e, j=j: e.indirect_dma_start(
                    out=RB[:, j, :], out_offset=None, in_=cs,
                    in_offset=bass.IndirectOffsetOnAxis(ap=idx[:, 16 + j:17 + j], axis=0)), r=["idx"], w=["bRB"])
            for l in range(2):
                with ExitStack() as p1:
                    w_in = sb("bw_in", [128, 8, 2 * D], BF16, p1)
                    xh = sb("bxh", [128, 4, D], BF16, p1)
                    xnT = sb("bxnT", [128, 8, 512], BF16, p1)
                    thg = sb("bthg", [128, 2, 512], F32, p1)
                    P.dma("pool", lambda e: e.dma_start(out=w_in[:], in_=b_w_in[l].rearrange("(kc p) n -> p kc n", p=128)),
                          w=["bw_in"])
                    P.dma("sp", lambda e: e.dma_start(out=gB[:], in_=rowv_d[2 + l:3 + l, :].partition_broadcast(128)), w=["bgB"])
                    for grp in range(4):
                        x_ap = lambda tt, grp=grp: xo[:, grp * 4 + tt, :]
                        xkey = lambda tt, grp=grp: ("bxo", grp * 4 + tt)
                        rms_to_featmajor(x_ap, xkey, 4, lambda kc: pv("bpre", l, kc), xh, xnT, [ps[0], ps[1], ps[2], ps[3]], "b")
                        for oc in range(16):
                            bank = ps[4 + oc % 2]
                            for kc in range(8):
                                P.op("pe", lambda e, oc=oc, kc=kc, bank=bank: e.matmul(
                                    out=bank[:], lhsT=w_in[:, kc, oc * 128:(oc + 1) * 128], rhs=xnT[:, kc, :],
                                    start=(kc == 0), stop=(kc == 7)), r=["bw_in", "bxnT"], w=[("ps", id(bank))])
                            if oc < 8:
                                P.op("act", lambda e, oc=oc, bank=bank, grp=grp: e.activation(
                                    out=QT[:, oc, grp * 512:(grp + 1) * 512], in_=bank[:], func=AF.Copy),
                                    r=[("ps", id(bank))], w=["bQT"])
                            else:
                                g = oc - 8
                                ts_ = g % 2
                                P.op("act", lambda e, bank=bank, ts_=ts_: e.activation(out=thg[:, ts_, :], in_=bank[:], func=AF.Tanh,
                                                                                       scale=0.5),
                                     r=[("ps", id(bank))], w=[("bthg", ts_)])
                                P.op("dve", lambda e, bank=bank, ts_=ts_: e.scalar_tensor_tensor(
                                    out=thg[:, ts_, :], in0=thg[:, ts_, :], scalar=1.0, in1=bank[:], op0=ALU.add, op1=ALU.mult),
                                    r=[("ps", id(bank)), ("bthg", ts_)], w=[("bthg", ts_)])
                                P.op("pool", lambda e, g=g, grp=grp, ts_=ts_: e.tensor_scalar(
                                    out=sgT[:, g, grp * 512:(grp + 1) * 512], in0=thg[:, ts_, :], scalar1=0.5, scalar2=0.0,
                                    op0=ALU.mult, op1=ALU.add), r=[("bthg", ts_)], w=["bsgT"])
                    P.drain("sp")
                    P.flush()
                with ExitStack() as p2:
                    kT = sb("bkT", [128, 2, T], BF16, p2)
                    vv = sb("bvv", [128, 2, NT, 2, 65], BF16, p2)
                    bias = sb("bbias", [128, 2, NT, H], F32, p2)
                    sm = sb("bsm", [128, 2, 128], F32, p2)
                    pT = sb("bpT", [128, 8, 128], BF16, p2)
                    on = sb("bon", [128, 2, 128], BF16, p2)
                    rden = sb("brden", [128, 2, 2], F32, p2)
                    P.op("pool", lambda e: e.memset(vv[:, :, :, :, 64:65], 1.0), w=[("bvv", 0), ("bvv", 1)])
                    for hp in range(8):
                        sl = hp % 2
                        P.dma("sp", lambda e, hp=hp, sl=sl: e.dma_start(out=kT[:, sl, :], in_=kTs[hp]),
                              w=[("bkT", sl)])
                        P.dma("sp", lambda e, hp=hp, sl=sl: e.dma_start(
                            out=vv[:, sl, :, :, 0:64], in_=vs[hp].rearrange("p t (e d) -> p t e d", e=2)),
                            w=[("bvv", sl)])
                        for j in range(16):
                            nk = 2 * j + 2
                            bs = j % 2
                            P.op("dve", lambda e, j=j, nk=nk, bs=bs: e.tensor_tensor(
                                out=bias[:, bs, 0:nk, :], in0=RB[:, j:j + 1, :].to_broadcast([128, nk, H]), in1=ck[:, 0:nk, :],
                                op=ALU.subtract), r=["bRB", "ck"], w=[("bbias", bs)])
                            obs = [ps[6], ps[7]]
                            items = [(ee, kt) for kt in range(nk) for ee in range(2)]

                            def emit_s(n, j=j, hp=hp, sl=sl, nk=nk, bs=bs):
                                ee, kt = items[n]
                                h = hp * 2 + ee
                                sbk = ps[n % 4]
                                pslot = n % 8
                                P.op("pe", lambda e: e.matmul(
                                    out=sbk[:, 0:128], lhsT=kT[ee * 64:(ee + 1) * 64, sl, kt * 128:(kt + 1) * 128],
                                    rhs=QT[ee * 64:(ee + 1) * 64, hp, j * 128:(j + 1) * 128], start=True, stop=True),
                                    r=[("bkT", sl), "bQT"], w=[("ps", id(sbk))])
                                if kt >= nk - 2:
                                    mk = mask_a if kt == nk - 2 else mask_b
                                    ms = kt - (nk - 2)
                                    P.op("dve", lambda e: e.scalar_tensor_tensor(
                                        out=sm[:, ms, :], in0=sbk[:, 0:128], scalar=0.125, in1=mk, op0=ALU.mult, op1=ALU.add),
                                        r=[("ps", id(sbk)), "cst"], w=[("bsm", ms)])
                                    P.op("act", lambda e: e.activation(
                                        out=pT[:, pslot, :], in_=sm[:, ms, :], func=AF.Exp, bias=bias[:, bs, kt, h:h + 1]),
                                        r=[("bsm", ms), ("bbias", bs)], w=[("bpT", pslot)])
                                else:
                                    P.op("act", lambda e: e.activation(
                                        out=pT[:, pslot, :], in_=sbk[:, 0:128], func=AF.Exp, scale=0.125,
                                        bias=bias[:, bs, kt, h:h + 1]),
                                        r=[("ps", id(sbk)), ("bbias", bs)], w=[("bpT", pslot)])

                            def emit_pv(n, sl=sl, nk=nk, obs=obs):
                                ee, kt = items[n]
                                pslot = n % 8
                                ob = obs[ee]
                                P.op("pe", lambda e: e.matmul(
                                    out=ob[:, 0:65], lhsT=pT[:, pslot, :], rhs=vv[:, sl, kt, ee, :],
                                    start=(kt == 0), stop=(kt == nk - 1)),
                                    r=[("bpT", pslot), ("bvv", sl)], w=[("ps", id(ob))])
                            for k in range(nk + 2):
                                if k < nk:
                                    emit_s(2 * k)
                                    emit_s(2 * k + 1)
                                if k >= 2:
                                    emit_pv(2 * k - 4)
                                    emit_pv(2 * k - 3)
                            osl = j % 2
                            for ee in range(2):
                                P.op("dve", lambda e, ob=obs[ee], osl=osl, ee=ee: e.reciprocal(
                                    out=rden[:, osl, ee:ee + 1], in_=ob[:, 64:65]),
                                    r=[("ps", id(obs[ee]))], w=[("brden", osl, ee)])
                                P.op("dve", lambda e, ob=obs[ee], osl=osl, ee=ee: e.tensor_scalar(
                                    out=on[:, osl, ee * 64:(ee + 1) * 64], in0=ob[:, 0:64],
                                    scalar1=rden[:, osl, ee:ee + 1], scalar2=None, op0=ALU.mult),
                                    r=[("ps", id(obs[ee])), ("brden", osl, ee)], w=[("bon", osl)])
                            tb = ps[4 + j % 2]
                            tbv = tb[:].bitcast(BF16)
                            P.op("pe", lambda e, osl=osl, tbv=tbv: e.transpose(out=tbv[:, 0:128], in_=on[:, osl, :], identity=identb[:]),
                                 r=[("bon", osl), "identb"], w=[("ps", id(tb))])
                            P.op("dve", lambda e, tbv=tbv, hp=hp, j=j: e.tensor_tensor(
                                out=sgT[:, hp, j * 128:(j + 1) * 128], in0=tbv[:, 0:128], in1=sgT[:, hp, j * 128:(j + 1) * 128],
                                op=ALU.mult), r=[("ps", id(tb)), "bsgT"], w=["bsgT"])
                    P.drain("sp")
                    P.flush()
                with ExitStack() as p3:
                    w_out = sb("bw_out", [128, 8, D], BF16, p3)
                    P.dma("pool", lambda e: e.dma_start(out=w_out[:], in_=b_w_out[l].rearrange("(kc p) n -> p kc n", p=128)),
                          w=["bw_out"])
                    for j in range(16):
                        yb = [ps[6], ps[7]]
                        for fc in range(2):
                            for kc in range(8):
                                P.op("pe", lambda e, j=j, fc=fc, kc=kc, yb=yb: e.matmul(
                                    out=yb[fc][:], lhsT=sgT[:, kc, j * 128:(j + 1) * 128], rhs=w_out[:, kc, fc * 512:(fc + 1) * 512],
                                    start=(kc == 0), stop=(kc == 7)), r=["bsgT", "bw_out"], w=[("ps", id(yb[fc]))])
                        post_norm_residual(yb, xo[:, j, :], ("bxo", j), gB, "bgB", ytmp)
                        if l == 1:
                            P.dma("sp", lambda e, j=j: e.dma_start(out=y_own[j], in_=xo[:, j, :]), r=[("bxo", j)],
                                  w=[("y_own", j)])
                    P.drain("sp")
                    P.flush()

    def s_phase():
        RG = [list(range(8))]
        with ExitStack() as ph:
            xs_t = sb("sxs", [128, 2, D], F32, ph)
            Dall = sb("sDall", [128, NS, 64, 2], F32, ph)
            P.op("dve", lambda e: e.memset(ytmp[:], 0.0), w=["ytmp"])
            for i_ in range(2):
                for t_ in range(2):
                    P.dma("sp", lambda e, i_=i_, t_=t_: e.dma_start(out=o_scr[i_][t_ * 128:(t_ + 1) * 128, :], in_=ytmp[:]),
                          r=["ytmp"], w=[("o_scr", i_)])
            gB = sb("sgB", [128, D], F32, ph)
            ytmp = sb("sytmp", [128, D], F32, ph)
            onesf = sb("sonesf", [128, 64], F32, ph)
            P.dma("sp", lambda e: e.dma_start(out=xs_t[:], in_=xs2.rearrange("(t p) f -> p t f", p=128)), w=["sxs"])
            P.op("dve", lambda e: e.memset(onesf[:], 1.0), w=["sonesf"])
            with ExitStack() as p0:
                Lg = sb("sLg", [128, 2, 2, 64], F32, p0)
                pre = sb("spre", [128, 2, 2, 64], F32, p0)
                LT = sb("sLT", [128, 2, 2], F32, p0)
                TT = sb("sTT", [128, 2, 2], F32, p0)
                zc = sb("szc", [128, 1], F32, p0)
                P.op("dve", lambda e: e.memset(zc[:], 0.0), w=["szc"])
                for sq in range(NS):
                    sl = sq % 2
                    P.dma("pool", lambda e, sq=sq, sl=sl: e.indirect_dma_start(
                        out=Lg[:, sl, :, :].rearrange("p a b -> p (a b)"), out_offset=None, in_=clf_full,
                        in_offset=bass.IndirectOffsetOnAxis(ap=idxL[:, sq // 8, sq % 8:sq % 8 + 1], axis=0)), r=["idxL"], w=[("sLg", sl)])
                    for ee in range(2):
                        P.op("dve", lambda e, sl=sl, ee=ee: e.tensor_tensor_scan(
                            out=pre[:, sl, ee, :], data0=onesf[:], data1=Lg[:, sl, ee, :], initial=zc[:, 0:1],
                            op0=ALU.mult, op1=ALU.add), r=[("sLg", sl), "sonesf", "szc"], w=[("spre", sl)])
                    bank = ps[sq % 2]
                    P.op("dve", lambda e, sl=sl: e.tensor_copy(out=TT[:, sl, :], in_=pre[:, sl, :, 63]),
                         r=[("spre", sl)], w=[("sTT", sl)])
                    P.op("pe", lambda e, sl=sl, bank=bank: e.matmul(out=bank[:, 0:2], lhsT=scst[:, 0, 0:128], rhs=TT[:, sl, :],
                                                                    start=True, stop=True),
                         r=["scst", ("sTT", sl)], w=[("ps", id(bank))])
                    P.op("dve", lambda e, sl=sl, bank=bank: e.tensor_tensor(out=LT[:, sl, :], in0=bank[:, 0:2], in1=TT[:, sl, :],
                                                                            op=ALU.add),
                         r=[("ps", id(bank)), ("sTT", sl)], w=[("sLT", sl)])
                    for ee in range(2):
                        P.op("dve", lambda e, sl=sl, ee=ee, sq=sq: e.tensor_scalar(
                            out=Dall[:, sq, :, ee], in0=pre[:, sl, ee, :], scalar1=-1.0, scalar2=LT[:, sl, ee:ee + 1],
                            op0=ALU.mult, op1=ALU.add), r=[("spre", sl), ("sLT", sl)], w=["sDall"])
                if debug:
                    P.dma("sp", lambda e: e.dma_start(out=dbg[:, 2836:2964], in_=Lg[:, 1].rearrange("p a b -> p (a b)")),
                          r=[("sLg", 1)], w=["dbg5"])
                    P.dma("sp", lambda e: e.dma_start(out=dbg[:, 3092:3220], in_=pre[:, 1].rearrange("p a b -> p (a b)")),
                          r=[("spre", 1)], w=["dbg6"])
                    P.dma("sp", lambda e: e.dma_start(out=dbg[:, 3348:3352], in_=LT[:].rearrange("p a b -> p (a b)")),
                          r=[("sLT", 0), ("sLT", 1)], w=["dbg7"])
                    P.dma("sp", lambda e: e.dma_start(out=dbg[:, 3352:3356], in_=TT[:].rearrange("p a b -> p (a b)")),
                          r=[("sTT", 0), ("sTT", 1)], w=["dbg8"])
                    P.dma("sp", lambda e: e.dma_start(out=dbg[:, 3356:3388].bitcast(I32), in_=idxL[:].rearrange("p a b -> p (a b)")),
                          r=["idxL"], w=["dbg9"])
                P.drain("sp")
                P.flush()
            for l in range(2):
                with ExitStack() as p1:
                    wg = sb("swg", [128, 8, D], BF16, p1)
                    wq = sb("swq", [128, 8, D], BF16, p1)
                    w_out = sb("sw_out", [128, 8, D], BF16, p1)
                    xh = sb("sxh", [128, 2, D], BF16, p1)
                    xnT = sb("sxnT", [128, 8, 256], BF16, p1)
                    sg = sb("ssg", [128, 2, D], BF16, p1)
                    thg = sb("sthg", [128, 512], F32, p1)
                    Qbd = sb("sQbd", [128, NS, 16], BF16, p1)
                    Kst = sb("sKst", [128, 2, 32, 128], F32, p1)
                    Vst = sb("sVst", [128, 1, 32, 128], F32, p1)
                    Vbf = sb("sVbf", [128, 2, 32, 129], BF16, p1)
                    KT = sb("sKT", [128, 2, 32, 128], BF16, p1)
                    Ssb = sb("sSsb", [128, 2, 512], F32, p1)
                    Psb = sb("sPsb", [128, 2, 512], BF16, p1)
                    Sn = sb("sSn", [128, 16], F32, p1)
                    Pn = sb("sPn", [128, 16], BF16, p1)
                    Osb = sb("sOsb", [16, 2, 130], F32, p1)
                    ofull = Kst[:, 0, 0:16, :].rearrange("p (a b) c -> p a (b c)", a=2)
                    mtok = sb("smtok", [128, 2, D], BF16, p1)
                    mT = sb("smT", [128, 8, 256], BF16, p1)
                    P.dma("pool", lambda e: e.dma_start(out=wg[:], in_=b_w_in[l][:, D:2 * D].rearrange("(kc p) n -> p kc n", p=128)),
                          w=["swg"])
                    P.dma("pool", lambda e: e.dma_start(out=wq[:], in_=b_w_in[l][:, 0:D].rearrange("(kc p) n -> p kc n", p=128)),
                          w=["swq"])
                    P.dma("pool", lambda e: e.dma_start(out=w_out[:], in_=b_w_out[l].rearrange("(kc p) n -> p kc n", p=128)),
                          w=["sw_out"])
                    P.dma("sp", lambda e: e.dma_start(out=gB[:], in_=rowv_d[2 + l:3 + l, :].partition_broadcast(128)), w=["sgB"])
                    P.op("pool", lambda e: e.memset(Vbf[:, :, :, 128:129], 1.0), w=[("sVbf", 0), ("sVbf", 1)])
                    P.op("pool", lambda e: e.memset(Qbd[:], 0.0), w=["sQbd"])
                    x_ap = lambda tt: xs_t[:, tt, :]
                    xkey = lambda tt: "sxs"
                    rms_to_featmajor(x_ap, xkey, 2, lambda kc: pv("bpre", l, kc), xh, xnT, [ps[0], ps[1]], "s")
                    Qv = Qbd[:].rearrange("p (s h) c -> p s h c", h=8)
                    for hp in range(8):
                        bank = ps[2 + hp % 2]
                        for kc in range(8):
                            P.op("pe", lambda e, kc=kc, hp=hp, bank=bank: e.matmul(
                                out=bank[:, 0:256], lhsT=wq[:, kc, hp * 128:(hp + 1) * 128], rhs=xnT[:, kc, :],
                                start=(kc == 0), stop=(kc == 7)), r=["swq", "sxnT"], w=[("ps", id(bank))])
                        for ee in range(2):
                            P.op("dve", lambda e, ee=ee, hp=hp, bank=bank: e.tensor_scalar(
                                out=Qv[ee * 64:(ee + 1) * 64, :, hp, ee * 8:(ee + 1) * 8],
                                in0=bank[ee * 64:(ee + 1) * 64, 0:32].rearrange("p (s t) -> p s t", t=8),
                                scalar1=0.125, scalar2=None, op0=ALU.mult), r=[("ps", id(bank))], w=["sQbd"])
                    for tt in range(2):
                        for fc in range(2):
                            bank = ps[4 + fc]
                            for kc in range(8):
                                P.op("pe", lambda e, kc=kc, tt=tt, fc=fc, bank=bank: e.matmul(
                                    out=bank[:], lhsT=xnT[:, kc, tt * 128:(tt + 1) * 128], rhs=wg[:, kc, fc * 512:(fc + 1) * 512],
                                    start=(kc == 0), stop=(kc == 7)), r=["swg", "sxnT"], w=[("ps", id(bank))])
                            P.op("act", lambda e, bank=bank: e.activation(out=thg[:], in_=bank[:], func=AF.Tanh, scale=0.5),
                                 r=[("ps", id(bank))], w=["sthg"])
                            P.op("dve", lambda e, bank=bank: e.scalar_tensor_tensor(
                                out=thg[:], in0=thg[:], scalar=1.0, in1=bank[:], op0=ALU.add, op1=ALU.mult),
                                r=[("ps", id(bank)), "sthg"], w=["sthg"])
                            P.op("pool", lambda e, tt=tt, fc=fc: e.tensor_scalar(
                                out=sg[:, tt, fc * 512:(fc + 1) * 512], in0=thg[:], scalar1=0.5, scalar2=0.0,
                                op0=ALU.mult, op1=ALU.add), r=["sthg"], w=["ssg"])
                    for sq in range(NS):
                        ob = ps[6 + sq % 2]
                        osl = sq % 2
                        for half in range(2):
                            bs = (sq * 2 + half) % 2
                            P.dma("pool", lambda e, sq=sq, half=half, bs=bs: e.indirect_dma_start(
                                out=Kst[:, bs, :, :].rearrange("p a b -> p (a b)"), out_offset=None, in_=ck_full,
                                in_offset=bass.IndirectOffsetOnAxis(ap=idxK[:, half, sq // 8, sq % 8:sq % 8 + 1], axis=0)),
                                r=["idxK"], w=[("sKst", bs)])
                            P.dma("pool", lambda e, sq=sq, half=half, bs=bs: e.indirect_dma_start(
                                out=Vst[:, 0, :, :].rearrange("p a b -> p (a b)"), out_offset=None, in_=cv_full,
                                in_offset=bass.IndirectOffsetOnAxis(ap=idxK[:, half, sq // 8, sq % 8:sq % 8 + 1], axis=0)),
                                r=["idxK"], w=[("sVst", 0)])
                            P.op("act", lambda e, bs=bs: e.activation(out=Vbf[:, bs, :, 0:128], in_=Vst[:, 0, :, :], func=AF.Copy),
                                 r=[("sVst", 0)], w=[("sVbf", bs)])
                            for g4 in range(8):
                                tb = ps[g4 % 2]
                                for k4 in range(4):
                                    t = g4 * 4 + k4
                                    P.op("pe", lambda e, bs=bs, t=t, k4=k4, tb=tb: e.transpose(
                                        out=tb[:, k4 * 128:(k4 + 1) * 128], in_=Kst[:, bs, t, :], identity=ident),
                                        r=[("sKst", bs), "cst"], w=[("ps", id(tb))])
                                if g4 % 2 == 0:
                                    P.op("act", lambda e, bs=bs, g4=g4, tb=tb: e.activation(
                                        out=KT[:, bs, g4 * 4:(g4 + 1) * 4, :], in_=tb[:].rearrange("p (a b) -> p a b", b=128),
                                        func=AF.Copy), r=[("ps", id(tb))], w=[("sKT", bs)])
                                else:
                                    P.op("dve", lambda e, bs=bs, g4=g4, tb=tb: e.tensor_copy(
                                        out=KT[:, bs, g4 * 4:(g4 + 1) * 4, :], in_=tb[:].rearrange("p (a b) -> p a b", b=128)),
                                        r=[("ps", id(tb))], w=[("sKT", bs)])
                            sbk = ps[2 + bs]
                            for t in range(32):
                                P.op("pe", lambda e, bs=bs, t=t, sq=sq, sbk=sbk: e.matmul(
                                    out=sbk[:, t * 16:(t + 1) * 16], lhsT=KT[:, bs, t, :], rhs=Qbd[:, sq, :], start=True, stop=True),
                                    r=[("sKT", bs), "sQbd"], w=[("ps", id(sbk))])
                            P.op("dve", lambda e, bs=bs, sq=sq, half=half, sbk=sbk: e.tensor_tensor(
                                out=Ssb[:, bs, :].rearrange("p (a q) -> p a q", q=8), in0=sbk[:].rearrange("p (a q) -> p a q", q=8),
                                in1=Dall[:, sq, half * 32:(half + 1) * 32, :].rearrange("p t e -> p (t e)").unsqueeze(2).to_broadcast([128, 64, 8]),
                                op=ALU.add), r=[("ps", id(sbk)), "sDall"], w=[("sSsb", bs)])
                            P.op("act", lambda e, bs=bs: e.activation(out=Psb[:, bs, :], in_=Ssb[:, bs, :], func=AF.Exp),
                                 r=[("sSsb", bs)], w=[("sPsb", bs)])
                            for t in range(32):
                                P.op("pe", lambda e, bs=bs, t=t, half=half, ob=ob: e.matmul(
                                    out=ob[0:16, 0:129], lhsT=Psb[:, bs, t * 16:(t + 1) * 16], rhs=Vbf[:, bs, t, :],
                                    start=(half == 0 and t == 0), stop=False),
                                    r=[("sPsb", bs), ("sVbf", bs)], w=[("ps", id(ob))])
                        sl_, hp_ = sq // 8, sq % 8
                        nb = ps[4]
                        P.op("pe", lambda e, hp_=hp_, sq=sq, nb=nb: e.matmul(out=nb[:, 0:16], lhsT=kTn[:, hp_, 0:128],
                                                                            rhs=Qbd[:, sq, :], start=True, stop=True),
                             r=["kTn", "sQbd"], w=[("ps", id(nb))])
                        P.op("dve", lambda e, sl_=sl_, nb=nb: e.tensor_tensor(
                            out=Sn[:], in0=nb[:, 0:16], in1=scst[:, 1, sl_ * 16:(sl_ + 1) * 16], op=ALU.add),
                            r=[("ps", id(nb)), "scst"], w=["sSn"])
                        P.op("dve", lambda e, hp_=hp_: e.tensor_tensor(
                            out=Sn[:].rearrange("p (e q) -> p e q", q=8), in0=Sn[:].rearrange("p (e q) -> p e q", q=8),
                            in1=negE[:, 2 * hp_:2 * hp_ + 2].unsqueeze(2).to_broadcast([128, 2, 8]), op=ALU.add),
                            r=["sSn", "negE"], w=["sSn"])
                        P.op("act", lambda e: e.activation(out=Pn[:], in_=Sn[:], func=AF.Exp), r=["sSn"], w=["sPn"])
                        P.op("pe", lambda e, hp_=hp_, ob=ob: e.matmul(out=ob[0:16, 0:129], lhsT=Pn[:], rhs=vn1[:, hp_, :],
                                                                      start=False, stop=True),
                             r=["sPn", "vn1"], w=[("ps", id(ob))])
                        P.op("dve", lambda e, ob=ob, osl=osl: e.reciprocal(out=Osb[:, osl, 129:130], in_=ob[0:16, 128:129]),
                             r=[("ps", id(ob))], w=[("sOsb", osl)])
                        P.op("dve", lambda e, ob=ob, osl=osl: e.tensor_scalar(
                            out=Osb[:, osl, 0:128], in0=ob[0:16, 0:128], scalar1=Osb[:, osl, 129:130], scalar2=None, op0=ALU.mult),
                            r=[("ps", id(ob)), ("sOsb", osl)], w=[("sOsb", osl)])
                        for ee in range(2):
                            P.dma("sp", lambda e, sq=sq, ee=ee, osl=osl: e.dma_start(
                                out=o_scr[l][(sq // 8) * 8:(sq // 8 + 1) * 8, (sq % 8) * 128 + ee * 64:(sq % 8) * 128 + (ee + 1) * 64],
                                in_=Osb[ee * 8:(ee + 1) * 8, osl, ee * 64:(ee + 1) * 64]),
                                r=[("sOsb", osl), ("o_scr", l)], w=[("o_scrw", l)])
                    P.dma("sp", lambda e: e.dma_start(out=ofull, in_=o_scr[l].rearrange("(t p) f -> p t f", p=128)),
                          r=[("o_scrw", l), ("o_scr", l)], w=[("sKst", 0)])
                    if debug and l == 0:
                        P.dma("sp", lambda e: e.dma_start(out=dbg[:, 0:512], in_=Dall[:, 0:4, :, :].rearrange("p a b c -> p (a b c)")),
                              r=["sDall"], w=["dbg0"])
                        P.dma("sp", lambda e: e.dma_start(out=dbg[:, 512:1536], in_=ofull[:, 0, :]), r=[("sKst", 0)], w=["dbg1"])
                        P.dma("sp", lambda e: e.dma_start(out=dbg[:, 1536:2560], in_=Ssb[:].rearrange("p a b -> p (a b)")),
                              r=[("sSsb", 0), ("sSsb", 1)], w=["dbg2"])
                        P.dma("sp", lambda e: e.dma_start(out=dbg[0:16, 2560:2816].rearrange("p (a b) -> p a b", a=2), in_=Osb[:, :, 0:128]),
                              r=[("sOsb", 0), ("sOsb", 1)], w=["dbg3"])
                        P.dma("sp", lambda e: e.dma_start(out=dbg[:, 2820:2836], in_=Sn[:]), r=["sSn"], w=["dbg4"])
                    for tt in range(2):
                        P.op("dve", lambda e, tt=tt: e.tensor_tensor(out=mtok[:, tt, :], in0=ofull[:, tt, :], in1=sg[:, tt, :],
                                                                     op=ALU.mult), r=[("sKst", 0), "ssg"], w=["smtok"])
                    for bi in range(2):
                        bank = ps[bi]
                        bv = bank[:].bitcast(BF16)
                        for kk in range(4):
                            kc = bi * 4 + kk
                            for tt in range(2):
                                P.op("pe", lambda e, kc=kc, kk=kk, tt=tt, bv=bv: e.transpose(
                                    out=bv[:, kk * 256 + tt * 128: kk * 256 + (tt + 1) * 128],
                                    in_=mtok[:, tt, kc * 128:(kc + 1) * 128], identity=identb[:]),
                                    r=["smtok", "identb"], w=[("ps", id(bank))])
                        P.op("dve", lambda e, bi=bi, bv=bv: e.tensor_copy(
                            out=mT[:, bi * 4:(bi + 1) * 4, :], in_=bv[:].rearrange("p (a b) -> p a b", b=256)),
                            r=[("ps", id(bank))], w=["smT"])
                    for tt in range(2):
                        yb = [ps[6], ps[7]]
                        for fc in range(2):
                            for kc in range(8):
                                P.op("pe", lambda e, tt=tt, fc=fc, kc=kc, yb=yb: e.matmul(
                                    out=yb[fc][:], lhsT=mT[:, kc, tt * 128:(tt + 1) * 128], rhs=w_out[:, kc, fc * 512:(fc + 1) * 512],
                                    start=(kc == 0), stop=(kc == 7)), r=["smT", "sw_out"], w=[("ps", id(yb[fc]))])
                        post_norm_residual(yb, xs_t[:, tt, :], "sxs", gB, "sgB", ytmp)
                    if l == 1:
                        P.dma("sp", lambda e: e.dma_start(out=ys_out.rearrange("(t p) f -> p t f", p=128), in_=xs_t[:]),
                              r=["sxs"], w=["ys_out"])
                    P.drain("sp")
                    P.flush()

    if do_prompt:
        b_phase()
    if do_sample:
        s_phase()

    P.drain("sp")
    P.flush()
    st.close()
    return nc


_NC_CACHE = {}


def kernel(x_prompt, x_sample, cache_k, cache_v, cache_logf, state_rnn, state_conv, page_table,
           a_pre_norm, a_post_norm, a_w_in, a_conv_w, a_conv_b, a_w_ga, a_b_ga, a_w_gx, a_b_gx,
           a_lambda, a_w_out, kv_norm, w_kv, b_f, b_pre_norm, b_post_norm, b_w_in, b_w_out,
           _trace=False, _do_prompt=True, _debug=False):
    f32 = np.float32
    A = lambda v: np.ascontiguousarray(np.asarray(v, f32))
    npool = int(np.asarray(cache_k).shape[0])
    key = ("prog", npool, _do_prompt, _debug)
    if key not in _NC_CACHE:
        _NC_CACHE[key] = build_program(_do_prompt, True, npool=npool, debug=_debug)
    nc = _NC_CACHE[key]

    pvec = np.zeros((128, NPV), f32)
    for l in range(2):
        pvec[:, PV[("apre", l)]:PV[("apre", l)] + 8] = fm(a_pre_norm[l])
        for j in range(4):
            pvec[:, PV[("cw%d" % j, l)]:PV[("cw%d" % j, l)] + 8] = fm(np.asarray(a_conv_w)[l, j])
        pvec[:, PV[("cb", l)]:PV[("cb", l)] + 8] = fm(a_conv_b[l])
        pvec[:, PV[("bga", l)]:PV[("bga", l)] + 8] = fm(a_b_ga[l])
        pvec[:, PV[("bgx", l)]:PV[("bgx", l)] + 8] = fm(a_b_gx[l])
        pvec[:, PV[("lam", l)]:PV[("lam", l)] + 8] = fm(a_lambda[l])
        pvec[:, PV[("bpre", l)]:PV[("bpre", l)] + 8] = fm(b_pre_norm[l])
    pvec[:, PV[("kvn", 0)]:PV[("kvn", 0)] + 8] = fm(kv_norm)
    rowv = np.zeros((5, D), f32)
    rowv[0:2] = A(a_post_norm); rowv[2:4] = A(b_post_norm); rowv[4, 0:H] = A(b_f)

    ii = np.arange(128)
    ident = np.eye(128, dtype=f32)
    tri = (ii[:, None] <= ii[None, :]).astype(f32)
    ones = np.ones((128, 128), f32)
    causal = np.where(ii[:, None] <= ii[None, :], 0.0, NEG).astype(f32)
    full_ok = np.zeros((128, 128), f32)
    full_no = np.full((128, 128), NEG, f32)

    scst = np.zeros((128, 3, 256), f32)
    order = (ii % 64) * 2 + ii // 64
    scst[:, 0, 0:128] = (order[:, None] > order[None, :]).astype(f32)
    scst[:, 0, 128] = ii // 64
    scst[:, 0, 129] = 2 * (ii // 64)
    scst[:, 0, 130] = 2 * (ii // 64) + 1
    ms = np.full((128, 16, 2, 8), NEG, f32)
    for p_ in range(128):
        s_, t_ = p_ // 8, p_ % 8
        ms[p_, s_, :, t_:] = 0.0
    scst[:, 1, :] = ms.reshape(128, 256)
    scst[:, 2, 0:128] = ((ii[:, None] // 8 == ii[None, :] // 8) & (ii[:, None] <= ii[None, :])).astype(f32)

    ck_full = np.ascontiguousarray(
        np.asarray(cache_k, f32).reshape(npool, 128, 8, 128).transpose(2, 0, 1, 3)).reshape(8 * npool * 4, 4096)
    cv_full = np.ascontiguousarray(
        np.asarray(cache_v, f32).reshape(npool, 128, 8, 128).transpose(2, 0, 1, 3)).reshape(8 * npool * 4, 4096)
    clf_full = np.ascontiguousarray(
        np.asarray(cache_logf, f32).reshape(npool, 2, 64, 8, 2).transpose(3, 0, 1, 4, 2)).reshape(8 * npool * 2, 128)
    pt = np.asarray(page_table).astype(np.int32)
    xs_all = A(x_sample)
    srnn = A(state_rnn); sconv = A(state_conv)

    xp_all = A(x_prompt)
    shared = {"a_w_in": A(a_w_in), "a_w_out": A(a_w_out), "a_w_ga": A(a_w_ga), "a_w_gx": A(a_w_gx), "w_kv": A(w_kv),
              "b_w_in": A(b_w_in), "b_w_out": A(b_w_out), "pvec": pvec, "scst": scst, "rowv": rowv,
              "ck_full": ck_full, "cv_full": cv_full, "clf_full": clf_full}
    in_maps = []
    for c in range(8):
        b, p = c // 2, c % 2
        cst = np.zeros((128, 6, 128), f32)
        cst[:, 0] = ident; cst[:, 1] = tri; cst[:, 2] = ones
        cst[:, 3] = causal if p == 0 else full_ok
        cst[:, 4] = full_no if p == 0 else causal
        idx = np.zeros((128, 32), np.int32)
        for j in range(16):
            idx[:, j] = (2 * j + p) * 128 + ii
            idx[:, 16 + j] = (2 * j + p) * 128 + 63
        xs = np.zeros((256, D), f32); xs[0:32] = xs_all[4 * c:4 * c + 4].reshape(32, D)
        st_rnn = np.zeros((2, NS, D), f32); st_rnn[:, 0:4] = srnn[:, 4 * c:4 * c + 4]
        st_conv = np.zeros((2, NS, 3, D), f32); st_conv[:, 0:4] = sconv[:, 4 * c:4 * c + 4]
        ptT2 = np.zeros((128, NS), np.int32)
        ptT2[:, 0:4] = np.tile(pt[4 * c:4 * c + 4].T, (2, 1))
        m = dict(shared)
        m.update({"xp": xp_all[b], "cst": cst, "idx": idx, "xs": xs, "ptT2": ptT2,
                  "st_rnn": np.ascontiguousarray(st_rnn.reshape(2, NS, 8, 128).transpose(0, 3, 2, 1)),
                  "st_conv": np.ascontiguousarray(st_conv.reshape(2, NS, 3, 8, 128).transpose(0, 4, 3, 1, 2))})
        in_maps.append(m)

    res = run_bass_kernel_spmd(nc, in_maps, core_ids=list(range(8)), trace=_trace)
    R = res.results
    y_prompt = np.zeros((4, T, D), f32)
    k_p = np.zeros((4, T, H, 64), f32); v_p = np.zeros((4, T, H, 64), f32); lf_p = np.zeros((4, T, H), f32)
    rnn_p = np.zeros((2, 4, D), f32); conv_p = np.zeros((2, 4, 3, D), f32)
    y_s = np.zeros((NS, 8, D), f32); k_s = np.zeros((NS, 8, H, 64), f32); v_s = np.zeros((NS, 8, H, 64), f32)
    lf_s = np.zeros((NS, 8, H), f32); rnn_s = np.zeros((2, NS, D), f32); conv_s = np.zeros((2, NS, 3, D), f32)
    for c in range(8):
        b, p = c // 2, c % 2
        r = R[c]
        if _do_prompt:
            y_prompt[b].reshape(16, 2, 128, D)[:, p] = np.asarray(r["y_own"])
            if p == 0:
                k_p[b] = np.asarray(r["k_out"]).reshape(T, H, 64)
                v_p[b] = np.asarray(r["v_out"]).reshape(T, H, 64)
                lf_p[b] = np.asarray(r["lf_out"])
                rnn_p[:, b] = np.asarray(r["rnn_out"]).transpose(0, 2, 1).reshape(2, D)
                conv_p[:, b] = np.asarray(r["conv_out"]).transpose(0, 3, 2, 1).reshape(2, 3, D)
        sl = slice(4 * c, 4 * c + 4)
        y_s[sl] = np.asarray(r["ys_out"], f32)[0:32].reshape(4, 8, D)
        k_s[sl] = np.asarray(r["ks_out"], f32)[0:32].reshape(4, 8, H, 64)
        v_s[sl] = np.asarray(r["vs_out"], f32)[0:32].reshape(4, 8, H, 64)
        lf_s[sl] = np.asarray(r["lfs_out"], f32)[0:32].reshape(4, 8, H)
        rnn_s[:, sl] = np.asarray(r["rnns_out"], f32).transpose(0, 3, 2, 1).reshape(2, NS, D)[:, 0:4]
        conv_s[:, sl] = np.asarray(r["convs_out"], f32).transpose(0, 3, 4, 2, 1).reshape(2, NS, 3, D)[:, 0:4]
    if _trace:
        kernel.last_exec_ns = res.exec_time_ns
    if _debug:
        kernel.last_dbg = np.asarray(R[0]["dbg"])
    return (y_prompt, y_s, k_p, v_p, lf_p, rnn_p, conv_p, k_s, v_s, lf_s, rnn_s, conv_s)
```
